# Optimizing a Trainium2 kernel written in Bass

```python
import math
import jax, jax.numpy as jnp
from jax import lax
import numpy as np

D_MODEL = 1024
BATCH = 16
SEQ = 256
DEPTH = 4
DEC_BATCH = 4
DEC_SEQ = 1024
PAST_LEN = 256

GRID_W = 64
N_MIXERS = 3
N_A = (DEPTH + 2) // 3
N_B = (DEPTH + 1) // 3
N_C = DEPTH // 3
H_A = 8
DH_A = D_MODEL // (2 * H_A)
DV_A = 2 * DH_A
E_A = H_A * DV_A
H_B = 4
DK_B = D_MODEL // H_B
DV_B = 2 * DK_B
E_B = H_B * DV_B
CHUNK = 128
E_C = D_MODEL
CONV_W = 3
ALPHA = (2.0 * DEPTH) ** 0.25
BETA = (8.0 * DEPTH) ** -0.25
ROPE_BASE = 10000.0
Q_BLOCK = 128
LN_EPS = 1e-5

kernel_name = "hybrid_diffusion_diffattn_retnet_shortconv_step"


def layer_norm(x, g, b):
    xf = x.astype(jnp.float32)
    mu = jnp.mean(xf, -1, keepdims=True)
    var = jnp.mean(jnp.square(xf - mu), -1, keepdims=True)
    return ((xf - mu) * lax.rsqrt(var + LN_EPS)).astype(x.dtype) * g + b


def rms_norm(x, eps=1e-6):
    xf = x.astype(jnp.float32)
    return (xf * lax.rsqrt(jnp.mean(jnp.square(xf), -1, keepdims=True) + eps)).astype(x.dtype)


def ada_modulation(cvec, w_mod, b_mod):
    m = jax.nn.silu(cvec) @ w_mod + b_mod
    shift, scale, gate = jnp.split(m[:, None, :], 3, axis=-1)
    return shift, scale, gate


def axial_rope(n_tokens, dim):
    rows = n_tokens // GRID_W
    r = jnp.repeat(jnp.arange(rows, dtype=jnp.float32), GRID_W)
    col = jnp.tile(jnp.arange(GRID_W, dtype=jnp.float32), rows)
    n_freq = dim // 4
    inv = ROPE_BASE ** (-jnp.arange(n_freq, dtype=jnp.float32) / n_freq)
    ang = jnp.concatenate([r[:, None] * inv, col[:, None] * inv], -1)
    return jnp.cos(ang), jnp.sin(ang)


def apply_rope(x, cos, sin):
    xf = x.astype(jnp.float32)
    x1, x2 = jnp.split(xf, 2, axis=-1)
    c = cos[None, :, None, None, :]
    s = sin[None, :, None, None, :]
    return jnp.concatenate([x1 * c - x2 * s, x1 * s + x2 * c], -1).astype(x.dtype)


def diff_lambda(lam, layer_idx):
    lam = lam.astype(jnp.float32)
    lam_init = 0.8 - 0.6 * math.exp(-0.3 * layer_idx)
    val = jnp.exp(jnp.sum(lam[0] * lam[1])) - jnp.exp(jnp.sum(lam[2] * lam[3])) + lam_init
    return val, lam_init


def diff_project(h, w_in):
    b, t, _ = h.shape
    q, k, v, z = jnp.split(h @ w_in, 4, axis=-1)
    return (q.reshape(b, t, H_A, 2, DH_A), k.reshape(b, t, H_A, 2, DH_A),
            v.reshape(b, t, H_A, DV_A), z)


def diff_attention(q, k, v, lam):
    b, t = q.shape[:2]
    nb = t // Q_BLOCK
    qb = jnp.moveaxis(q.reshape(b, nb, Q_BLOCK, H_A, 2, DH_A), 1, 0)
    scale = DH_A ** -0.5

    def block(qblk):
        s = jnp.einsum('bqhmd,bshmd->bhmqs', qblk, k).astype(jnp.float32) * scale
        p = jax.nn.softmax(s, axis=-1)
        a = p[:, :, 0] - lam * p[:, :, 1]
        return jnp.einsum('bhqs,bshd->bqhd', a.astype(v.dtype), v)

    o = lax.map(block, qb)
    return jnp.moveaxis(o, 0, 1).reshape(b, t, H_A, DV_A)


def diff_output(o, z, lam_init, subln_g, w_out):
    o = rms_norm(o) * subln_g * (1.0 - lam_init)
    o = o.reshape(o.shape[0], o.shape[1], E_A)
    return (o * jax.nn.silu(z)) @ w_out


def ret_project(h, w_in):
    b, t, _ = h.shape
    q, k, v, g = jnp.split(h @ w_in, [H_B * DK_B, 2 * H_B * DK_B, 2 * H_B * DK_B + E_B], axis=-1)
    q = q.reshape(b, t, H_B, DK_B)
    k = k.reshape(b, t, H_B, DK_B) * (DK_B ** -0.5)
    v = v.reshape(b, t, H_B, DV_B)
    return q, k, v, g


def log_decay(a):
    return jnp.log1p(-jnp.exp(a.astype(jnp.float32)))


def retention_chunkwise(q, k, v, lg, state0):
    b, t = q.shape[:2]
    nc = t // CHUNK

    def chunks(x):
        return jnp.moveaxis(x.reshape(b, nc, CHUNK, *x.shape[2:]), 1, 0)

    idx = jnp.arange(CHUNK, dtype=jnp.float32)
    diff = idx[:, None] - idx[None, :]
    dmask = jnp.where(diff[None] >= 0, jnp.exp(jnp.maximum(diff, 0.0)[None] * lg[:, None, None]), 0.0)
    q_decay = jnp.exp((idx[:, None] + 1.0) * lg[None, :])
    k_decay = jnp.exp((CHUNK - 1.0 - idx)[:, None] * lg[None, :])
    c_decay = jnp.exp(CHUNK * lg)

    def step(s, xs):
        qc, kc, vc = xs
        qf, kf, vf = qc.astype(jnp.float32), kc.astype(jnp.float32), vc.astype(jnp.float32)
        qk = jnp.einsum('bnhd,bmhd->bhnm', qf, kf) * dmask
        o = (jnp.einsum('bhnm,bmhv->bnhv', qk, vf)
             + jnp.einsum('bnhd,bhdv->bnhv', qf, s) * q_decay[None, :, :, None])
        s = s * c_decay[None, :, None, None] + jnp.einsum('bmhd,bmhv->bhdv', kf * k_decay[None, :, :, None], vf)
        return s, o

    s_fin, o = lax.scan(step, state0.astype(jnp.float32), (chunks(q), chunks(k), chunks(v)))
    return jnp.moveaxis(o, 0, 1).reshape(b, t, H_B, DV_B).astype(v.dtype), s_fin


def bi_retention(q, k, v, lg_f, lg_b, s0_f, s0_b):
    o_f, s_f = retention_chunkwise(q, k, v, lg_f, s0_f)
    fl = lambda a: jnp.flip(a, axis=1)
    o_b, s_b = retention_chunkwise(fl(q), fl(k), fl(v), lg_b, s0_b)
    return o_f + fl(o_b), s_f, s_b


def ret_output(o, g, w_out):
    o = rms_norm(o).reshape(o.shape[0], o.shape[1], E_B)
    return (o * jax.nn.silu(g)) @ w_out


def short_conv3(u, w):
    up = jnp.pad(u, ((0, 0), (1, 1), (0, 0)))
    return up[:, :-2] * w[0] + up[:, 1:-1] * w[1] + up[:, 2:] * w[2]


def conv_mixer(h, w_in, conv_w, w_out):
    bg, cg, u, z = jnp.split(h @ w_in, 4, axis=-1)
    y = bg * short_conv3(cg * u, conv_w)
    return (y * jax.nn.silu(z)) @ w_out


def setup_inputs(seed: int = 0) -> dict:
    key = jax.random.key(seed)
    ks = jax.random.split(key, 26)
    f32 = jnp.float32

    def nrm(k, shape, s):
        return jax.random.normal(k, shape, f32) * s

    D = D_MODEL
    heads_b = jnp.arange(H_B, dtype=f32)
    decay_init = -(5.0 + heads_b) * math.log(2.0)
    return {
        "x_prompt": nrm(ks[0], (BATCH, SEQ, D), 1.0),
        "x_sample": nrm(ks[1], (DEC_BATCH, DEC_SEQ, D), 1.0),
        "cache_k": nrm(ks[2], (DEC_BATCH, N_A, PAST_LEN, H_A, 2 * DH_A), 1.0),
        "cache_v": nrm(ks[3], (DEC_BATCH, N_A, PAST_LEN, H_A, DV_A), 1.0),
        "state_fwd": nrm(ks[4], (DEC_BATCH, N_B, H_B, DK_B, DV_B), 0.5),
        "state_bwd": nrm(ks[5], (DEC_BATCH, N_B, H_B, DK_B, DV_B), 0.5),
        "c": nrm(ks[6], (DEC_BATCH, D), 1.0),
        "c_ctx": nrm(ks[7], (D,), 1.0),
        "w_mod": nrm(ks[8], (DEPTH, D, 3 * D), 0.5 * D ** -0.5),
        "b_mod": nrm(ks[9], (DEPTH, 3 * D), 0.02),
        "ln_g": 1.0 + nrm(ks[10], (DEPTH, D), 0.02),
        "ln_b": nrm(ks[11], (DEPTH, D), 0.02),
        "w_in_a": nrm(ks[12], (N_A, D, 4 * E_A), D ** -0.5),
        "lam_a": nrm(ks[13], (N_A, 4, DH_A), 0.1),
        "subln_a": 1.0 + nrm(ks[14], (N_A, DV_A), 0.02),
        "w_out_a": nrm(ks[15], (N_A, E_A, D), BETA * E_A ** -0.5),
        "w_in_b": nrm(ks[16], (N_B, D, 2 * H_B * DK_B + 2 * E_B), D ** -0.5),
        "decay_fwd": decay_init[None] + nrm(ks[17], (N_B, H_B), 0.1),
        "decay_bwd": decay_init[None] + nrm(ks[18], (N_B, H_B), 0.1),
        "w_out_b": nrm(ks[19], (N_B, E_B, D), BETA * E_B ** -0.5),
        "w_in_c": nrm(ks[20], (N_C, D, 4 * E_C), D ** -0.5),
        "conv_c": nrm(ks[21], (N_C, CONV_W, E_C), CONV_W ** -0.5),
        "w_out_c": nrm(ks[22], (N_C, E_C, D), BETA * E_C ** -0.5),
    }


def reference(x_prompt, x_sample, cache_k, cache_v, state_fwd, state_bwd, c, c_ctx,
              w_mod, b_mod, ln_g, ln_b, w_in_a, lam_a, subln_a, w_out_a,
              w_in_b, decay_fwd, decay_bwd, w_out_b, w_in_c, conv_c, w_out_c):
    xp = x_prompt
    xs = x_sample
    b_s, t_s = xs.shape[0], xs.shape[1]
    l_c = cache_k.shape[2]
    new_k, new_v, new_sf, new_sb = [], [], [], []
    for i in range(DEPTH):
        kind, j = i % N_MIXERS, i // N_MIXERS
        sh_p, sc_p, g_p = ada_modulation(c_ctx[None], w_mod[i], b_mod[i])
        sh_s, sc_s, g_s = ada_modulation(c, w_mod[i], b_mod[i])
        hp = xp * (1 + sc_p) + sh_p
        hs = xs * (1 + sc_s) + sh_s
        if kind == 0:
            lam, lam_init = diff_lambda(lam_a[j], i)
            q, k, v, z = diff_project(hp, w_in_a[j])
            out_p = diff_output(diff_attention(q, k, v, lam), z, lam_init, subln_a[j], w_out_a[j])
            new_k.append(k.reshape(k.shape[0], k.shape[1], H_A, 2 * DH_A))
            new_v.append(v)
            q, k, v, z = diff_project(hs, w_in_a[j])
            cos, sin = axial_rope(t_s, DH_A)
            q = apply_rope(q, cos, sin)
            k = apply_rope(k, cos, sin)
            k_all = jnp.concatenate([k, cache_k[:, j].reshape(b_s, l_c, H_A, 2, DH_A).astype(k.dtype)], axis=1)
            v_all = jnp.concatenate([v, cache_v[:, j].astype(v.dtype)], axis=1)
            out_s = diff_output(diff_attention(q, k_all, v_all, lam), z, lam_init, subln_a[j], w_out_a[j])
        elif kind == 1:
            lg_f = log_decay(decay_fwd[j])
            lg_b = log_decay(decay_bwd[j])
            q, k, v, g = ret_project(hp, w_in_b[j])
            zero = jnp.zeros((q.shape[0], H_B, DK_B, DV_B), jnp.float32)
            o, s_f, s_b = bi_retention(q, k, v, lg_f, lg_b, zero, zero)
            out_p = ret_output(o, g, w_out_b[j])
            new_sf.append(s_f)
            new_sb.append(s_b)
            q, k, v, g = ret_project(hs, w_in_b[j])
            o, _, _ = bi_retention(q, k, v, lg_f, lg_b, state_fwd[:, j], state_bwd[:, j])
            out_s = ret_output(o, g, w_out_b[j])
        else:
            out_p = conv_mixer(hp, w_in_c[j], conv_c[j], w_out_c[j])
            out_s = conv_mixer(hs, w_in_c[j], conv_c[j], w_out_c[j])
        xp = layer_norm(ALPHA * xp + g_p * out_p, ln_g[i], ln_b[i])
        xs = layer_norm(ALPHA * xs + g_s * out_s, ln_g[i], ln_b[i])
    y_prompt = xp
    y_sample = xs
    new_cache_k = jnp.stack(new_k, axis=1)
    new_cache_v = jnp.stack(new_v, axis=1)
    new_state_fwd = jnp.stack(new_sf, axis=1)
    new_state_bwd = jnp.stack(new_sb, axis=1)
    return (y_prompt, y_sample, new_cache_k, new_cache_v, new_state_fwd, new_state_bwd)
```

```python
import math
import numpy as np
import ml_dtypes
import concourse.bass as bass
import concourse.mybir as mybir
from concourse.bass_utils import run_bass_kernel_spmd

F32 = mybir.dt.float32
BF16 = mybir.dt.bfloat16
AF = mybir.ActivationFunctionType
ALU = mybir.AluOpType
AX = mybir.AxisListType

D = 1024
T = 1024
NCH = 8
DEPTH = 4
ALPHA = (2.0 * DEPTH) ** 0.25
LN_EPS = 1e-5
NEG = -30000.0
BIG = 1.0e6
SAME_ENGINE_SYNC = True
NDMA = 16


class Tile:
    __slots__ = ("ap", "w", "r", "name", "psum")

    def __init__(self, ap, name="", psum=False):
        self.ap = ap
        self.w = None
        self.r = {}
        self.name = name
        self.psum = psum

    def __getitem__(self, k):
        return self.ap[k]


class _Eng:
    def __init__(self, name, sem):
        self.name = name
        self.sem = sem
        self.count = 0
        self.ops = []
        self.seen = {}
        self.dma_sems = []
        self.dma_cnt = []
        self.dma_i = 0


class Sched:
    def __init__(self, nc):
        self.nc = nc
        self.E = {}
        self.semobjs = {}

    def add_engine(self, name, sem, dma_sems=()):
        e = _Eng(name, sem)
        e.dma_sems = list(dma_sems)
        e.dma_cnt = [0] * len(e.dma_sems)
        self.E[name] = e
        self.semobjs[id(sem)] = sem
        for s in dma_sems:
            self.semobjs[id(s)] = s

    def _collect(self, E, reads, writes, waits):
        def need(tok):
            if tok is None:
                return
            sid, val = tok
            if sid == id(E.sem) and (E.name == "pe" or not SAME_ENGINE_SYNC):
                return
            if E.seen.get(sid, 0) >= val:
                return
            if waits.get(sid, 0) < val:
                waits[sid] = val

        for t in reads:
            need(t.w)
            if t.psum:
                for sid, val in t.r.items():
                    if sid != id(E.sem):
                        need((sid, val))
        for t in writes:
            need(t.w)
            for sid, val in t.r.items():
                need((sid, val))

    def op(self, eng, fn, reads=(), writes=()):
        E = self.E[eng]
        waits = {}
        self._collect(E, reads, writes, waits)
        for sid, val in waits.items():
            E.seen[sid] = val
        E.count += 1
        tok = (id(E.sem), E.count)
        E.ops.append((list(waits.items()), fn, E.sem, 1))
        for t in reads:
            if t.r.get(tok[0], 0) < tok[1]:
                t.r[tok[0]] = tok[1]
        for t in writes:
            t.w = tok
            t.r = {}
        return tok

    def dma(self, eng, out, in_, reads=(), writes=()):
        E = self.E[eng]
        slot = E.dma_i % len(E.dma_sems)
        E.dma_i += 1
        sem = E.dma_sems[slot]
        waits = {}
        prev = E.dma_cnt[slot]
        if prev > 0 and E.seen.get(id(sem), 0) < prev:
            waits[id(sem)] = prev
        E.dma_cnt[slot] += 16
        val = E.dma_cnt[slot]
        self._collect(E, reads, writes, waits)
        for sid, v in waits.items():
            E.seen[sid] = v
        tok = (id(sem), val)

        def fn(e, out=out, in_=in_):
            return e.dma_start(out=out, in_=in_)

        E.ops.append((list(waits.items()), fn, sem, 16))
        for t in reads:
            if t.r.get(tok[0], 0) < tok[1]:
                t.r[tok[0]] = tok[1]
        for t in writes:
            t.w = tok
            t.r = {}
        return tok

    def finish(self, eng="sp"):
        E = self.E[eng]
        waits = {}
        for o in self.E.values():
            if o.count > 0 and o is not E:
                waits[id(o.sem)] = o.count
            for s, c in zip(o.dma_sems, o.dma_cnt):
                if c > 0:
                    waits[id(s)] = c
        E.ops.append((list(waits.items()), None, None, 0))

    def replay(self, name, e):
        E = self.E[name]
        for waits, fn, sem, inc in E.ops:
            for sid, val in waits:
                e.wait_ge(self.semobjs[sid], val)
            if fn is None:
                continue
            ins = fn(e)
            ins.then_inc(sem, inc)


LAYERS = [(0, 0), (1, 0), (2, 0), (0, 1)]


def build(layers=LAYERS, debug_xT=False, stop_after=None):
    nc = bass.Bass("TRN2", target_bir_lowering=False)
    dt_in = lambda name, shape, dt=F32: nc.dram_tensor(name, list(shape), dt, kind="ExternalInput").ap()
    dt_out = lambda name, shape, dt=F32: nc.dram_tensor(name, list(shape), dt, kind="ExternalOutput").ap()

    x_d = dt_in("x", [T, D])
    cvec_d = dt_in("cvec", [128, 8])
    bmod_d = dt_in("bmod", [DEPTH, 128, 24])
    lng_d = dt_in("lng", [DEPTH, 128, 8])
    lnb_d = dt_in("lnb", [DEPTH, 128, 8])
    wmod_d = dt_in("wmod", [max(1, len(layers)), D, 3 * D])
    kinds = set(k for k, _ in layers)
    if 0 in kinds:
        wina_d = dt_in("wina", [2, D, 4096])
        wouta_d = dt_in("wouta", [2, D, D])
    if 1 in kinds:
        winb_d = dt_in("winb", [D, 6144])
        woutb_d = dt_in("woutb", [2048, D])
    if 2 in kinds:
        winc_d = dt_in("winc", [D, 4096])
        woutc_d = dt_in("woutc", [D, D])
    convw_d = dt_in("convw", [128, 8, 3])
    lam_d = dt_in("lam", [2, 256])
    subln_d = dt_in("subln", [128, 2])
    dec_d = dt_in("dec", [8])
    ck_d = dt_in("cachek", [2, 256, 1024])
    cv_d = dt_in("cachev", [2, 256, 1024])
    sf_d = dt_in("statef", [4, 256, 512])
    sb_d = dt_in("stateb", [4, 256, 512])
    ident_d = dt_in("ident", [128, 128])
    ropec_d = dt_in("ropec", [128, 8, 32])
    ropes_d = dt_in("ropes", [128, 8, 32])
    maskq_d = dt_in("maskq", [5, 8, T], BF16)
    maskk_d = dt_in("maskk", [5, 8, 1280], BF16)
    rtab_d = dt_in("rtab", [2, T])
    rchk_d = dt_in("rchk", [16])
    rdt_d = dt_in("rdt", [4, 128, 128])
    rkd_d = dt_in("rkd", [128, 2])
    cflag_d = dt_in("cflag", [128, 1])

    y_d = dt_out("y", [T, D])
    cko_d = dt_out("cko", [2, T, 1024])
    cvo_d = dt_out("cvo", [2, T, 1024])
    sfo_d = dt_out("sfo", [4, 4, 256, 512])
    sbo_d = dt_out("sbo", [4, 4, 256, 512])
    dbg_d = dt_out("dbg", [128, 8, T]) if debug_xT else None

    S = Sched(nc)
    nsem = 5 + 2 * NDMA
    sems = [nc.alloc_semaphore(name=f"s{i}") for i in range(nsem)]
    S.add_engine("pe", sems[0])
    S.add_engine("act", sems[1])
    S.add_engine("dve", sems[2])
    S.add_engine("pool", sems[3], sems[5:5 + NDMA])
    S.add_engine("sp", sems[4], sems[5 + NDMA:5 + 2 * NDMA])

    def sb(name, shape, dt=F32):
        return nc.alloc_sbuf_tensor("sb_" + name, list(shape), dt)

    def tile(name, shape, dt=F32):
        return Tile(sb(name, shape, dt).ap(), name)

    xT = [Tile(None) for _ in range(NCH)]
    xT_t = sb("xT", [128, NCH, T], F32).ap()
    for c in range(NCH):
        xT[c].ap = xT_t[:, c, :]
    hT_t = sb("hT", [128, NCH, T], BF16).ap()
    hT = [Tile(hT_t[:, c, :], f"hT{c}") for c in range(NCH)]
    NSLAB = 3
    slab_t = sb("slab", [128, NSLAB, 8 * 512], BF16).ap()
    slabs = [Tile(slab_t[:, i, :], f"slab{i}") for i in range(NSLAB)]
    slab_i = [0]

    ident = tile("ident", [128, 128], F32)
    identb = tile("identb", [128, 128], BF16)
    onesM = tile("onesM", [128, 128], BF16)
    cvec = tile("cvec", [128, 8], F32)
    scb = tile("scb", [128, 8], BF16)
    bmod = tile("bmod", [128, DEPTH, 24], F32)
    lng = tile("lng", [128, DEPTH, 8], F32)
    lnb = tile("lnb", [128, DEPTH, 8], F32)
    sc1 = tile("sc1", [128, 8], F32)
    gA = tile("gA", [128, 8], F32)
    epsln = tile("epsln", [128, 1], F32)

    pd_t = [nc.alloc_psum_tensor(f"pd{i}", [128, 1024], F32).ap() for i in range(4)]
    PS = [Tile(pd_t[i // 2][:, (i % 2) * 512:(i % 2 + 1) * 512], f"ps{i}", psum=True) for i in range(8)]
    PD = [Tile(pd_t[i], f"pd{i}", psum=True) for i in range(2)]
    ps_i = [0]
    pd_i = [0]
    ps_mode = {"double": False}

    def _xfer(srcs, dsts):
        acc = {}
        for t in srcs:
            if t.w is not None and acc.get(t.w[0], 0) < t.w[1]:
                acc[t.w[0]] = t.w[1]
            for s_, v_ in t.r.items():
                if acc.get(s_, 0) < v_:
                    acc[s_] = v_
        for t in dsts:
            t.w = None
            t.r = dict(acc)

    def psum_double_mode(on):
        if on:
            _xfer(PS[0:2], [PD[0]])
            _xfer(PS[2:4], [PD[1]])
        else:
            _xfer([PD[0]], PS[0:2])
            _xfer([PD[1]], PS[2:4])
        ps_mode["double"] = on

    def psum():
        if ps_mode["double"]:
            t = PS[4 + ps_i[0] % 4]
        else:
            t = PS[ps_i[0] % 8]
        ps_i[0] += 1
        return t

    def psum2():
        t = PD[pd_i[0] % 2]
        pd_i[0] += 1
        return t

    ARENA_BYTES = 132 * 1024
    arena_t = sb("arena", [128, ARENA_BYTES // 4], F32).ap()

    class Arena:
        def __init__(self):
            self.off = 0
            self.live = []
            self.pending = {}
            self.offs = {}

        def reset(self):
            for t in self.live:
                if t.w is not None:
                    s_, v_ = t.w
                    if self.pending.get(s_, 0) < v_:
                        self.pending[s_] = v_
                for s_, v_ in t.r.items():
                    if self.pending.get(s_, 0) < v_:
                        self.pending[s_] = v_
            self.live = []
            self.off = 0

        def alloc(self, name, shape, dt=F32):
            esz = 4 if dt == F32 else 2
            n = int(np.prod(shape[1:]))
            nbytes = (n * esz + 3) // 4 * 4
            assert self.off + nbytes <= ARENA_BYTES, (name, self.off, nbytes)
            words = nbytes // 4
            base = arena_t[0:shape[0], self.off // 4:self.off // 4 + words]
            if dt != F32:
                base = base.bitcast(dt)
            if len(shape) > 2:
                letters = "abcdefg"[:len(shape) - 1]
                pat = "p (" + " ".join(letters) + ") -> p " + " ".join(letters)
                kw = {l: s for l, s in zip(letters, shape[1:])}
                base = base.rearrange(pat, **kw)
            t = Tile(base, name)
            t.r = dict(self.pending)
            self.live.append(t)
            self.offs[id(t)] = (self.off, nbytes)
            self.off += nbytes
            return t

        def view(self, name, first_tile, shape, dt, dep_tiles):
            off = self.offs[id(first_tile)][0]
            esz = 4 if dt == F32 else 2
            n = int(np.prod(shape[1:]))
            words = (n * esz + 3) // 4
            end = max(self.offs[id(d)][0] + self.offs[id(d)][1] for d in dep_tiles)
            assert off + words * 4 <= end, (name, off, words * 4, end)
            base = arena_t[0:shape[0], off // 4:off // 4 + words]
            if dt != F32:
                base = base.bitcast(dt)
            if len(shape) > 2:
                letters = "abcdefg"[:len(shape) - 1]
                pat = "p (" + " ".join(letters) + ") -> p " + " ".join(letters)
                base = base.rearrange(pat, **{l: s_ for l, s_ in zip(letters, shape[1:])})
            t = Tile(base, name)
            acc = dict(self.pending)
            for d in dep_tiles:
                if d.w is not None and acc.get(d.w[0], 0) < d.w[1]:
                    acc[d.w[0]] = d.w[1]
                for s_, v_ in d.r.items():
                    if acc.get(s_, 0) < v_:
                        acc[s_] = v_
            t.r = acc
            self.live.append(t)
            self.offs[id(t)] = (off, words * 4)
            return t

        def sub(self, parent, ap, name=""):
            t = Tile(ap, name)
            t.r = dict(self.pending)
            self.live.append(t)
            return t

    AR = Arena()

    def act(out, in_, func, reads, writes, bias=None, scale=None, accum_out=None):
        kw = {}
        if bias is not None:
            kw["bias"] = bias
        if scale is not None:
            kw["scale"] = scale
        if accum_out is not None:
            kw["accum_out"] = accum_out
        return S.op("act", lambda e: e.activation(out=out, in_=in_, func=func, **kw), reads, writes)

    def tt(eng, out, in0, in1, op, reads, writes):
        return S.op(eng, lambda e: e.tensor_tensor(out=out, in0=in0, in1=in1, op=op), reads, writes)

    def ts(eng, out, in0, s1, s2, op0, op1, reads, writes):
        if s2 is None:
            return S.op(eng, lambda e: e.tensor_scalar(out=out, in0=in0, scalar1=s1, scalar2=None, op0=op0), reads, writes)
        return S.op(eng, lambda e: e.tensor_scalar(out=out, in0=in0, scalar1=s1, scalar2=s2, op0=op0, op1=op1), reads, writes)

    def stt(out, in0, scalar, in1, op0, op1, reads, writes):
        return S.op("dve", lambda e: e.scalar_tensor_tensor(out=out, in0=in0, scalar=scalar, in1=in1, op0=op0, op1=op1), reads, writes)

    def cp(eng, out, in_, reads, writes):
        if eng == "act":
            return S.op("act", lambda e: e.copy(out=out, in_=in_), reads, writes)
        return S.op(eng, lambda e: e.tensor_copy(out=out, in_=in_), reads, writes)

    def mm_group(out_tile, out_ap, pairs, extra_reads):
        n = len(pairs)

        def fn(e):
            ins = None
            for i, (l, r) in enumerate(pairs):
                ins = e.matmul(out_ap, lhsT=l, rhs=r, start=(i == 0), stop=(i == n - 1))
            return ins

        return S.op("pe", fn, extra_reads, [out_tile])

    def layer_slabs(kind, j):
        L = []
        if kind == 0:
            for g in range(2):
                for sec in range(3):
                    L.append(("wina", j, sec * 1024 + g * 512, 512))
            for g in range(2):
                L.append(("wina", j, 3072 + g * 512, 512))
            for s_ in range(2):
                L.append(("wouta", j, s_ * 512, 512))
        elif kind == 1:
            for h in range(4):
                for q in range(3):
                    L.append(("winb", 0, h * 1536 + q * 512, 512))
            for s_ in range(4):
                L.append(("woutb", 0, s_ * 256, 256))
        else:
            for e_ in range(8):
                L.append(("winc", 0, e_ * 512, 512))
            for s_ in range(2):
                L.append(("woutc", 0, s_ * 512, 512))
        return L

    def slab_plan():
        plan = []
        for li, (kind, j) in enumerate(layers):
            L = layer_slabs(kind, j)
            if li == 0:
                plan += [("wmod", 0, s6 * 512, 512) for s6 in range(4)]
                plan += [L[0], ("wmod", 0, 4 * 512, 512), ("wmod", 0, 5 * 512, 512)] + L[1:]
            else:
                plan += [("wmod", li, s6 * 512, 512) for s6 in range(6)]
                plan += L
        return plan

    PLAN = slab_plan()
    plan_state = {"next": 0, "issued": 0, "views": {}}

    def slab_src(key):
        name, idx, c0, ncol = key
        base = {"wmod": lambda: wmod_d[idx], "wina": lambda: wina_d[idx], "wouta": lambda: wouta_d[idx],
                "winb": lambda: winb_d, "woutb": lambda: woutb_d, "winc": lambda: winc_d, "woutc": lambda: woutc_d}[name]()
        return base[:, c0:c0 + ncol].rearrange("(k p) n -> p k n", p=128)

    def issue_slab(i):
        src_ap = slab_src(PLAN[i])
        t = slabs[i % NSLAB]
        k, n = src_ap.shape[1], src_ap.shape[2]
        dst = t.ap[:, 0:k * n].rearrange("p (k n) -> p k n", k=k)
        S.dma("pool", dst, src_ap, [], [t])
        plan_state["views"][i] = (t, dst)

    def load_slab(key):
        i = plan_state["next"]
        deferred = None
        while key[0] != "wmod" and PLAN[i][0] == "wmod":
            deferred = PLAN[i][1]
            mod_slab(PLAN[i][1], PLAN[i][2] // 512)
            i = plan_state["next"]
        if deferred is not None:
            finish_gate(deferred)
        assert PLAN[i] == key, (i, PLAN[i], key)
        plan_state["next"] += 1
        oldest = plan_state.get("hold", i)
        while plan_state["issued"] < min(len(PLAN), oldest + NSLAB):
            issue_slab(plan_state["issued"])
            plan_state["issued"] += 1
        ret = plan_state["views"].pop(i)
        if False:
            plan_state["hold"] = i
            mod_slab(PLAN[i + 1][1], PLAN[i + 1][2] // 512)
            del plan_state["hold"]
        return ret

    S.dma("sp", ident.ap, ident_d, [], [ident])
    S.dma("sp", cvec.ap, cvec_d, [], [cvec])
    S.dma("sp", bmod.ap, bmod_d.rearrange("l p j -> p l j"), [], [bmod])
    S.dma("sp", lng.ap, lng_d.rearrange("l p j -> p l j"), [], [lng])
    S.dma("sp", lnb.ap, lnb_d.rearrange("l p j -> p l j"), [], [lnb])
    cp("dve", identb.ap, ident.ap, [ident], [identb])
    S.op("dve", lambda e: e.memset(onesM.ap, 1.0 / 1024.0), [], [onesM])
    S.op("dve", lambda e: e.memset(epsln.ap, LN_EPS / (ALPHA * ALPHA)), [], [epsln])
    act(scb.ap, cvec.ap, AF.Silu, [cvec], [scb])

    modv_l = [tile(f"modv{l}", [128, 24], F32) for l in range(len(layers))]

    def mod_slab(li, s6):
        t, v = load_slab(("wmod", li, s6 * 512, 512))
        pm = psum()

        def fn(e, v=v, pm=pm):
            ins = None
            for jj in range(4):
                for k in range(8):
                    ins = e.matmul(pm.ap[:, jj:jj + 1], lhsT=v[:, k, jj * 128:(jj + 1) * 128], rhs=scb.ap[:, k:k + 1],
                                   start=(k == 0), stop=(k == 7))
            return ins

        S.op("pe", fn, [t, scb], [pm])
        tt("dve", modv_l[li].ap[:, 4 * s6:4 * s6 + 4], pm.ap[:, 0:4], bmod.ap[:, li, 4 * s6:4 * s6 + 4], ALU.add,
           [pm, bmod], [modv_l[li]])

    def finish_gate(li):
        modv = modv_l[li]
        ts("dve", gA.ap, modv.ap[:, 16:24], 1.0 / ALPHA, None, ALU.mult, None, [modv], [gA])

    def modulation(li):
        modv = modv_l[li]
        if li > 0:
            for s6 in range(6):
                mod_slab(li, s6)
            finish_gate(li)
        ts("dve", sc1.ap, modv.ap[:, 8:16], 1.0, None, ALU.add, None, [modv], [sc1])
        for c in range(NCH):
            if c % 2 == 0:
                ts("dve", hT[c].ap, xT[c].ap, sc1.ap[:, c:c + 1], modv.ap[:, c:c + 1], ALU.mult, ALU.add,
                   [xT[c], sc1, modv], [hT[c]])
            else:
                act(hT[c].ap, xT[c].ap, AF.Identity, [xT[c], sc1, modv], [hT[c]],
                    bias=modv.ap[:, c:c + 1], scale=sc1.ap[:, c:c + 1])

    def out_proj_residual(wkey, n_e, ogT_tiles, ogT_ap, ws):
        LNWS["ws"] = ws
        ncols = 512 * 8 // n_e
        for s_ in range(D // ncols):
            t, v = load_slab((wkey[0], wkey[1], s_ * ncols, ncols))
            for dcc in range(ncols // 128):
                dc = s_ * (ncols // 128) + dcc
                for th in range(2):
                    p = psum()
                    pairs = [(v[:, e_, dcc * 128:(dcc + 1) * 128], ogT_ap[:, e_, th * 512:(th + 1) * 512]) for e_ in range(n_e)]
                    mm_group(p, p.ap, pairs, [t] + list(ogT_tiles))
                    xs = xT[dc].ap[:, th * 512:(th + 1) * 512]
                    stt(xs, p.ap, gA.ap[:, dc:dc + 1], xs, ALU.mult, ALU.add, [p, gA, xT[dc]], [xT[dc]])
                cp("act", hT[dc].ap, xT[dc].ap, [xT[dc]], [hT[dc]])
                if dc % 2 == 0:
                    act(ws["ysq"].ap[:, dc, :], xT[dc].ap, AF.Square, [xT[dc]], [ws["ysq"]])
                else:
                    tt("dve", ws["ysq"].ap[:, dc, :], xT[dc].ap, xT[dc].ap, ALU.mult, [xT[dc]], [ws["ysq"]])

    LNWS = {}

    def layer_norm(li):
        ws = LNWS["ws"]
        ysq, mean, rstd, tmp = ws["ysq"], ws["mean"], ws["rstd"], ws["tmp"]
        for th in range(2):
            pm = psum()
            mm_group(pm, pm.ap, [(onesM.ap, hT[c].ap[:, th * 512:(th + 1) * 512]) for c in range(NCH)], [onesM] + hT)
            pq = psum()
            mm_group(pq, pq.ap, [(onesM.ap, ysq.ap[:, c, th * 512:(th + 1) * 512]) for c in range(NCH)], [onesM, ysq])
            sl = slice(th * 512, (th + 1) * 512)
            cp("dve", mean.ap[:, sl], pm.ap, [pm], [mean])
            act(tmp[0].ap[:, sl], pm.ap, AF.Square, [pm], [tmp[0]])
            tt("dve", rstd.ap[:, sl], pq.ap, tmp[0].ap[:, sl], ALU.subtract, [pq, tmp[0]], [rstd])
        act(rstd.ap, rstd.ap, AF.Ln, [rstd, epsln], [rstd], bias=epsln.ap[:, 0:1])
        act(rstd.ap, rstd.ap, AF.Exp, [rstd], [rstd], scale=-0.5)
        for c in range(NCH):
            tm = tmp[c % 2]
            tt("dve", tm.ap, xT[c].ap, mean.ap, ALU.subtract, [xT[c], mean], [tm])
            tt("dve", tm.ap, tm.ap, rstd.ap, ALU.mult, [tm, rstd], [tm])
            act(xT[c].ap, tm.ap, AF.Identity, [tm, lng, lnb], [xT[c]],
                bias=lnb.ap[:, li, c:c + 1], scale=lng.ap[:, li, c:c + 1])

    def ln_ws_alloc():
        return {"ysq": AR.alloc("ysq", [128, NCH, T], BF16), "mean": AR.alloc("mean", [128, T], F32),
                "rstd": AR.alloc("rstd", [128, T], F32), "tmp": [AR.alloc(f"lntmp{i}", [128, T], F32) for i in range(2)]}

    def ln_ws_alias(ysq_first, ysq_deps, scr_first, scr_deps):
        ysq = AR.view("ysq", ysq_first, [128, NCH, T], BF16, ysq_deps)
        scr = AR.view("lnscr", scr_first, [128, 4 * T], F32, scr_deps)
        parts = []
        for i in range(4):
            t_ = Tile(scr.ap[:, i * T:(i + 1) * T], f"lnscr{i}")
            t_.r = dict(scr.r)
            AR.live.append(t_)
            parts.append(t_)
        return {"ysq": ysq, "mean": parts[0], "rstd": parts[1], "tmp": [parts[2], parts[3]]}

    def conv_layer(li, j):
        AR.reset()
        ygT = AR.alloc("ygT", [128, 8, T], BF16)
        cu = AR.alloc("cu", [128, T + 2], F32)
        usb = AR.alloc("usb", [128, T], F32)
        szb = AR.alloc("szb", [128, T], F32)
        bgz = AR.alloc("bgz", [128, T], F32)
        yc = AR.alloc("yc", [128, T], F32)
        cw = AR.alloc("cw", [128, 8, 3], F32)
        cwf = AR.alloc("cwf", [128, 8, 3], F32)
        cfl = AR.alloc("cfl", [128, 1], F32)
        ws = ln_ws_alloc()
        S.dma("sp", cw.ap, convw_d, [], [cw])
        S.dma("sp", cfl.ap, cflag_d, [], [cfl])
        ts("dve", cwf.ap, cw.ap, cfl.ap[:, 0:1], -1.0, ALU.mult, ALU.mult, [cw, cfl], [cwf])
        S.op("dve", lambda e: e.memset(cu.ap[:, 0:1], 0.0), [], [cu])
        S.op("dve", lambda e: e.memset(cu.ap[:, T + 1:T + 2], 0.0), [], [cu])
        for ech in range(8):
            t, v = load_slab(("winc", 0, ech * 512, 512))
            for th in range(2):
                sl = slice(th * 512, (th + 1) * 512)
                pp = []
                for q in range(4):
                    p = psum()
                    mm_group(p, p.ap, [(v[:, k, q * 128:(q + 1) * 128], hT[k].ap[:, sl]) for k in range(8)], [t] + hT)
                    pp.append(p)
                cp("act", usb.ap[:, sl], pp[2].ap, [pp[2]], [usb])
                act(szb.ap[:, sl], pp[3].ap, AF.Silu, [pp[3]], [szb])
                tt("dve", cu.ap[:, 1 + th * 512:1 + (th + 1) * 512], pp[1].ap, usb.ap[:, sl], ALU.mult, [pp[1], usb], [cu])
                tt("dve", bgz.ap[:, sl], pp[0].ap, szb.ap[:, sl], ALU.mult, [pp[0], szb], [bgz])
            ts("dve", yc.ap, cu.ap[:, 1:T + 1], cw.ap[:, ech, 1:2], None, ALU.mult, None, [cu, cw], [yc])
            stt(yc.ap, cu.ap[:, 0:T], cw.ap[:, ech, 0:1], yc.ap, ALU.mult, ALU.add, [cu, cw, yc], [yc])
            stt(yc.ap, cu.ap[:, 2:T + 2], cw.ap[:, ech, 2:3], yc.ap, ALU.mult, ALU.add, [cu, cw, yc], [yc])
            ycb = yc.ap.rearrange("p (s t) -> p s t", t=256)
            cub = cu.ap[:, 1:T + 1].rearrange("p (s t) -> p s t", t=256)
            stt(ycb[:, 1:4, 0:1], cub[:, 0:3, 255:256], cwf.ap[:, ech, 0:1], ycb[:, 1:4, 0:1], ALU.mult, ALU.add,
                [cu, cwf, yc], [yc])
            stt(ycb[:, 0:3, 255:256], cub[:, 1:4, 0:1], cwf.ap[:, ech, 2:3], ycb[:, 0:3, 255:256], ALU.mult, ALU.add,
                [cu, cwf, yc], [yc])
            tt("dve", ygT.ap[:, ech, :], yc.ap, bgz.ap, ALU.mult, [yc, bgz], [ygT])
        out_proj_residual(("woutc", 0), 8, [ygT], ygT.ap, ws)


    def ret_layer(li, j):
        AR.reset()
        ogT = AR.alloc("ogT", [128, 16, T], BF16)
        qT = AR.alloc("qT", [128, 2, T], BF16)
        qTf = AR.alloc("qTf", [128, 2, T], BF16)
        qTb = AR.alloc("qTb", [128, 2, T], BF16)
        kT = AR.alloc("kT", [128, 2, T], BF16)
        ktf = AR.alloc("ktf", [128, 8, 256], BF16)
        ktb = AR.alloc("ktb", [128, 8, 256], BF16)
        vt = AR.alloc("vt", [128, 8, 512], BF16)
        sgT = AR.alloc("sgT", [128, 4, T], BF16)
        SbE = [AR.alloc(f"SbE{c}", [128, 2, 512], BF16) for c in range(8)]
        Sf = [AR.alloc(f"Sf{d}", [128, 512], F32) for d in range(2)]
        Sb = [AR.alloc(f"Sb{d}", [128, 512], F32) for d in range(2)]
        Sfb = [AR.alloc(f"Sfb{i}", [128, 2, 512], BF16) for i in range(2)]
        QDF = AR.alloc("QDF", [128, T], F32)
        QDB = AR.alloc("QDB", [128, T], F32)
        itab = AR.alloc("itab", [128, 2, T], F32)
        DT = AR.alloc("DT", [128, 128], F32)
        rdt = AR.alloc("rdt", [128, 4, 128], F32)
        dtmp = [AR.alloc(f"dtmp{i}", [128, 128], F32) for i in range(2)]
        lg = AR.alloc("lg", [128, 8], F32)
        rchk = AR.alloc("rchk", [128, 16], F32)
        rkd = AR.alloc("rkd", [128, 2], F32)
        CDf = AR.alloc("CDf", [128, 8], F32)
        CDb = AR.alloc("CDb", [128, 8], F32)
        KD = AR.alloc("KD", [128, 2], F32)
        ATb = [AR.alloc(f"ATb{c}", [128, 128], BF16) for c in range(8)]
        onb = [AR.alloc(f"onb{i}", [128, 512], BF16) for i in range(3)]
        junk = AR.alloc("junk", [128, 512], BF16)
        ss = [AR.alloc(f"ss{i}", [128, 1], F32) for i in range(3)]
        rs = [AR.alloc(f"rs{i}", [128, 1], F32) for i in range(3)]

        S.dma("sp", lg.ap, dec_d.partition_broadcast(128), [], [lg])
        S.dma("sp", itab.ap, rtab_d.partition_broadcast(128), [], [itab])
        S.dma("sp", rchk.ap, rchk_d.partition_broadcast(128), [], [rchk])
        S.dma("sp", rkd.ap, rkd_d, [], [rkd])
        S.dma("sp", rdt.ap, rdt_d.rearrange("a m n -> m a n"), [], [rdt])
        act(lg.ap, lg.ap, AF.Exp, [lg], [lg])
        ts("dve", lg.ap, lg.ap, -1.0, 1.0, ALU.mult, ALU.add, [lg], [lg])
        act(lg.ap, lg.ap, AF.Ln, [lg], [lg])

        for h in range(4):
            lgf = lg.ap[:, h:h + 1]
            lgb = lg.ap[:, 4 + h:5 + h]
            act(QDF.ap, itab.ap[:, 0, :], AF.Exp, [itab, lg], [QDF], scale=lgf)
            act(QDB.ap, itab.ap[:, 1, :], AF.Exp, [itab, lg], [QDB], scale=lgb)
            act(CDf.ap, rchk.ap[:, 0:8], AF.Exp, [rchk, lg], [CDf], scale=lgf)
            act(CDb.ap, rchk.ap[:, 8:16], AF.Exp, [rchk, lg], [CDb], scale=lgb)
            act(KD.ap[:, 0:1], rkd.ap[:, 0:1], AF.Exp, [rkd, lg], [KD], scale=lgf)
            act(KD.ap[:, 1:2], rkd.ap[:, 1:2], AF.Exp, [rkd, lg], [KD], scale=lgb)
            ts("dve", KD.ap, KD.ap, 1.0 / 16.0, None, ALU.mult, None, [KD], [KD])
            act(dtmp[0].ap, rdt.ap[:, 0, :], AF.Exp, [rdt, lg], [dtmp[0]], scale=lgf)
            tt("dve", dtmp[0].ap, dtmp[0].ap, rdt.ap[:, 1, :], ALU.mult, [dtmp[0], rdt], [dtmp[0]])
            act(dtmp[1].ap, rdt.ap[:, 2, :], AF.Exp, [rdt, lg], [dtmp[1]], scale=lgb)
            tt("dve", dtmp[1].ap, dtmp[1].ap, rdt.ap[:, 3, :], ALU.mult, [dtmp[1], rdt], [dtmp[1]])
            tt("dve", DT.ap, dtmp[0].ap, dtmp[1].ap, ALU.add, [dtmp[0], dtmp[1]], [DT])
            ts("dve", DT.ap, DT.ap, 1.0 / 16.0, None, ALU.mult, None, [DT], [DT])

            base = h * 1536
            tA, vA = load_slab(("winb", 0, base, 512))
            for f in range(4):
                for th in range(2):
                    sl = slice(th * 512, (th + 1) * 512)
                    p = psum()
                    mm_group(p, p.ap, [(vA[:, k, f * 128:(f + 1) * 128], hT[k].ap[:, sl]) for k in range(8)], [tA] + hT)
                    if f < 2:
                        cp("act", qT.ap[:, f, sl], p.ap, [p], [qT])
                        tt("dve", qTf.ap[:, f, sl], p.ap, QDF.ap[:, sl], ALU.mult, [p, QDF], [qTf])
                        tt("dve", qTb.ap[:, f, sl], p.ap, QDB.ap[:, sl], ALU.mult, [p, QDB], [qTb])
                    else:
                        cp("act" if th else "dve", kT.ap[:, f - 2, sl], p.ap, [p], [kT])
            for half in range(2):
                pT = psum()
                pTb = pT.ap.bitcast(BF16)

                def fnT(e, pTb=pTb, half=half):
                    ins = None
                    for tbl in range(4):
                        tb = half * 4 + tbl
                        for dh in range(2):
                            ins = e.transpose(pTb[:, (tbl * 2 + dh) * 128:(tbl * 2 + dh + 1) * 128],
                                              kT.ap[:, dh, tb * 128:(tb + 1) * 128], identb.ap)
                    return ins

                S.op("pe", fnT, [kT, identb], [pT])
                src = pTb.rearrange("p (a n) -> p a n", a=4)
                ts("dve", ktf.ap[:, half * 4:(half + 1) * 4, :], src, KD.ap[:, 0:1], None, ALU.mult, None, [pT, KD], [ktf])
                act(ktb.ap[:, half * 4:(half + 1) * 4, :], src, AF.Identity, [pT, KD], [ktb], scale=KD.ap[:, 1:2])
            tB, vB = load_slab(("winb", 0, base + 512, 512))
            for tb in range(8):
                p = psum()
                mm_group(p, p.ap, [(hT[k].ap[:, tb * 128:(tb + 1) * 128], vB[:, k, :]) for k in range(8)], [tB] + hT)
                cp("act" if tb % 2 else "dve", vt.ap[:, tb, :], p.ap, [p], [vt])
            for dh in range(2):
                S.dma("sp", Sf[dh].ap, sf_d[h, dh * 128:(dh + 1) * 128, :], [], [Sf[dh]])
                S.dma("sp", Sb[dh].ap, sb_d[h, dh * 128:(dh + 1) * 128, :], [], [Sb[dh]])
            for c in range(8):
                csl = slice(c * 128, (c + 1) * 128)
                pA = psum()
                mm_group(pA, pA.ap[:, 0:128], [(kT.ap[:, dh, csl], qT.ap[:, dh, csl]) for dh in range(2)], [kT, qT])
                tt("dve", ATb[c].ap, pA.ap[:, 0:128], DT.ap, ALU.mult, [pA, DT], [ATb[c]])
            tC, vC = load_slab(("winb", 0, base + 1024, 512))
            idx = 0
            for f in range(4):
                for th in range(2):
                    sl = slice(th * 512, (th + 1) * 512)
                    p = psum()
                    mm_group(p, p.ap, [(vC[:, k, f * 128:(f + 1) * 128], hT[k].ap[:, sl]) for k in range(8)], [tC] + hT)
                    act(sgT.ap[:, f, sl], p.ap, AF.Silu, [p], [sgT])
                    c = 7 - idx
                    idx += 1
                    for dh in range(2):
                        cp("act", SbE[c].ap[:, dh, :], Sb[dh].ap, [Sb[dh]], [SbE[c]])
                        p = psum()
                        mm_group(p, p.ap, [(ktb.ap[:, c, dh * 128:(dh + 1) * 128], vt.ap[:, c, :])], [ktb, vt])
                        stt(Sb[dh].ap, Sb[dh].ap, CDb.ap[:, c:c + 1], p.ap, ALU.mult, ALU.add, [Sb[dh], CDb, p], [Sb[dh]])
                        if c % 2 == 0:
                            S.dma("sp", sbo_d[c // 2, h, dh * 128:(dh + 1) * 128, :], Sb[dh].ap, [Sb[dh]], [])
            for dh in range(2):
                cp("act", Sfb[0].ap[:, dh, :], Sf[dh].ap, [Sf[dh]], [Sfb[0]])

            def finish_chunk(c):
                csl = slice(c * 128, (c + 1) * 128)
                ob = onb[c % 3]
                pT = psum()
                pTb = pT.ap.bitcast(BF16)

                def fnT(e, pTb=pTb, ob=ob):
                    ins = None
                    for jv in range(4):
                        ins = e.transpose(pTb[:, jv * 128:(jv + 1) * 128], ob.ap[:, jv * 128:(jv + 1) * 128], identb.ap)
                    return ins

                S.op("pe", fnT, [ob, identb], [pT])
                tt("dve", ogT.ap[:, h * 4:(h + 1) * 4, csl], pTb[:, 0:512].rearrange("p (a n) -> p a n", a=4),
                   sgT.ap[:, :, csl], ALU.mult, [pT, sgT], [ogT])

            pO_l = {}

            def rms_tail(c):
                pO = pO_l.pop(c)
                ss_, rs_, ob = ss[c % 3], rs[c % 3], onb[c % 3]
                ts("dve", rs_.ap, ss_.ap, 1.0 / 512.0, 1e-6, ALU.mult, ALU.add, [ss_], [rs_])
                act(rs_.ap, rs_.ap, AF.Ln, [rs_], [rs_])
                act(rs_.ap, rs_.ap, AF.Exp, [rs_], [rs_], scale=-0.5)
                ts("dve", ob.ap, pO.ap, rs_.ap[:, 0:1], None, ALU.mult, None, [pO, rs_], [ob])

            for c in range(8):
                csl = slice(c * 128, (c + 1) * 128)
                cur = Sfb[c % 2]
                nxt = Sfb[(c + 1) % 2]
                pU = []
                for dh in range(2):
                    p = psum()
                    mm_group(p, p.ap, [(ktf.ap[:, c, dh * 128:(dh + 1) * 128], vt.ap[:, c, :])], [ktf, vt])
                    pU.append(p)
                pO = psum()
                pairs = [(ATb[c].ap, vt.ap[:, c, :])]
                pairs += [(qTf.ap[:, dh, csl], cur.ap[:, dh, :]) for dh in range(2)]
                pairs += [(qTb.ap[:, dh, csl], SbE[c].ap[:, dh, :]) for dh in range(2)]
                mm_group(pO, pO.ap, pairs, [ATb[c], vt, qTf, cur, qTb, SbE[c]])
                pO_l[c] = pO
                for dh in range(2):
                    stt(Sf[dh].ap, Sf[dh].ap, CDf.ap[:, c:c + 1], pU[dh].ap, ALU.mult, ALU.add, [Sf[dh], CDf, pU[dh]], [Sf[dh]])
                    cp("act", nxt.ap[:, dh, :], Sf[dh].ap, [Sf[dh]], [nxt])
                    if c % 2 == 1:
                        S.dma("sp", sfo_d[c // 2, h, dh * 128:(dh + 1) * 128, :], Sf[dh].ap, [Sf[dh]], [])
                ss_ = ss[c % 3]
                act(junk.ap, pO.ap, AF.Square, [pO], [junk, ss_], accum_out=ss_.ap[:, 0:1])
                if c >= 1:
                    rms_tail(c - 1)
                if c >= 2:
                    finish_chunk(c - 2)
            rms_tail(7)
            finish_chunk(6)
            finish_chunk(7)
        ws = ln_ws_alias(SbE[0], SbE, qT, [qT, qTf, qTb, kT])
        out_proj_residual(("woutb", 0), 16, [ogT], ogT.ap, ws)

    def attn_layer(li, j):
        lam_init = 0.8 - 0.6 * math.exp(-0.3 * li)
        AR.reset()
        on_all = AR.alloc("on_all", [128, 8, 8, 128], BF16)
        QT = AR.alloc("QT", [128, 8, T], BF16)
        ogT = QT
        KT = AR.alloc("KT", [128, 8, 1280], BF16)
        Va = AR.alloc("Va", [128, 10, 4, 129], BF16)
        ET = [AR.alloc(f"ET{i}", [128, 10, 1024], BF16) for i in range(2)]
        kst = [AR.alloc(f"kst{i}", [128, 512], F32) for i in range(2)]
        vst = kst
        qr = [AR.alloc(f"qr{i}", [128, 512], BF16) for i in range(2)]
        rt = [AR.alloc(f"rt{i}", [128, 8, 32], F32) for i in range(8)]
        ropeC = AR.alloc("ropeC", [128, 8, 32], F32)
        ropeS = AR.alloc("ropeS", [128, 8, 32], F32)
        Osave = AR.alloc("Osave", [128, 8, 129], F32)
        otmp = AR.alloc("otmp", [128, 128], F32)
        ofp = AR.alloc("ofp", [128, 8, 128], F32)
        junk = AR.alloc("junk", [128, 128], F32)
        ss = AR.alloc("ss", [128, 8], F32)
        rstd = AR.alloc("rstd", [128, 8], F32)
        r0 = AR.alloc("r0", [128, 1], F32)
        r1 = AR.alloc("r1", [128, 1], F32)
        lamt = AR.alloc("lamt", [128, 256], F32)
        lprod = AR.alloc("lprod", [128, 2, 64], F32)
        lsum = AR.alloc("lsum", [128, 2], F32)
        nlam = AR.alloc("nlam", [128, 1], F32)
        subS = AR.alloc("subS", [128, 2], F32)
        szb = [AR.alloc(f"szb{i}", [128, 512], BF16) for i in range(2)]

        S.dma("sp", ropeC.ap, ropec_d, [], [ropeC])
        S.dma("sp", ropeS.ap, ropes_d, [], [ropeS])
        S.dma("sp", lamt.ap, lam_d[j].partition_broadcast(128), [], [lamt])
        S.dma("sp", subS.ap, subln_d, [], [subS])
        S.dma("sp", QT.ap[64:69, :, :], maskq_d, [], [QT])
        S.dma("sp", KT.ap[64:69, :, :], maskk_d, [], [KT])
        lt = lamt.ap.rearrange("p (a b) -> p a b", a=4)
        tt("dve", lprod.ap[:, 0, :], lt[:, 0, :], lt[:, 1, :], ALU.mult, [lamt], [lprod])
        tt("dve", lprod.ap[:, 1, :], lt[:, 2, :], lt[:, 3, :], ALU.mult, [lamt], [lprod])
        S.op("dve", lambda e: e.tensor_reduce(out=lsum.ap, in_=lprod.ap, axis=AX.X, op=ALU.add), [lprod], [lsum])
        act(lsum.ap, lsum.ap, AF.Exp, [lsum], [lsum])
        tt("dve", nlam.ap, lsum.ap[:, 1:2], lsum.ap[:, 0:1], ALU.subtract, [lsum], [nlam])
        ts("dve", nlam.ap, nlam.ap, -lam_init, None, ALU.add, None, [nlam], [nlam])
        ts("dve", subS.ap, subS.ap, 1.0 - lam_init, None, ALU.mult, None, [subS], [subS])
        S.op("dve", lambda e: e.memset(Va.ap[:, :, :, 128:129], 1.0), [], [Va])

        w_in = wina_d[j]
        cnt = [0]

        rope_i = [0]

        def rope(p, tb, dst4):
            rt_ = rt[4 * (rope_i[0] % 2):4 * (rope_i[0] % 2) + 4]
            rope_i[0] += 1
            p4 = p.ap.rearrange("p (a b c) -> p a b c", a=8, b=2, c=32)
            x1 = p4[:, :, 0, :]
            x2 = p4[:, :, 1, :]
            Cb = ropeC.ap[:, tb:tb + 1, :].to_broadcast([128, 8, 32])
            Sb_ = ropeS.ap[:, tb:tb + 1, :].to_broadcast([128, 8, 32])
            tt("dve", rt_[0].ap, x1, Cb, ALU.mult, [p, ropeC], [rt_[0]])
            tt("dve", rt_[1].ap, x2, Sb_, ALU.mult, [p, ropeS], [rt_[1]])
            tt("dve", rt_[2].ap, x1, Sb_, ALU.mult, [p, ropeS], [rt_[2]])
            tt("dve", rt_[3].ap, x2, Cb, ALU.mult, [p, ropeC], [rt_[3]])
            return [(dst4[:, :, 0, :], rt_[0], rt_[1], ALU.subtract), (dst4[:, :, 1, :], rt_[2], rt_[3], ALU.add)]

        for g in range(2):
            tQ, vQ = load_slab(("wina", j, g * 512, 512))
            QTv = QT.ap.rearrange("p (h m) t -> p h m t", m=2)
            KTv = KT.ap.rearrange("p (h m) t -> p h m t", m=2)

            def transpose_evac(src_tile, src_ap, dstv, dst_tile, col0, idn, bf):
                pT = psum()
                pTv = pT.ap.bitcast(BF16) if bf else pT.ap

                def fnT(e, pTv=pTv, src_ap=src_ap):
                    ins = None
                    for hl in range(4):
                        ins = e.transpose(pTv[:, hl * 128:(hl + 1) * 128], src_ap[:, hl * 128:(hl + 1) * 128], idn.ap)
                    return ins

                S.op("pe", fnT, [src_tile, idn], [pT])
                cs = slice(col0, col0 + 128)
                cp("act", dstv[0:64, :, 0, cs], pTv[0:64, 0:512].rearrange("p (a n) -> p a n", a=4), [pT], [dst_tile])
                cp("act", dstv[0:64, :, 1, cs], pTv[64:128, 0:512].rearrange("p (a n) -> p a n", a=4), [pT], [dst_tile])

            def proj_rope(tW, vW, tb, dst_tile):
                tsl = slice(tb * 128, (tb + 1) * 128)
                p = psum()
                mm_group(p, p.ap, [(hT[k].ap[:, tsl], vW[:, k, :]) for k in range(8)], [tW] + hT)
                for dst, a_, b_, op in rope(p, tb, dst_tile.ap.rearrange("p (a b c) -> p a b c", a=8, b=2, c=32)):
                    tt("pool", dst, a_.ap, b_.ap, op, [a_, b_], [dst_tile])

            for tb in range(9):
                if tb < 8:
                    proj_rope(tQ, vQ, tb, qr[tb % 2])
                if tb >= 1:
                    transpose_evac(qr[(tb - 1) % 2], qr[(tb - 1) % 2].ap, QTv, QT, (tb - 1) * 128, identb, True)
            tK, vK = load_slab(("wina", j, 1024 + g * 512, 512))
            for tb in range(9):
                if tb < 8:
                    proj_rope(tK, vK, tb, kst[tb % 2])
                if tb >= 1:
                    k_ = kst[(tb - 1) % 2]
                    kb_ = qr[(tb - 1) % 2]
                    tsl = slice((tb - 1) * 128, tb * 128)
                    S.dma("sp", cko_d[j, tsl, g * 512:(g + 1) * 512], k_.ap, [k_], [])
                    cp("act", kb_.ap, k_.ap, [k_], [kb_])
                    transpose_evac(kb_, kb_.ap, KTv, KT, (tb - 1) * 128, identb, True)
            for sb_ in range(2):
                S.dma("pool", qr[sb_].ap, ck_d[j, sb_ * 128:(sb_ + 1) * 128, g * 512:(g + 1) * 512], [], [qr[sb_]])
            for sb_ in range(2):
                transpose_evac(qr[sb_], qr[sb_].ap, KTv, KT, 1024 + sb_ * 128, identb, True)
            tV, vV = load_slab(("wina", j, 2048 + g * 512, 512))
            for tb in range(8):
                tsl = slice(tb * 128, (tb + 1) * 128)
                p = psum()
                mm_group(p, p.ap, [(hT[k].ap[:, tsl], vV[:, k, :]) for k in range(8)], [tV] + hT)
                cp("act", Va.ap[:, tb, :, 0:128], p.ap.rearrange("p (a n) -> p a n", a=4), [p], [Va])
                v_ = vst[tb % 2]
                cp("dve", v_.ap, p.ap, [p], [v_])
                S.dma("sp", cvo_d[j, tsl, g * 512:(g + 1) * 512], v_.ap, [v_], [])
            for sb_ in range(2):
                S.dma("pool", Va.ap[:, 8 + sb_, :, 0:128],
                      cv_d[j, sb_ * 128:(sb_ + 1) * 128, g * 512:(g + 1) * 512].rearrange("p (a n) -> p a n", a=4), [], [Va])
            units = [(hl, m) for hl in range(4) for m in range(2)]
            psum_double_mode(True)

            def stageA(u, E):
                hl, m = u
                hm = hl * 2 + m
                items = []
                for kb in range(10):
                    def it(kb=kb):
                        pS = psum2()

                        def fn(e, pS=pS, kb=kb):
                            ins = None
                            for qh in range(2):
                                ins = e.matmul(pS.ap[:, qh * 512:(qh + 1) * 512], lhsT=KT.ap[0:69, hm, kb * 128:(kb + 1) * 128],
                                               rhs=QT.ap[0:69, hm, qh * 512:(qh + 1) * 512], start=True, stop=True)
                            return ins

                        S.op("pe", fn, [KT, QT], [pS])
                        act(E.ap[:, kb, :], pS.ap, AF.Exp, [pS], [E], scale=0.125)
                    items.append(it)
                return items

            def stageB(u, E):
                hl, m = u
                h = g * 4 + hl
                items = []
                for qb in range(8):
                    def it(qb=qb):
                        pO = psum()
                        mm_group(pO, pO.ap[:, 0:129],
                                 [(E.ap[:, kb, qb * 128:(qb + 1) * 128], Va.ap[:, kb, hl, :]) for kb in range(10)], [E, Va])
                        if m == 0:
                            cp("dve", Osave.ap[:, qb, :], pO.ap[:, 0:129], [pO], [Osave])
                        else:
                            S.op("dve", lambda e, pO=pO: e.reciprocal(out=r1.ap, in_=pO.ap[:, 128:129]), [pO], [r1])
                            S.op("dve", lambda e, qb=qb: e.reciprocal(out=r0.ap, in_=Osave.ap[:, qb, 128:129]), [Osave], [r0])
                            ts("dve", r1.ap, r1.ap, nlam.ap[:, 0:1], None, ALU.mult, None, [r1, nlam], [r1])
                            ts("dve", otmp.ap, Osave.ap[:, qb, 0:128], r0.ap[:, 0:1], None, ALU.mult, None, [Osave, r0], [otmp])
                            stt(ofp.ap[:, qb, :], pO.ap[:, 0:128], r1.ap[:, 0:1], otmp.ap, ALU.mult, ALU.add, [pO, r1, otmp], [ofp])
                            S.op("dve", lambda e, qb=qb: e.scalar_tensor_tensor(out=junk.ap, in0=ofp.ap[:, qb, :], scalar=1.0, in1=ofp.ap[:, qb, :],
                                                                             op0=ALU.mult, op1=ALU.mult, accum_out=ss.ap[:, qb:qb + 1]),
                                 [ofp], [junk, ss])
                    items.append(it)
                if m == 1:
                    def fin():
                        ts("dve", rstd.ap, ss.ap, 1.0 / 128.0, 1e-6, ALU.mult, ALU.add, [ss], [rstd])
                        act(rstd.ap, rstd.ap, AF.Ln, [rstd], [rstd])
                        act(rstd.ap, rstd.ap, AF.Exp, [rstd], [rstd], scale=-0.5)
                        tt("dve", on_all.ap[:, :, h, :], ofp.ap, rstd.ap.unsqueeze(2).to_broadcast([128, 8, 128]), ALU.mult,
                           [ofp, rstd], [on_all])
                    items.append(fin)
                return items

            Ebuf = {}
            for i in range(len(units) + 1):
                A = []
                B = []
                if i < len(units):
                    Ebuf[i] = ET[cnt[0] % 2]
                    cnt[0] += 1
                    A = stageA(units[i], Ebuf[i])
                if i >= 1:
                    B = stageB(units[i - 1], Ebuf[i - 1])
                pattern = [3, 2, 1, 1, 1, 1, 1]
                ai = 0
                bi = 0
                for npat in pattern:
                    for _ in range(npat):
                        if ai < len(A):
                            A[ai]()
                            ai += 1
                    if bi < len(B):
                        B[bi]()
                        bi += 1
                while ai < len(A):
                    A[ai]()
                    ai += 1
                while bi < len(B):
                    B[bi]()
                    bi += 1
            psum_double_mode(False)
        for g in range(2):
            tZ, vZ = load_slab(("wina", j, 3072 + g * 512, 512))
            for tb in range(8):
                tsl = slice(tb * 128, (tb + 1) * 128)
                p = psum()
                mm_group(p, p.ap, [(hT[k].ap[:, tsl], vZ[:, k, :]) for k in range(8)], [tZ] + hT)
                z_ = szb[tb % 2]
                act(z_.ap, p.ap, AF.Silu, [p], [z_])
                dst = on_all.ap[:, tb, g * 4:(g + 1) * 4, :]
                tt("dve", dst, dst, z_.ap.rearrange("p (a n) -> p a n", a=4), ALU.mult, [on_all, z_], [on_all])
        for tb in range(8):
            tsl = slice(tb * 128, (tb + 1) * 128)
            pT = psum()
            pTb = pT.ap.bitcast(BF16)

            def fnT(e, pTb=pTb, tb=tb):
                ins = None
                for h in range(8):
                    ins = e.transpose(pTb[:, h * 128:(h + 1) * 128], on_all.ap[:, tb, h, :], identb.ap)
                return ins

            S.op("pe", fnT, [on_all, identb], [pT])
            ts("dve" if tb % 2 else "dve", ogT.ap[:, :, tsl], pTb.rearrange("p (a n) -> p a n", a=8), subS.ap[:, j:j + 1], None,
               ALU.mult, None, [pT, subS], [ogT])
        ws = ln_ws_alias(ET[0], [ET[0]], ET[1], [ET[1]])
        out_proj_residual(("wouta", j), 8, [ogT], ogT.ap, ws)

    AR.reset()
    xin = [AR.alloc(f"xin{i}", [128, D], F32) for i in range(4)]

    def x_block(tb):
        xi = xin[tb % 4]
        S.dma("sp", xi.ap, x_d[tb * 128:(tb + 1) * 128, :], [], [xi])
        for half in range(2):
            p = psum()

            def fn(e, p=p, xi=xi, half=half):
                ins = None
                for q in range(4):
                    c = half * 4 + q
                    ins = e.transpose(p.ap[:, q * 128:(q + 1) * 128], xi.ap[:, c * 128:(c + 1) * 128], ident.ap)
                return ins

            S.op("pe", fn, [xi, ident], [p])
            dst = xT_t[:, half * 4:half * 4 + 4, tb * 128:(tb + 1) * 128]
            src = p.ap.rearrange("p (q n) -> p q n", q=4)
            cp("dve" if half == 0 else "act", dst, src, [p], xT[half * 4:half * 4 + 4])

    x_block(0)
    x_block(1)
    for s6 in range(4):
        mod_slab(0, s6)
        if s6 < 3:
            x_block(2 + 2 * s6)
            x_block(3 + 2 * s6)

    for li, (kind, j) in enumerate(layers):
        modulation(li)
        if stop_after == "mod":
            break
        if kind == 2:
            conv_layer(li, j)
        elif kind == 1:
            ret_layer(li, j)
        else:
            attn_layer(li, j)
        if stop_after == "mix":
            break
        layer_norm(li)

    AR.reset()
    yout = [AR.alloc(f"yout{i}", [128, D], F32) for i in range(2)]
    if debug_xT:
        for c in range(NCH):
            S.dma("sp", dbg_d[:, c, :], xT[c].ap, [xT[c]], [])
    for tb in range(8):
        yo = yout[tb % 2]
        for half in range(2):
            p = psum()

            def fn(e, p=p, tb=tb, half=half):
                ins = None
                for q in range(4):
                    c = half * 4 + q
                    ins = e.transpose(p.ap[:, q * 128:(q + 1) * 128], xT_t[:, c, tb * 128:(tb + 1) * 128], ident.ap)
                return ins

            S.op("pe", fn, xT[half * 4:half * 4 + 4] + [ident], [p])
            cp("dve" if half == 0 else "act", yo.ap[:, half * 512:(half + 1) * 512], p.ap, [p], [yo])
        S.dma("sp", y_d[tb * 128:(tb + 1) * 128, :], yo.ap, [yo], [])
    S.finish("sp")

    with nc.Block() as block:
        @block.tensor
        def _(e):
            S.replay("pe", e)

        @block.scalar
        def _(e):
            S.replay("act", e)

        @block.vector
        def _(e):
            S.replay("dve", e)

        @block.gpsimd
        def _(e):
            S.replay("pool", e)

        @block.sync
        def _(e):
            S.replay("sp", e)

    return nc


def _col(v, n):
    return np.ascontiguousarray(np.asarray(v, np.float32).reshape(n, 128).T)


def make_in_maps(inp, layers=LAYERS):
    f32 = np.float32
    g = {k: np.asarray(v) for k, v in inp.items()}
    shared = {}
    shared["bmod"] = np.stack([_col(g["b_mod"][i], 24) for i in range(DEPTH)]).astype(f32)
    shared["lng"] = np.stack([_col(g["ln_g"][i], 8) for i in range(DEPTH)]).astype(f32)
    shared["lnb"] = np.stack([_col(g["ln_b"][i], 8) for i in range(DEPTH)]).astype(f32)
    kinds = set(k for k, _ in layers)
    shared["wmod"] = np.ascontiguousarray(g["w_mod"][:max(1, len(layers))], f32)
    if 0 in kinds:
        shared["wina"] = np.ascontiguousarray(g["w_in_a"], f32)
        shared["wouta"] = np.ascontiguousarray(g["w_out_a"], f32)
    wb = g["w_in_b"][0]
    cols = []
    for h in range(4):
        cols.append(wb[:, h * 256:(h + 1) * 256])
        cols.append(wb[:, 1024 + h * 256:1024 + (h + 1) * 256])
        cols.append(wb[:, 2048 + h * 512:2048 + (h + 1) * 512])
        cols.append(wb[:, 4096 + h * 512:4096 + (h + 1) * 512])
    if 1 in kinds:
        shared["winb"] = np.ascontiguousarray(np.concatenate(cols, 1), f32)
        shared["woutb"] = np.ascontiguousarray(g["w_out_b"][0], f32)
    wc = g["w_in_c"][0]
    cols = []
    for e in range(8):
        for q in range(4):
            cols.append(wc[:, q * 1024 + e * 128:q * 1024 + (e + 1) * 128])
    if 2 in kinds:
        shared["winc"] = np.ascontiguousarray(np.concatenate(cols, 1), f32)
        shared["woutc"] = np.ascontiguousarray(g["w_out_c"][0], f32)
    cw = g["conv_c"][0]
    shared["convw"] = np.ascontiguousarray(cw.reshape(3, 8, 128).transpose(2, 1, 0), f32)
    shared["lam"] = np.ascontiguousarray(g["lam_a"].reshape(2, 256), f32)
    shared["subln"] = np.ascontiguousarray(g["subln_a"].T, f32)
    shared["dec"] = np.concatenate([g["decay_fwd"][0], g["decay_bwd"][0]]).astype(f32)
    shared["ident"] = np.eye(128, dtype=f32)
    m = np.arange(128, dtype=f32)[:, None]
    n = np.arange(128, dtype=f32)[None, :]
    shared["rdt"] = np.stack([np.maximum(n - m, 0), (n >= m).astype(f32), np.maximum(m - n, 0), (n <= m).astype(f32)]).astype(f32)
    shared["rkd"] = np.stack([127.0 - np.arange(128), np.arange(128)], 1).astype(f32)

    tok = np.arange(T)
    r = (tok // 64).astype(np.float64)
    col = (tok % 64).astype(np.float64)
    inv = 10000.0 ** (-np.arange(16, dtype=np.float64) / 16)
    ang = np.concatenate([r[:, None] * inv, col[:, None] * inv], -1)
    cosS = np.cos(ang).astype(f32).reshape(8, 128, 32).transpose(1, 0, 2)
    sinS = np.sin(ang).astype(f32).reshape(8, 128, 32).transpose(1, 0, 2)

    in_maps = []
    for r_ in range(8):
        d = dict(shared)
        if r_ < 4:
            d["x"] = np.ascontiguousarray(g["x_prompt"][4 * r_:4 * r_ + 4].reshape(T, D), f32)
            d["cvec"] = _col(g["c_ctx"], 8)
            d["cachek"] = np.zeros((2, 256, 1024), f32)
            d["cachev"] = np.zeros((2, 256, 1024), f32)
            d["statef"] = np.zeros((4, 256, 512), f32)
            d["stateb"] = np.zeros((4, 256, 512), f32)
            d["ropec"] = np.ones((128, 8, 32), f32)
            d["ropes"] = np.zeros((128, 8, 32), f32)
            seq_q = np.arange(T) // 256
            seq_k = np.concatenate([np.arange(T) // 256, np.full(256, 4)])
            mq = np.zeros((5, T), f32)
            for grp in range(5):
                mq[grp] = np.where(seq_q == grp, 0.0, NEG)
            mk = np.zeros((5, 1280), f32)
            for grp in range(5):
                mk[grp] = (seq_k == grp).astype(f32)
            d["maskq"] = np.ascontiguousarray(np.broadcast_to(mq[:, None, :], (5, 8, T))).astype(ml_dtypes.bfloat16)
            d["maskk"] = np.ascontiguousarray(np.broadcast_to(mk[:, None, :], (5, 8, 1280))).astype(ml_dtypes.bfloat16)
            d["cflag"] = np.ones((128, 1), f32)
            fstart = np.array([c % 2 == 0 for c in range(8)])
            bstart = np.array([c % 2 == 1 for c in range(8)])
        else:
            b = r_ - 4
            d["x"] = np.ascontiguousarray(g["x_sample"][b], f32)
            d["cvec"] = _col(g["c"][b], 8)
            d["cachek"] = np.ascontiguousarray(g["cache_k"][b].reshape(2, 256, 1024), f32)
            d["cachev"] = np.ascontiguousarray(g["cache_v"][b].reshape(2, 256, 1024), f32)
            d["statef"] = np.ascontiguousarray(g["state_fwd"][b, 0], f32)
            d["stateb"] = np.ascontiguousarray(g["state_bwd"][b, 0], f32)
            d["ropec"] = np.ascontiguousarray(cosS)
            d["ropes"] = np.ascontiguousarray(sinS)
            d["maskq"] = np.zeros((5, 8, T), ml_dtypes.bfloat16)
            d["maskk"] = np.zeros((5, 8, 1280), ml_dtypes.bfloat16)
            d["cflag"] = np.zeros((128, 1), f32)
            fstart = np.zeros(8, bool)
            bstart = np.zeros(8, bool)
        i_in = (np.arange(T) % 128).astype(f32)
        cidx = np.arange(T) // 128
        qf = np.where(fstart[cidx], BIG, i_in + 1.0)
        qb = np.where(bstart[cidx], BIG, 128.0 - i_in)
        d["rtab"] = np.stack([qf, qb]).astype(f32)
        d["rchk"] = np.concatenate([np.where(fstart, BIG, 128.0), np.where(bstart, BIG, 128.0)]).astype(f32)
        in_maps.append(d)
    return in_maps


_NC_CACHE = {}


def kernel(**inputs):
    if "full" not in _NC_CACHE:
        _NC_CACHE["full"] = build()
    nc = _NC_CACHE["full"]
    in_maps = make_in_maps(inputs)
    res = run_bass_kernel_spmd(nc, in_maps, core_ids=list(range(8)))
    R = res.results
    f32 = np.float32
    y_prompt = np.stack([R[r]["y"].reshape(4, 256, D) for r in range(4)]).reshape(16, 256, D).astype(f32)
    y_sample = np.stack([R[4 + b]["y"] for b in range(4)]).astype(f32)
    nk = np.stack([R[r]["cko"].reshape(2, 4, 256, 8, 128).transpose(1, 0, 2, 3, 4) for r in range(4)]).reshape(16, 2, 256, 8, 128)
    nv = np.stack([R[r]["cvo"].reshape(2, 4, 256, 8, 128).transpose(1, 0, 2, 3, 4) for r in range(4)]).reshape(16, 2, 256, 8, 128)
    nsf = np.stack([R[r]["sfo"] for r in range(4)]).reshape(16, 1, 4, 256, 512)
    nsb = np.stack([R[r]["sbo"] for r in range(4)]).reshape(16, 1, 4, 256, 512)
    return (y_prompt, y_sample, nk.astype(f32), nv.astype(f32), nsf.astype(f32), nsb.astype(f32))
```

```python
import math
import numpy as np
import ml_dtypes
import concourse.bass as bass
import concourse.mybir as mybir
from concourse.bass_utils import run_bass_kernel_spmd

F32 = mybir.dt.float32
BF16 = mybir.dt.bfloat16
AF = mybir.ActivationFunctionType
ALU = mybir.AluOpType
AX = mybir.AxisListType

D = 1024
T = 1024
NCH = 8
DEPTH = 4
ALPHA = (2.0 * DEPTH) ** 0.25
LN_EPS = 1e-5
NEG = -30000.0
BIG = 1.0e6
SAME_ENGINE_SYNC = True
NDMA = 16


class Tile:
    __slots__ = ("ap", "w", "r", "name", "psum")

    def __init__(self, ap, name="", psum=False):
        self.ap = ap
        self.w = None
        self.r = {}
        self.name = name
        self.psum = psum

    def __getitem__(self, k):
        return self.ap[k]


class _Eng:
    def __init__(self, name, sem):
        self.name = name
        self.sem = sem
        self.count = 0
        self.ops = []
        self.seen = {}
        self.dma_sems = []
        self.dma_cnt = []
        self.dma_i = 0


class Sched:
    def __init__(self, nc):
        self.nc = nc
        self.E = {}
        self.semobjs = {}

    def add_engine(self, name, sem, dma_sems=()):
        e = _Eng(name, sem)
        e.dma_sems = list(dma_sems)
        e.dma_cnt = [0] * len(e.dma_sems)
        self.E[name] = e
        self.semobjs[id(sem)] = sem
        for s in dma_sems:
            self.semobjs[id(s)] = s

    def _collect(self, E, reads, writes, waits):
        def need(tok):
            if tok is None:
                return
            sid, val = tok
            if sid == id(E.sem) and (E.name == "pe" or not SAME_ENGINE_SYNC):
                return
            if E.seen.get(sid, 0) >= val:
                return
            if waits.get(sid, 0) < val:
                waits[sid] = val

        for t in reads:
            need(t.w)
            if t.psum:
                for sid, val in t.r.items():
                    if sid != id(E.sem):
                        need((sid, val))
        for t in writes:
            need(t.w)
            for sid, val in t.r.items():
                need((sid, val))

    def op(self, eng, fn, reads=(), writes=()):
        E = self.E[eng]
        waits = {}
        self._collect(E, reads, writes, waits)
        for sid, val in waits.items():
            E.seen[sid] = val
        E.count += 1
        tok = (id(E.sem), E.count)
        E.ops.append((list(waits.items()), fn, E.sem, 1))
        for t in reads:
            if t.r.get(tok[0], 0) < tok[1]:
                t.r[tok[0]] = tok[1]
        for t in writes:
            t.w = tok
            t.r = {}
        return tok

    def dma(self, eng, out, in_, reads=(), writes=()):
        E = self.E[eng]
        slot = E.dma_i % len(E.dma_sems)
        E.dma_i += 1
        sem = E.dma_sems[slot]
        waits = {}
        prev = E.dma_cnt[slot]
        if prev > 0 and E.seen.get(id(sem), 0) < prev:
            waits[id(sem)] = prev
        E.dma_cnt[slot] += 16
        val = E.dma_cnt[slot]
        self._collect(E, reads, writes, waits)
        for sid, v in waits.items():
            E.seen[sid] = v
        tok = (id(sem), val)

        def fn(e, out=out, in_=in_):
            return e.dma_start(out=out, in_=in_)

        E.ops.append((list(waits.items()), fn, sem, 16))
        for t in reads:
            if t.r.get(tok[0], 0) < tok[1]:
                t.r[tok[0]] = tok[1]
        for t in writes:
            t.w = tok
            t.r = {}
        return tok

    def finish(self, eng="sp"):
        E = self.E[eng]
        waits = {}
        for o in self.E.values():
            if o.count > 0 and o is not E:
                waits[id(o.sem)] = o.count
            for s, c in zip(o.dma_sems, o.dma_cnt):
                if c > 0:
                    waits[id(s)] = c
        E.ops.append((list(waits.items()), None, None, 0))

    def replay(self, name, e):
        E = self.E[name]
        for waits, fn, sem, inc in E.ops:
            for sid, val in waits:
                e.wait_ge(self.semobjs[sid], val)
            if fn is None:
                continue
            ins = fn(e)
            ins.then_inc(sem, inc)


LAYERS = [(0, 0), (1, 0), (2, 0), (0, 1)]


def build(layers=LAYERS, debug_xT=False, stop_after=None):
    nc = bass.Bass("TRN2", target_bir_lowering=False)
    dt_in = lambda name, shape, dt=F32: nc.dram_tensor(name, list(shape), dt, kind="ExternalInput").ap()
    dt_out = lambda name, shape, dt=F32: nc.dram_tensor(name, list(shape), dt, kind="ExternalOutput").ap()

    x_d = dt_in("x", [T, D])
    cvec_d = dt_in("cvec", [128, 8])
    bmod_d = dt_in("bmod", [DEPTH, 128, 24])
    lng_d = dt_in("lng", [DEPTH, 128, 8])
    lnb_d = dt_in("lnb", [DEPTH, 128, 8])
    wmod_d = dt_in("wmod", [max(1, len(layers)), D, 3 * D])
    kinds = set(k for k, _ in layers)
    if 0 in kinds:
        wina_d = dt_in("wina", [2, D, 4096])
        wouta_d = dt_in("wouta", [2, D, D])
    if 1 in kinds:
        winb_d = dt_in("winb", [D, 6144])
        woutb_d = dt_in("woutb", [2048, D])
    if 2 in kinds:
        winc_d = dt_in("winc", [D, 4096])
        woutc_d = dt_in("woutc", [D, D])
    convw_d = dt_in("convw", [128, 8, 3])
    lam_d = dt_in("lam", [2, 256])
    subln_d = dt_in("subln", [128, 2])
    dec_d = dt_in("dec", [8])
    ck_d = dt_in("cachek", [2, 256, 1024])
    cv_d = dt_in("cachev", [2, 256, 1024])
    sf_d = dt_in("statef", [4, 256, 512])
    sb_d = dt_in("stateb", [4, 256, 512])
    ident_d = dt_in("ident", [128, 128])
    ropec_d = dt_in("ropec", [128, 8, 32])
    ropes_d = dt_in("ropes", [128, 8, 32])
    maskq_d = dt_in("maskq", [5, 8, T], BF16)
    maskk_d = dt_in("maskk", [5, 8, 1280], BF16)
    rtab_d = dt_in("rtab", [2, T])
    rchk_d = dt_in("rchk", [16])
    rdt_d = dt_in("rdt", [4, 128, 128])
    rkd_d = dt_in("rkd", [128, 2])
    cflag_d = dt_in("cflag", [128, 1])

    y_d = dt_out("y", [T, D])
    cko_d = dt_out("cko", [2, T, 1024])
    cvo_d = dt_out("cvo", [2, T, 1024])
    sfo_d = dt_out("sfo", [4, 4, 256, 512])
    sbo_d = dt_out("sbo", [4, 4, 256, 512])
    dbg_d = dt_out("dbg", [128, 8, T]) if debug_xT else None

    S = Sched(nc)
    nsem = 5 + 2 * NDMA
    sems = [nc.alloc_semaphore(name=f"s{i}") for i in range(nsem)]
    S.add_engine("pe", sems[0])
    S.add_engine("act", sems[1])
    S.add_engine("dve", sems[2])
    S.add_engine("pool", sems[3], sems[5:5 + NDMA])
    S.add_engine("sp", sems[4], sems[5 + NDMA:5 + 2 * NDMA])

    def sb(name, shape, dt=F32):
        return nc.alloc_sbuf_tensor("sb_" + name, list(shape), dt)

    def tile(name, shape, dt=F32):
        return Tile(sb(name, shape, dt).ap(), name)

    xT = [Tile(None) for _ in range(NCH)]
    xT_t = sb("xT", [128, NCH, T], F32).ap()
    for c in range(NCH):
        xT[c].ap = xT_t[:, c, :]
    hT_t = sb("hT", [128, NCH, T], BF16).ap()
    hT = [Tile(hT_t[:, c, :], f"hT{c}") for c in range(NCH)]
    NSLAB = 3
    slab_t = sb("slab", [128, NSLAB, 8 * 512], BF16).ap()
    slabs = [Tile(slab_t[:, i, :], f"slab{i}") for i in range(NSLAB)]
    slab_i = [0]

    ident = tile("ident", [128, 128], F32)
    identb = tile("identb", [128, 128], BF16)
    onesM = tile("onesM", [128, 128], BF16)
    cvec = tile("cvec", [128, 8], F32)
    scb = tile("scb", [128, 8], BF16)
    bmod = tile("bmod", [128, DEPTH, 24], F32)
    lng = tile("lng", [128, DEPTH, 8], F32)
    lnb = tile("lnb", [128, DEPTH, 8], F32)
    sc1 = tile("sc1", [128, 8], F32)
    gA = tile("gA", [128, 8], F32)
    epsln = tile("epsln", [128, 1], F32)

    pd_t = [nc.alloc_psum_tensor(f"pd{i}", [128, 1024], F32).ap() for i in range(4)]
    PS = [Tile(pd_t[i // 2][:, (i % 2) * 512:(i % 2 + 1) * 512], f"ps{i}", psum=True) for i in range(8)]
    PD = [Tile(pd_t[i], f"pd{i}", psum=True) for i in range(2)]
    ps_i = [0]
    pd_i = [0]
    ps_mode = {"double": False}

    def _xfer(srcs, dsts):
        acc = {}
        for t in srcs:
            if t.w is not None and acc.get(t.w[0], 0) < t.w[1]:
                acc[t.w[0]] = t.w[1]
            for s_, v_ in t.r.items():
                if acc.get(s_, 0) < v_:
                    acc[s_] = v_
        for t in dsts:
            t.w = None
            t.r = dict(acc)

    def psum_double_mode(on):
        if on:
            _xfer(PS[0:2], [PD[0]])
            _xfer(PS[2:4], [PD[1]])
        else:
            _xfer([PD[0]], PS[0:2])
            _xfer([PD[1]], PS[2:4])
        ps_mode["double"] = on

    def psum():
        if ps_mode["double"]:
            t = PS[4 + ps_i[0] % 4]
        else:
            t = PS[ps_i[0] % 8]
        ps_i[0] += 1
        return t

    def psum2():
        t = PD[pd_i[0] % 2]
        pd_i[0] += 1
        return t

    ARENA_BYTES = 132 * 1024
    arena_t = sb("arena", [128, ARENA_BYTES // 4], F32).ap()

    class Arena:
        def __init__(self):
            self.off = 0
            self.live = []
            self.pending = {}
            self.offs = {}

        def reset(self):
            for t in self.live:
                if t.w is not None:
                    s_, v_ = t.w
                    if self.pending.get(s_, 0) < v_:
                        self.pending[s_] = v_
                for s_, v_ in t.r.items():
                    if self.pending.get(s_, 0) < v_:
                        self.pending[s_] = v_
            self.live = []
            self.off = 0

        def alloc(self, name, shape, dt=F32):
            esz = 4 if dt == F32 else 2
            n = int(np.prod(shape[1:]))
            nbytes = (n * esz + 3) // 4 * 4
            assert self.off + nbytes <= ARENA_BYTES, (name, self.off, nbytes)
            words = nbytes // 4
            base = arena_t[0:shape[0], self.off // 4:self.off // 4 + words]
            if dt != F32:
                base = base.bitcast(dt)
            if len(shape) > 2:
                letters = "abcdefg"[:len(shape) - 1]
                pat = "p (" + " ".join(letters) + ") -> p " + " ".join(letters)
                kw = {l: s for l, s in zip(letters, shape[1:])}
                base = base.rearrange(pat, **kw)
            t = Tile(base, name)
            t.r = dict(self.pending)
            self.live.append(t)
            self.offs[id(t)] = (self.off, nbytes)
            self.off += nbytes
            return t

        def view(self, name, first_tile, shape, dt, dep_tiles):
            off = self.offs[id(first_tile)][0]
            esz = 4 if dt == F32 else 2
            n = int(np.prod(shape[1:]))
            words = (n * esz + 3) // 4
            end = max(self.offs[id(d)][0] + self.offs[id(d)][1] for d in dep_tiles)
            assert off + words * 4 <= end, (name, off, words * 4, end)
            base = arena_t[0:shape[0], off // 4:off // 4 + words]
            if dt != F32:
                base = base.bitcast(dt)
            if len(shape) > 2:
                letters = "abcdefg"[:len(shape) - 1]
                pat = "p (" + " ".join(letters) + ") -> p " + " ".join(letters)
                base = base.rearrange(pat, **{l: s_ for l, s_ in zip(letters, shape[1:])})
            t = Tile(base, name)
            acc = dict(self.pending)
            for d in dep_tiles:
                if d.w is not None and acc.get(d.w[0], 0) < d.w[1]:
                    acc[d.w[0]] = d.w[1]
                for s_, v_ in d.r.items():
                    if acc.get(s_, 0) < v_:
                        acc[s_] = v_
            t.r = acc
            self.live.append(t)
            self.offs[id(t)] = (off, words * 4)
            return t

        def sub(self, parent, ap, name=""):
            t = Tile(ap, name)
            t.r = dict(self.pending)
            self.live.append(t)
            return t

    AR = Arena()

    def act(out, in_, func, reads, writes, bias=None, scale=None, accum_out=None):
        kw = {}
        if bias is not None:
            kw["bias"] = bias
        if scale is not None:
            kw["scale"] = scale
        if accum_out is not None:
            kw["accum_out"] = accum_out
        return S.op("act", lambda e: e.activation(out=out, in_=in_, func=func, **kw), reads, writes)

    def tt(eng, out, in0, in1, op, reads, writes):
        return S.op(eng, lambda e: e.tensor_tensor(out=out, in0=in0, in1=in1, op=op), reads, writes)

    def ts(eng, out, in0, s1, s2, op0, op1, reads, writes):
        if s2 is None:
            return S.op(eng, lambda e: e.tensor_scalar(out=out, in0=in0, scalar1=s1, scalar2=None, op0=op0), reads, writes)
        return S.op(eng, lambda e: e.tensor_scalar(out=out, in0=in0, scalar1=s1, scalar2=s2, op0=op0, op1=op1), reads, writes)

    def stt(out, in0, scalar, in1, op0, op1, reads, writes):
        return S.op("dve", lambda e: e.scalar_tensor_tensor(out=out, in0=in0, scalar=scalar, in1=in1, op0=op0, op1=op1), reads, writes)

    def cp(eng, out, in_, reads, writes):
        if eng == "act":
            return S.op("act", lambda e: e.copy(out=out, in_=in_), reads, writes)
        return S.op(eng, lambda e: e.tensor_copy(out=out, in_=in_), reads, writes)

    def mm_group(out_tile, out_ap, pairs, extra_reads, first=True, last=True):
        n = len(pairs)

        def fn(e):
            ins = None
            for i, (l, r) in enumerate(pairs):
                ins = e.matmul(out_ap, lhsT=l, rhs=r, start=(first and i == 0), stop=(last and i == n - 1))
            return ins

        return S.op("pe", fn, extra_reads, [out_tile])

    def layer_slabs(kind, j):
        L = []
        if kind == 0:
            for g in range(2):
                for sec in range(3):
                    L.append(("wina", j, sec * 1024 + g * 512, 512))
            for g in range(2):
                L.append(("wina", j, 3072 + g * 512, 512))
            for s_ in range(2):
                L.append(("wouta", j, s_ * 512, 512))
        elif kind == 1:
            for h in range(4):
                for q in range(3):
                    L.append(("winb", 0, h * 1536 + q * 512, 512))
            for s_ in range(4):
                L.append(("woutb", 0, s_ * 256, 256))
        else:
            for e_ in range(8):
                L.append(("winc", 0, e_ * 512, 512))
            for s_ in range(2):
                L.append(("woutc", 0, s_ * 512, 512))
        return L

    def slab_plan():
        plan = []
        for li, (kind, j) in enumerate(layers):
            L = layer_slabs(kind, j)
            if li == 0:
                plan += [("wmod", 0, s6 * 512, 512) for s6 in range(4)]
                plan += [L[0], ("wmod", 0, 4 * 512, 512), ("wmod", 0, 5 * 512, 512)] + L[1:]
            else:
                plan += [("wmod", li, s6 * 512, 512) for s6 in range(6)]
                plan += L
        return plan

    PLAN = slab_plan()
    plan_state = {"next": 0, "issued": 0, "views": {}}

    def slab_src(key):
        name, idx, c0, ncol = key
        base = {"wmod": lambda: wmod_d[idx], "wina": lambda: wina_d[idx], "wouta": lambda: wouta_d[idx],
                "winb": lambda: winb_d, "woutb": lambda: woutb_d, "winc": lambda: winc_d, "woutc": lambda: woutc_d}[name]()
        return base[:, c0:c0 + ncol].rearrange("(k p) n -> p k n", p=128)

    def issue_slab(i):
        src_ap = slab_src(PLAN[i])
        t = slabs[i % NSLAB]
        k, n = src_ap.shape[1], src_ap.shape[2]
        dst = t.ap[:, 0:k * n].rearrange("p (k n) -> p k n", k=k)
        S.dma("pool", dst, src_ap, [], [t])
        plan_state["views"][i] = (t, dst)

    def load_slab(key):
        i = plan_state["next"]
        deferred = None
        while key[0] != "wmod" and PLAN[i][0] == "wmod":
            deferred = PLAN[i][1]
            mod_slab(PLAN[i][1], PLAN[i][2] // 512)
            i = plan_state["next"]
        if deferred is not None:
            finish_gate(deferred)
        assert PLAN[i] == key, (i, PLAN[i], key)
        plan_state["next"] += 1
        oldest = plan_state.get("hold", i)
        while plan_state["issued"] < min(len(PLAN), oldest + NSLAB):
            issue_slab(plan_state["issued"])
            plan_state["issued"] += 1
        ret = plan_state["views"].pop(i)
        if False:
            plan_state["hold"] = i
            mod_slab(PLAN[i + 1][1], PLAN[i + 1][2] // 512)
            del plan_state["hold"]
        return ret

    S.dma("sp", ident.ap, ident_d, [], [ident])
    S.dma("sp", cvec.ap, cvec_d, [], [cvec])
    S.dma("sp", bmod.ap, bmod_d.rearrange("l p j -> p l j"), [], [bmod])
    S.dma("sp", lng.ap, lng_d.rearrange("l p j -> p l j"), [], [lng])
    S.dma("sp", lnb.ap, lnb_d.rearrange("l p j -> p l j"), [], [lnb])
    cp("dve", identb.ap, ident.ap, [ident], [identb])
    S.op("dve", lambda e: e.memset(onesM.ap, 1.0 / 1024.0), [], [onesM])
    S.op("dve", lambda e: e.memset(epsln.ap, LN_EPS / (ALPHA * ALPHA)), [], [epsln])
    act(scb.ap, cvec.ap, AF.Silu, [cvec], [scb])

    modv_l = [tile(f"modv{l}", [128, 24], F32) for l in range(len(layers))]

    def mod_slab(li, s6):
        t, v = load_slab(("wmod", li, s6 * 512, 512))
        pm = psum()

        def fn(e, v=v, pm=pm):
            ins = None
            for jj in range(4):
                for k in range(8):
                    ins = e.matmul(pm.ap[:, jj:jj + 1], lhsT=v[:, k, jj * 128:(jj + 1) * 128], rhs=scb.ap[:, k:k + 1],
                                   start=(k == 0), stop=(k == 7))
            return ins

        S.op("pe", fn, [t, scb], [pm])
        tt("dve", modv_l[li].ap[:, 4 * s6:4 * s6 + 4], pm.ap[:, 0:4], bmod.ap[:, li, 4 * s6:4 * s6 + 4], ALU.add,
           [pm, bmod], [modv_l[li]])

    def finish_gate(li):
        modv = modv_l[li]
        ts("dve", gA.ap, modv.ap[:, 16:24], 1.0 / ALPHA, None, ALU.mult, None, [modv], [gA])

    def modulation(li):
        modv = modv_l[li]
        if li > 0:
            for s6 in range(6):
                mod_slab(li, s6)
            finish_gate(li)
        ts("dve", sc1.ap, modv.ap[:, 8:16], 1.0, None, ALU.add, None, [modv], [sc1])
        for c in range(NCH):
            if c % 2 == 0:
                ts("dve", hT[c].ap, xT[c].ap, sc1.ap[:, c:c + 1], modv.ap[:, c:c + 1], ALU.mult, ALU.add,
                   [xT[c], sc1, modv], [hT[c]])
            else:
                act(hT[c].ap, xT[c].ap, AF.Identity, [xT[c], sc1, modv], [hT[c]],
                    bias=modv.ap[:, c:c + 1], scale=sc1.ap[:, c:c + 1])

    def out_proj_residual(wkey, n_e, ogT_tiles, ogT_ap, ws):
        LNWS["ws"] = ws
        ncols = 512 * 8 // n_e
        for s_ in range(D // ncols):
            t, v = load_slab((wkey[0], wkey[1], s_ * ncols, ncols))
            for dcc in range(ncols // 128):
                dc = s_ * (ncols // 128) + dcc
                for th in range(2):
                    p = psum()
                    pairs = [(v[:, e_, dcc * 128:(dcc + 1) * 128], ogT_ap[:, e_, th * 512:(th + 1) * 512]) for e_ in range(n_e)]
                    mm_group(p, p.ap, pairs, [t] + list(ogT_tiles))
                    xs = xT[dc].ap[:, th * 512:(th + 1) * 512]
                    stt(xs, p.ap, gA.ap[:, dc:dc + 1], xs, ALU.mult, ALU.add, [p, gA, xT[dc]], [xT[dc]])
                cp("act", hT[dc].ap, xT[dc].ap, [xT[dc]], [hT[dc]])
                if dc % 2 == 0:
                    act(ws["ysq"].ap[:, dc, :], xT[dc].ap, AF.Square, [xT[dc]], [ws["ysq"]])
                else:
                    tt("dve", ws["ysq"].ap[:, dc, :], xT[dc].ap, xT[dc].ap, ALU.mult, [xT[dc]], [ws["ysq"]])

    LNWS = {}

    def layer_norm(li):
        ws = LNWS["ws"]
        ysq, mean, rstd, tmp = ws["ysq"], ws["mean"], ws["rstd"], ws["tmp"]
        for th in range(2):
            pm = psum()
            mm_group(pm, pm.ap, [(onesM.ap, hT[c].ap[:, th * 512:(th + 1) * 512]) for c in range(NCH)], [onesM] + hT)
            pq = psum()
            mm_group(pq, pq.ap, [(onesM.ap, ysq.ap[:, c, th * 512:(th + 1) * 512]) for c in range(NCH)], [onesM, ysq])
            sl = slice(th * 512, (th + 1) * 512)
            cp("dve", mean.ap[:, sl], pm.ap, [pm], [mean])
            act(tmp[0].ap[:, sl], pm.ap, AF.Square, [pm], [tmp[0]])
            tt("dve", rstd.ap[:, sl], pq.ap, tmp[0].ap[:, sl], ALU.subtract, [pq, tmp[0]], [rstd])
        act(rstd.ap, rstd.ap, AF.Ln, [rstd, epsln], [rstd], bias=epsln.ap[:, 0:1])
        act(rstd.ap, rstd.ap, AF.Exp, [rstd], [rstd], scale=-0.5)
        for c in range(NCH):
            tm = tmp[c % 2]
            tt("dve", tm.ap, xT[c].ap, mean.ap, ALU.subtract, [xT[c], mean], [tm])
            tt("dve", tm.ap, tm.ap, rstd.ap, ALU.mult, [tm, rstd], [tm])
            act(xT[c].ap, tm.ap, AF.Identity, [tm, lng, lnb], [xT[c]],
                bias=lnb.ap[:, li, c:c + 1], scale=lng.ap[:, li, c:c + 1])

    def ln_ws_alloc():
        return {"ysq": AR.alloc("ysq", [128, NCH, T], BF16), "mean": AR.alloc("mean", [128, T], F32),
                "rstd": AR.alloc("rstd", [128, T], F32), "tmp": [AR.alloc(f"lntmp{i}", [128, T], F32) for i in range(2)]}

    def ln_ws_alias(ysq_first, ysq_deps, scr_first, scr_deps):
        ysq = AR.view("ysq", ysq_first, [128, NCH, T], BF16, ysq_deps)
        scr = AR.view("lnscr", scr_first, [128, 4 * T], F32, scr_deps)
        parts = []
        for i in range(4):
            t_ = Tile(scr.ap[:, i * T:(i + 1) * T], f"lnscr{i}")
            t_.r = dict(scr.r)
            AR.live.append(t_)
            parts.append(t_)
        return {"ysq": ysq, "mean": parts[0], "rstd": parts[1], "tmp": [parts[2], parts[3]]}

    def conv_layer(li, j):
        AR.reset()
        ygT = AR.alloc("ygT", [128, 8, T], BF16)
        cu = AR.alloc("cu", [128, T + 2], F32)
        usb = AR.alloc("usb", [128, T], F32)
        szb = AR.alloc("szb", [128, T], F32)
        bgz = AR.alloc("bgz", [128, T], F32)
        yc = AR.alloc("yc", [128, T], F32)
        cw = AR.alloc("cw", [128, 8, 3], F32)
        cwf = AR.alloc("cwf", [128, 8, 3], F32)
        cfl = AR.alloc("cfl", [128, 1], F32)
        ws = ln_ws_alloc()
        S.dma("sp", cw.ap, convw_d, [], [cw])
        S.dma("sp", cfl.ap, cflag_d, [], [cfl])
        ts("dve", cwf.ap, cw.ap, cfl.ap[:, 0:1], -1.0, ALU.mult, ALU.mult, [cw, cfl], [cwf])
        S.op("dve", lambda e: e.memset(cu.ap[:, 0:1], 0.0), [], [cu])
        S.op("dve", lambda e: e.memset(cu.ap[:, T + 1:T + 2], 0.0), [], [cu])
        for ech in range(8):
            t, v = load_slab(("winc", 0, ech * 512, 512))
            for th in range(2):
                sl = slice(th * 512, (th + 1) * 512)
                pp = []
                for q in range(4):
                    p = psum()
                    mm_group(p, p.ap, [(v[:, k, q * 128:(q + 1) * 128], hT[k].ap[:, sl]) for k in range(8)], [t] + hT)
                    pp.append(p)
                cp("act", usb.ap[:, sl], pp[2].ap, [pp[2]], [usb])
                act(szb.ap[:, sl], pp[3].ap, AF.Silu, [pp[3]], [szb])
                tt("dve", cu.ap[:, 1 + th * 512:1 + (th + 1) * 512], pp[1].ap, usb.ap[:, sl], ALU.mult, [pp[1], usb], [cu])
                tt("dve", bgz.ap[:, sl], pp[0].ap, szb.ap[:, sl], ALU.mult, [pp[0], szb], [bgz])
            ts("dve", yc.ap, cu.ap[:, 1:T + 1], cw.ap[:, ech, 1:2], None, ALU.mult, None, [cu, cw], [yc])
            stt(yc.ap, cu.ap[:, 0:T], cw.ap[:, ech, 0:1], yc.ap, ALU.mult, ALU.add, [cu, cw, yc], [yc])
            stt(yc.ap, cu.ap[:, 2:T + 2], cw.ap[:, ech, 2:3], yc.ap, ALU.mult, ALU.add, [cu, cw, yc], [yc])
            ycb = yc.ap.rearrange("p (s t) -> p s t", t=256)
            cub = cu.ap[:, 1:T + 1].rearrange("p (s t) -> p s t", t=256)
            stt(ycb[:, 1:4, 0:1], cub[:, 0:3, 255:256], cwf.ap[:, ech, 0:1], ycb[:, 1:4, 0:1], ALU.mult, ALU.add,
                [cu, cwf, yc], [yc])
            stt(ycb[:, 0:3, 255:256], cub[:, 1:4, 0:1], cwf.ap[:, ech, 2:3], ycb[:, 0:3, 255:256], ALU.mult, ALU.add,
                [cu, cwf, yc], [yc])
            tt("dve", ygT.ap[:, ech, :], yc.ap, bgz.ap, ALU.mult, [yc, bgz], [ygT])
        out_proj_residual(("woutc", 0), 8, [ygT], ygT.ap, ws)


    def ret_layer(li, j):
        AR.reset()
        ogT = AR.alloc("ogT", [128, 16, T], BF16)
        qT = AR.alloc("qT", [128, 2, T], BF16)
        qTf = AR.alloc("qTf", [128, 2, T], BF16)
        qTb = AR.alloc("qTb", [128, 2, T], BF16)
        kT = AR.alloc("kT", [128, 2, T], BF16)
        ktf = AR.alloc("ktf", [128, 8, 256], BF16)
        ktb = AR.alloc("ktb", [128, 8, 256], BF16)
        vt = AR.alloc("vt", [128, 8, 512], BF16)
        sgT = AR.alloc("sgT", [128, 4, T], BF16)
        SbE = [AR.alloc(f"SbE{c}", [128, 2, 512], BF16) for c in range(8)]
        Sf = [AR.alloc(f"Sf{d}", [128, 512], F32) for d in range(2)]
        Sb = [AR.alloc(f"Sb{d}", [128, 512], F32) for d in range(2)]
        Sfb = [AR.alloc(f"Sfb{i}", [128, 2, 512], BF16) for i in range(2)]
        QDF = AR.alloc("QDF", [128, T], F32)
        QDB = AR.alloc("QDB", [128, T], F32)
        itab = AR.alloc("itab", [128, 2, T], F32)
        DT = AR.alloc("DT", [128, 128], F32)
        rdt = AR.alloc("rdt", [128, 4, 128], F32)
        dtmp = [AR.alloc(f"dtmp{i}", [128, 128], F32) for i in range(2)]
        lg = AR.alloc("lg", [128, 8], F32)
        rchk = AR.alloc("rchk", [128, 16], F32)
        rkd = AR.alloc("rkd", [128, 2], F32)
        CDf = AR.alloc("CDf", [128, 8], F32)
        CDb = AR.alloc("CDb", [128, 8], F32)
        KD = AR.alloc("KD", [128, 2], F32)
        ATb = [AR.alloc(f"ATb{c}", [128, 128], BF16) for c in range(8)]
        onb = [AR.alloc(f"onb{i}", [128, 512], BF16) for i in range(3)]
        junk = AR.alloc("junk", [128, 512], BF16)
        ss = [AR.alloc(f"ss{i}", [128, 1], F32) for i in range(3)]
        rs = [AR.alloc(f"rs{i}", [128, 1], F32) for i in range(3)]

        S.dma("sp", lg.ap, dec_d.partition_broadcast(128), [], [lg])
        S.dma("sp", itab.ap, rtab_d.partition_broadcast(128), [], [itab])
        S.dma("sp", rchk.ap, rchk_d.partition_broadcast(128), [], [rchk])
        S.dma("sp", rkd.ap, rkd_d, [], [rkd])
        S.dma("sp", rdt.ap, rdt_d.rearrange("a m n -> m a n"), [], [rdt])
        act(lg.ap, lg.ap, AF.Exp, [lg], [lg])
        ts("dve", lg.ap, lg.ap, -1.0, 1.0, ALU.mult, ALU.add, [lg], [lg])
        act(lg.ap, lg.ap, AF.Ln, [lg], [lg])

        for h in range(4):
            lgf = lg.ap[:, h:h + 1]
            lgb = lg.ap[:, 4 + h:5 + h]
            act(QDF.ap, itab.ap[:, 0, :], AF.Exp, [itab, lg], [QDF], scale=lgf)
            act(QDB.ap, itab.ap[:, 1, :], AF.Exp, [itab, lg], [QDB], scale=lgb)
            act(CDf.ap, rchk.ap[:, 0:8], AF.Exp, [rchk, lg], [CDf], scale=lgf)
            act(CDb.ap, rchk.ap[:, 8:16], AF.Exp, [rchk, lg], [CDb], scale=lgb)
            act(KD.ap[:, 0:1], rkd.ap[:, 0:1], AF.Exp, [rkd, lg], [KD], scale=lgf)
            act(KD.ap[:, 1:2], rkd.ap[:, 1:2], AF.Exp, [rkd, lg], [KD], scale=lgb)
            ts("dve", KD.ap, KD.ap, 1.0 / 16.0, None, ALU.mult, None, [KD], [KD])
            act(dtmp[0].ap, rdt.ap[:, 0, :], AF.Exp, [rdt, lg], [dtmp[0]], scale=lgf)
            tt("dve", dtmp[0].ap, dtmp[0].ap, rdt.ap[:, 1, :], ALU.mult, [dtmp[0], rdt], [dtmp[0]])
            act(dtmp[1].ap, rdt.ap[:, 2, :], AF.Exp, [rdt, lg], [dtmp[1]], scale=lgb)
            tt("dve", dtmp[1].ap, dtmp[1].ap, rdt.ap[:, 3, :], ALU.mult, [dtmp[1], rdt], [dtmp[1]])
            tt("dve", DT.ap, dtmp[0].ap, dtmp[1].ap, ALU.add, [dtmp[0], dtmp[1]], [DT])
            ts("dve", DT.ap, DT.ap, 1.0 / 16.0, None, ALU.mult, None, [DT], [DT])

            base = h * 1536
            tA, vA = load_slab(("winb", 0, base, 512))
            for f in range(4):
                for th in range(2):
                    sl = slice(th * 512, (th + 1) * 512)
                    p = psum()
                    mm_group(p, p.ap, [(vA[:, k, f * 128:(f + 1) * 128], hT[k].ap[:, sl]) for k in range(8)], [tA] + hT)
                    if f < 2:
                        cp("act", qT.ap[:, f, sl], p.ap, [p], [qT])
                        tt("dve", qTf.ap[:, f, sl], p.ap, QDF.ap[:, sl], ALU.mult, [p, QDF], [qTf])
                        tt("dve", qTb.ap[:, f, sl], p.ap, QDB.ap[:, sl], ALU.mult, [p, QDB], [qTb])
                    else:
                        cp("act" if th else "dve", kT.ap[:, f - 2, sl], p.ap, [p], [kT])
            for half in range(2):
                pT = psum()
                pTb = pT.ap.bitcast(BF16)

                def fnT(e, pTb=pTb, half=half):
                    ins = None
                    for tbl in range(4):
                        tb = half * 4 + tbl
                        for dh in range(2):
                            ins = e.transpose(pTb[:, (tbl * 2 + dh) * 128:(tbl * 2 + dh + 1) * 128],
                                              kT.ap[:, dh, tb * 128:(tb + 1) * 128], identb.ap)
                    return ins

                S.op("pe", fnT, [kT, identb], [pT])
                src = pTb.rearrange("p (a n) -> p a n", a=4)
                ts("dve", ktf.ap[:, half * 4:(half + 1) * 4, :], src, KD.ap[:, 0:1], None, ALU.mult, None, [pT, KD], [ktf])
                act(ktb.ap[:, half * 4:(half + 1) * 4, :], src, AF.Identity, [pT, KD], [ktb], scale=KD.ap[:, 1:2])
            tB, vB = load_slab(("winb", 0, base + 512, 512))
            for tb in range(8):
                p = psum()
                mm_group(p, p.ap, [(hT[k].ap[:, tb * 128:(tb + 1) * 128], vB[:, k, :]) for k in range(8)], [tB] + hT)
                cp("act" if tb % 2 else "dve", vt.ap[:, tb, :], p.ap, [p], [vt])
            for dh in range(2):
                S.dma("sp", Sf[dh].ap, sf_d[h, dh * 128:(dh + 1) * 128, :], [], [Sf[dh]])
                S.dma("sp", Sb[dh].ap, sb_d[h, dh * 128:(dh + 1) * 128, :], [], [Sb[dh]])
            for c in range(8):
                csl = slice(c * 128, (c + 1) * 128)
                pA = psum()
                mm_group(pA, pA.ap[:, 0:128], [(kT.ap[:, dh, csl], qT.ap[:, dh, csl]) for dh in range(2)], [kT, qT])
                tt("dve", ATb[c].ap, pA.ap[:, 0:128], DT.ap, ALU.mult, [pA, DT], [ATb[c]])
            tC, vC = load_slab(("winb", 0, base + 1024, 512))
            idx = 0
            for f in range(4):
                for th in range(2):
                    sl = slice(th * 512, (th + 1) * 512)
                    p = psum()
                    mm_group(p, p.ap, [(vC[:, k, f * 128:(f + 1) * 128], hT[k].ap[:, sl]) for k in range(8)], [tC] + hT)
                    act(sgT.ap[:, f, sl], p.ap, AF.Silu, [p], [sgT])
                    c = 7 - idx
                    idx += 1
                    for dh in range(2):
                        cp("act", SbE[c].ap[:, dh, :], Sb[dh].ap, [Sb[dh]], [SbE[c]])
                        p = psum()
                        mm_group(p, p.ap, [(ktb.ap[:, c, dh * 128:(dh + 1) * 128], vt.ap[:, c, :])], [ktb, vt])
                        stt(Sb[dh].ap, Sb[dh].ap, CDb.ap[:, c:c + 1], p.ap, ALU.mult, ALU.add, [Sb[dh], CDb, p], [Sb[dh]])
                        if c % 2 == 0:
                            S.dma("sp", sbo_d[c // 2, h, dh * 128:(dh + 1) * 128, :], Sb[dh].ap, [Sb[dh]], [])
            for dh in range(2):
                cp("act", Sfb[0].ap[:, dh, :], Sf[dh].ap, [Sf[dh]], [Sfb[0]])

            def finish_chunk(c):
                csl = slice(c * 128, (c + 1) * 128)
                ob = onb[c % 3]
                pT = psum()
                pTb = pT.ap.bitcast(BF16)

                def fnT(e, pTb=pTb, ob=ob):
                    ins = None
                    for jv in range(4):
                        ins = e.transpose(pTb[:, jv * 128:(jv + 1) * 128], ob.ap[:, jv * 128:(jv + 1) * 128], identb.ap)
                    return ins

                S.op("pe", fnT, [ob, identb], [pT])
                tt("dve", ogT.ap[:, h * 4:(h + 1) * 4, csl], pTb[:, 0:512].rearrange("p (a n) -> p a n", a=4),
                   sgT.ap[:, :, csl], ALU.mult, [pT, sgT], [ogT])

            pO_l = {}

            def rms_tail(c):
                pO = pO_l.pop(c)
                ss_, rs_, ob = ss[c % 3], rs[c % 3], onb[c % 3]
                ts("dve", rs_.ap, ss_.ap, 1.0 / 512.0, 1e-6, ALU.mult, ALU.add, [ss_], [rs_])
                act(rs_.ap, rs_.ap, AF.Ln, [rs_], [rs_])
                act(rs_.ap, rs_.ap, AF.Exp, [rs_], [rs_], scale=-0.5)
                ts("dve", ob.ap, pO.ap, rs_.ap[:, 0:1], None, ALU.mult, None, [pO, rs_], [ob])

            for c in range(8):
                csl = slice(c * 128, (c + 1) * 128)
                cur = Sfb[c % 2]
                nxt = Sfb[(c + 1) % 2]
                pU = []
                for dh in range(2):
                    p = psum()
                    mm_group(p, p.ap, [(ktf.ap[:, c, dh * 128:(dh + 1) * 128], vt.ap[:, c, :])], [ktf, vt])
                    pU.append(p)
                pO = psum()
                pairs = [(ATb[c].ap, vt.ap[:, c, :])]
                pairs += [(qTf.ap[:, dh, csl], cur.ap[:, dh, :]) for dh in range(2)]
                pairs += [(qTb.ap[:, dh, csl], SbE[c].ap[:, dh, :]) for dh in range(2)]
                mm_group(pO, pO.ap, pairs, [ATb[c], vt, qTf, cur, qTb, SbE[c]])
                pO_l[c] = pO
                for dh in range(2):
                    stt(Sf[dh].ap, Sf[dh].ap, CDf.ap[:, c:c + 1], pU[dh].ap, ALU.mult, ALU.add, [Sf[dh], CDf, pU[dh]], [Sf[dh]])
                    cp("act", nxt.ap[:, dh, :], Sf[dh].ap, [Sf[dh]], [nxt])
                    if c % 2 == 1:
                        S.dma("sp", sfo_d[c // 2, h, dh * 128:(dh + 1) * 128, :], Sf[dh].ap, [Sf[dh]], [])
                ss_ = ss[c % 3]
                act(junk.ap, pO.ap, AF.Square, [pO], [junk, ss_], accum_out=ss_.ap[:, 0:1])
                if c >= 1:
                    rms_tail(c - 1)
                if c >= 2:
                    finish_chunk(c - 2)
            rms_tail(7)
            finish_chunk(6)
            finish_chunk(7)
        ws = ln_ws_alias(SbE[0], SbE, qT, [qT, qTf, qTb, kT])
        out_proj_residual(("woutb", 0), 16, [ogT], ogT.ap, ws)

    def attn_layer(li, j):
        lam_init = 0.8 - 0.6 * math.exp(-0.3 * li)
        AR.reset()
        on_all = AR.alloc("on_all", [128, 8, 8, 128], BF16)
        QT = AR.alloc("QT", [128, 8, T], BF16)
        ogT = QT
        KT = AR.alloc("KT", [128, 8, 1280], BF16)
        Va = AR.alloc("Va", [128, 10, 4, 129], BF16)
        ET = [AR.alloc(f"ET{i}", [128, 10, 1024], BF16) for i in range(2)]
        kst = [AR.alloc(f"kst{i}", [128, 512], F32) for i in range(2)]
        vst = kst
        qr = [AR.alloc(f"qr{i}", [128, 512], BF16) for i in range(2)]
        rt = [AR.alloc(f"rt{i}", [128, 8, 32], F32) for i in range(8)]
        ropeC = AR.alloc("ropeC", [128, 8, 32], F32)
        ropeS = AR.alloc("ropeS", [128, 8, 32], F32)
        Osave = AR.alloc("Osave", [128, 8, 129], F32)
        otmp = AR.alloc("otmp", [128, 128], F32)
        ofp = AR.alloc("ofp", [128, 8, 128], F32)
        junk = AR.alloc("junk", [128, 128], F32)
        ss = AR.alloc("ss", [128, 8], F32)
        rstd = AR.alloc("rstd", [128, 8], F32)
        r0 = AR.alloc("r0", [128, 1], F32)
        r1 = AR.alloc("r1", [128, 1], F32)
        lamt = AR.alloc("lamt", [128, 256], F32)
        lprod = AR.alloc("lprod", [128, 2, 64], F32)
        lsum = AR.alloc("lsum", [128, 2], F32)
        nlam = AR.alloc("nlam", [128, 1], F32)
        subS = AR.alloc("subS", [128, 2], F32)
        szb = [AR.alloc(f"szb{i}", [128, 512], BF16) for i in range(2)]

        S.dma("sp", ropeC.ap, ropec_d, [], [ropeC])
        S.dma("sp", ropeS.ap, ropes_d, [], [ropeS])
        S.dma("sp", lamt.ap, lam_d[j].partition_broadcast(128), [], [lamt])
        S.dma("sp", subS.ap, subln_d, [], [subS])
        S.dma("sp", QT.ap[64:69, :, :], maskq_d, [], [QT])
        S.dma("sp", KT.ap[64:69, :, :], maskk_d, [], [KT])
        lt = lamt.ap.rearrange("p (a b) -> p a b", a=4)
        tt("dve", lprod.ap[:, 0, :], lt[:, 0, :], lt[:, 1, :], ALU.mult, [lamt], [lprod])
        tt("dve", lprod.ap[:, 1, :], lt[:, 2, :], lt[:, 3, :], ALU.mult, [lamt], [lprod])
        S.op("dve", lambda e: e.tensor_reduce(out=lsum.ap, in_=lprod.ap, axis=AX.X, op=ALU.add), [lprod], [lsum])
        act(lsum.ap, lsum.ap, AF.Exp, [lsum], [lsum])
        tt("dve", nlam.ap, lsum.ap[:, 1:2], lsum.ap[:, 0:1], ALU.subtract, [lsum], [nlam])
        ts("dve", nlam.ap, nlam.ap, -lam_init, None, ALU.add, None, [nlam], [nlam])
        ts("dve", subS.ap, subS.ap, 1.0 - lam_init, None, ALU.mult, None, [subS], [subS])
        S.op("dve", lambda e: e.memset(Va.ap[:, :, :, 128:129], 1.0), [], [Va])

        w_in = wina_d[j]
        cnt = [0]

        rope_i = [0]

        def rope(p, tb, dst4):
            rt_ = rt[4 * (rope_i[0] % 2):4 * (rope_i[0] % 2) + 4]
            rope_i[0] += 1
            p4 = p.ap.rearrange("p (a b c) -> p a b c", a=8, b=2, c=32)
            x1 = p4[:, :, 0, :]
            x2 = p4[:, :, 1, :]
            Cb = ropeC.ap[:, tb:tb + 1, :].to_broadcast([128, 8, 32])
            Sb_ = ropeS.ap[:, tb:tb + 1, :].to_broadcast([128, 8, 32])
            tt("dve", rt_[0].ap, x1, Cb, ALU.mult, [p, ropeC], [rt_[0]])
            tt("dve", rt_[1].ap, x2, Sb_, ALU.mult, [p, ropeS], [rt_[1]])
            tt("dve", rt_[2].ap, x1, Sb_, ALU.mult, [p, ropeS], [rt_[2]])
            tt("dve", rt_[3].ap, x2, Cb, ALU.mult, [p, ropeC], [rt_[3]])
            return [(dst4[:, :, 0, :], rt_[0], rt_[1], ALU.subtract), (dst4[:, :, 1, :], rt_[2], rt_[3], ALU.add)]

        for g in range(2):
            tQ, vQ = load_slab(("wina", j, g * 512, 512))
            QTv = QT.ap.rearrange("p (h m) t -> p h m t", m=2)
            KTv = KT.ap.rearrange("p (h m) t -> p h m t", m=2)

            def transpose_evac(src_tile, src_ap, dstv, dst_tile, col0, idn, bf):
                pT = psum()
                pTv = pT.ap.bitcast(BF16) if bf else pT.ap

                def fnT(e, pTv=pTv, src_ap=src_ap):
                    ins = None
                    for hl in range(4):
                        ins = e.transpose(pTv[:, hl * 128:(hl + 1) * 128], src_ap[:, hl * 128:(hl + 1) * 128], idn.ap)
                    return ins

                S.op("pe", fnT, [src_tile, idn], [pT])
                cs = slice(col0, col0 + 128)
                cp("act", dstv[0:64, :, 0, cs], pTv[0:64, 0:512].rearrange("p (a n) -> p a n", a=4), [pT], [dst_tile])
                cp("act", dstv[0:64, :, 1, cs], pTv[64:128, 0:512].rearrange("p (a n) -> p a n", a=4), [pT], [dst_tile])

            def proj_rope(tW, vW, tb, dst_tile):
                tsl = slice(tb * 128, (tb + 1) * 128)
                p = psum()
                mm_group(p, p.ap, [(hT[k].ap[:, tsl], vW[:, k, :]) for k in range(8)], [tW] + hT)
                for dst, a_, b_, op in rope(p, tb, dst_tile.ap.rearrange("p (a b c) -> p a b c", a=8, b=2, c=32)):
                    tt("pool", dst, a_.ap, b_.ap, op, [a_, b_], [dst_tile])

            for tb in range(9):
                if tb < 8:
                    proj_rope(tQ, vQ, tb, qr[tb % 2])
                if tb >= 1:
                    transpose_evac(qr[(tb - 1) % 2], qr[(tb - 1) % 2].ap, QTv, QT, (tb - 1) * 128, identb, True)
            tK, vK = load_slab(("wina", j, 1024 + g * 512, 512))
            for tb in range(9):
                if tb < 8:
                    proj_rope(tK, vK, tb, kst[tb % 2])
                if tb >= 1:
                    k_ = kst[(tb - 1) % 2]
                    kb_ = qr[(tb - 1) % 2]
                    tsl = slice((tb - 1) * 128, tb * 128)
                    S.dma("sp", cko_d[j, tsl, g * 512:(g + 1) * 512], k_.ap, [k_], [])
                    cp("act", kb_.ap, k_.ap, [k_], [kb_])
                    transpose_evac(kb_, kb_.ap, KTv, KT, (tb - 1) * 128, identb, True)
            for sb_ in range(2):
                S.dma("pool", qr[sb_].ap, ck_d[j, sb_ * 128:(sb_ + 1) * 128, g * 512:(g + 1) * 512], [], [qr[sb_]])
            for sb_ in range(2):
                transpose_evac(qr[sb_], qr[sb_].ap, KTv, KT, 1024 + sb_ * 128, identb, True)
            tV, vV = load_slab(("wina", j, 2048 + g * 512, 512))
            for tb in range(8):
                tsl = slice(tb * 128, (tb + 1) * 128)
                p = psum()
                mm_group(p, p.ap, [(hT[k].ap[:, tsl], vV[:, k, :]) for k in range(8)], [tV] + hT)
                cp("act", Va.ap[:, tb, :, 0:128], p.ap.rearrange("p (a n) -> p a n", a=4), [p], [Va])
                v_ = vst[tb % 2]
                cp("dve", v_.ap, p.ap, [p], [v_])
                S.dma("sp", cvo_d[j, tsl, g * 512:(g + 1) * 512], v_.ap, [v_], [])
            for sb_ in range(2):
                S.dma("pool", Va.ap[:, 8 + sb_, :, 0:128],
                      cv_d[j, sb_ * 128:(sb_ + 1) * 128, g * 512:(g + 1) * 512].rearrange("p (a n) -> p a n", a=4), [], [Va])
            units = [(hl, m) for hl in range(4) for m in range(2)]
            psum_double_mode(True)

            def stageA(u, E):
                hl, m = u
                hm = hl * 2 + m
                items = []
                for kb in range(10):
                    def it(kb=kb):
                        pS = psum2()

                        def fn(e, pS=pS, kb=kb):
                            ins = None
                            for qh in range(2):
                                ins = e.matmul(pS.ap[:, qh * 512:(qh + 1) * 512], lhsT=KT.ap[0:69, hm, kb * 128:(kb + 1) * 128],
                                               rhs=QT.ap[0:69, hm, qh * 512:(qh + 1) * 512], start=True, stop=True)
                            return ins

                        S.op("pe", fn, [KT, QT], [pS])
                        act(E.ap[:, kb, :], pS.ap, AF.Exp, [pS], [E], scale=0.125)
                    items.append(it)
                return items

            def stageB(u, E):
                hl, m = u
                h = g * 4 + hl
                items = []
                pOs = {}
                for qb in range(8):
                    def it0(qb=qb):
                        pO = psum()
                        pOs[qb] = pO
                        mm_group(pO, pO.ap[:, 0:129],
                                 [(E.ap[:, kb, qb * 128:(qb + 1) * 128], Va.ap[:, kb, hl, :]) for kb in range(5)], [E, Va], last=False)
                    items.append(it0)

                    def it(qb=qb):
                        pO = pOs.pop(qb)
                        mm_group(pO, pO.ap[:, 0:129],
                                 [(E.ap[:, kb, qb * 128:(qb + 1) * 128], Va.ap[:, kb, hl, :]) for kb in range(5, 10)], [E, Va], first=False)
                        if m == 0:
                            cp("dve", Osave.ap[:, qb, :], pO.ap[:, 0:129], [pO], [Osave])
                        else:
                            S.op("dve", lambda e, pO=pO: e.reciprocal(out=r1.ap, in_=pO.ap[:, 128:129]), [pO], [r1])
                            S.op("dve", lambda e, qb=qb: e.reciprocal(out=r0.ap, in_=Osave.ap[:, qb, 128:129]), [Osave], [r0])
                            ts("dve", r1.ap, r1.ap, nlam.ap[:, 0:1], None, ALU.mult, None, [r1, nlam], [r1])
                            ts("dve", otmp.ap, Osave.ap[:, qb, 0:128], r0.ap[:, 0:1], None, ALU.mult, None, [Osave, r0], [otmp])
                            stt(ofp.ap[:, qb, :], pO.ap[:, 0:128], r1.ap[:, 0:1], otmp.ap, ALU.mult, ALU.add, [pO, r1, otmp], [ofp])
                            S.op("dve", lambda e, qb=qb: e.scalar_tensor_tensor(out=junk.ap, in0=ofp.ap[:, qb, :], scalar=1.0, in1=ofp.ap[:, qb, :],
                                                                             op0=ALU.mult, op1=ALU.mult, accum_out=ss.ap[:, qb:qb + 1]),
                                 [ofp], [junk, ss])
                    items.append(it)
                if m == 1:
                    def fin():
                        ts("dve", rstd.ap, ss.ap, 1.0 / 128.0, 1e-6, ALU.mult, ALU.add, [ss], [rstd])
                        act(rstd.ap, rstd.ap, AF.Ln, [rstd], [rstd])
                        act(rstd.ap, rstd.ap, AF.Exp, [rstd], [rstd], scale=-0.5)
                        tt("dve", on_all.ap[:, :, h, :], ofp.ap, rstd.ap.unsqueeze(2).to_broadcast([128, 8, 128]), ALU.mult,
                           [ofp, rstd], [on_all])
                    items.append(fin)
                return items

            Ebuf = {}
            for i in range(len(units) + 1):
                A = []
                B = []
                if i < len(units):
                    Ebuf[i] = ET[cnt[0] % 2]
                    cnt[0] += 1
                    A = stageA(units[i], Ebuf[i])
                if i >= 1:
                    B = stageB(units[i - 1], Ebuf[i - 1])
                pattern = [1] * 10
                bsteps = [2, 1, 2, 1, 2, 2, 1, 2, 1, 2]
                ai = 0
                bi = 0
                for si, npat in enumerate(pattern):
                    for _ in range(npat):
                        if ai < len(A):
                            A[ai]()
                            ai += 1
                    for _ in range(bsteps[si]):
                        if bi < len(B):
                            B[bi]()
                            bi += 1
                while ai < len(A):
                    A[ai]()
                    ai += 1
                while bi < len(B):
                    B[bi]()
                    bi += 1
            psum_double_mode(False)
        for g in range(2):
            tZ, vZ = load_slab(("wina", j, 3072 + g * 512, 512))
            for tb in range(8):
                tsl = slice(tb * 128, (tb + 1) * 128)
                p = psum()
                mm_group(p, p.ap, [(hT[k].ap[:, tsl], vZ[:, k, :]) for k in range(8)], [tZ] + hT)
                z_ = szb[tb % 2]
                act(z_.ap, p.ap, AF.Silu, [p], [z_])
                dst = on_all.ap[:, tb, g * 4:(g + 1) * 4, :]
                tt("dve", dst, dst, z_.ap.rearrange("p (a n) -> p a n", a=4), ALU.mult, [on_all, z_], [on_all])
        for tb in range(8):
            tsl = slice(tb * 128, (tb + 1) * 128)
            pT = psum()
            pTb = pT.ap.bitcast(BF16)

            def fnT(e, pTb=pTb, tb=tb):
                ins = None
                for h in range(8):
                    ins = e.transpose(pTb[:, h * 128:(h + 1) * 128], on_all.ap[:, tb, h, :], identb.ap)
                return ins

            S.op("pe", fnT, [on_all, identb], [pT])
            ts("dve" if tb % 2 else "dve", ogT.ap[:, :, tsl], pTb.rearrange("p (a n) -> p a n", a=8), subS.ap[:, j:j + 1], None,
               ALU.mult, None, [pT, subS], [ogT])
        ws = ln_ws_alias(ET[0], [ET[0]], ET[1], [ET[1]])
        out_proj_residual(("wouta", j), 8, [ogT], ogT.ap, ws)

    AR.reset()
    xin = [AR.alloc(f"xin{i}", [128, D], F32) for i in range(4)]

    def x_block(tb):
        xi = xin[tb % 4]
        S.dma("sp", xi.ap, x_d[tb * 128:(tb + 1) * 128, :], [], [xi])
        for half in range(2):
            p = psum()

            def fn(e, p=p, xi=xi, half=half):
                ins = None
                for q in range(4):
                    c = half * 4 + q
                    ins = e.transpose(p.ap[:, q * 128:(q + 1) * 128], xi.ap[:, c * 128:(c + 1) * 128], ident.ap)
                return ins

            S.op("pe", fn, [xi, ident], [p])
            dst = xT_t[:, half * 4:half * 4 + 4, tb * 128:(tb + 1) * 128]
            src = p.ap.rearrange("p (q n) -> p q n", q=4)
            cp("dve" if half == 0 else "act", dst, src, [p], xT[half * 4:half * 4 + 4])

    x_block(0)
    x_block(1)
    for s6 in range(4):
        mod_slab(0, s6)
        if s6 < 3:
            x_block(2 + 2 * s6)
            x_block(3 + 2 * s6)

    for li, (kind, j) in enumerate(layers):
        modulation(li)
        if stop_after == "mod":
            break
        if kind == 2:
            conv_layer(li, j)
        elif kind == 1:
            ret_layer(li, j)
        else:
            attn_layer(li, j)
        if stop_after == "mix":
            break
        layer_norm(li)

    AR.reset()
    yout = [AR.alloc(f"yout{i}", [128, D], F32) for i in range(2)]
    if debug_xT:
        for c in range(NCH):
            S.dma("sp", dbg_d[:, c, :], xT[c].ap, [xT[c]], [])
    for tb in range(8):
        yo = yout[tb % 2]
        for half in range(2):
            p = psum()

            def fn(e, p=p, tb=tb, half=half):
                ins = None
                for q in range(4):
                    c = half * 4 + q
                    ins = e.transpose(p.ap[:, q * 128:(q + 1) * 128], xT_t[:, c, tb * 128:(tb + 1) * 128], ident.ap)
                return ins

            S.op("pe", fn, xT[half * 4:half * 4 + 4] + [ident], [p])
            cp("dve" if half == 0 else "act", yo.ap[:, half * 512:(half + 1) * 512], p.ap, [p], [yo])
        S.dma("sp", y_d[tb * 128:(tb + 1) * 128, :], yo.ap, [yo], [])
    S.finish("sp")

    with nc.Block() as block:
        @block.tensor
        def _(e):
            S.replay("pe", e)

        @block.scalar
        def _(e):
            S.replay("act", e)

        @block.vector
        def _(e):
            S.replay("dve", e)

        @block.gpsimd
        def _(e):
            S.replay("pool", e)

        @block.sync
        def _(e):
            S.replay("sp", e)

    return nc


def _col(v, n):
    return np.ascontiguousarray(np.asarray(v, np.float32).reshape(n, 128).T)


def make_in_maps(inp, layers=LAYERS):
    f32 = np.float32
    g = {k: np.asarray(v) for k, v in inp.items()}
    shared = {}
    shared["bmod"] = np.stack([_col(g["b_mod"][i], 24) for i in range(DEPTH)]).astype(f32)
    shared["lng"] = np.stack([_col(g["ln_g"][i], 8) for i in range(DEPTH)]).astype(f32)
    shared["lnb"] = np.stack([_col(g["ln_b"][i], 8) for i in range(DEPTH)]).astype(f32)
    kinds = set(k for k, _ in layers)
    shared["wmod"] = np.ascontiguousarray(g["w_mod"][:max(1, len(layers))], f32)
    if 0 in kinds:
        shared["wina"] = np.ascontiguousarray(g["w_in_a"], f32)
        shared["wouta"] = np.ascontiguousarray(g["w_out_a"], f32)
    wb = g["w_in_b"][0]
    cols = []
    for h in range(4):
        cols.append(wb[:, h * 256:(h + 1) * 256])
        cols.append(wb[:, 1024 + h * 256:1024 + (h + 1) * 256])
        cols.append(wb[:, 2048 + h * 512:2048 + (h + 1) * 512])
        cols.append(wb[:, 4096 + h * 512:4096 + (h + 1) * 512])
    if 1 in kinds:
        shared["winb"] = np.ascontiguousarray(np.concatenate(cols, 1), f32)
        shared["woutb"] = np.ascontiguousarray(g["w_out_b"][0], f32)
    wc = g["w_in_c"][0]
    cols = []
    for e in range(8):
        for q in range(4):
            cols.append(wc[:, q * 1024 + e * 128:q * 1024 + (e + 1) * 128])
    if 2 in kinds:
        shared["winc"] = np.ascontiguousarray(np.concatenate(cols, 1), f32)
        shared["woutc"] = np.ascontiguousarray(g["w_out_c"][0], f32)
    cw = g["conv_c"][0]
    shared["convw"] = np.ascontiguousarray(cw.reshape(3, 8, 128).transpose(2, 1, 0), f32)
    shared["lam"] = np.ascontiguousarray(g["lam_a"].reshape(2, 256), f32)
    shared["subln"] = np.ascontiguousarray(g["subln_a"].T, f32)
    shared["dec"] = np.concatenate([g["decay_fwd"][0], g["decay_bwd"][0]]).astype(f32)
    shared["ident"] = np.eye(128, dtype=f32)
    m = np.arange(128, dtype=f32)[:, None]
    n = np.arange(128, dtype=f32)[None, :]
    shared["rdt"] = np.stack([np.maximum(n - m, 0), (n >= m).astype(f32), np.maximum(m - n, 0), (n <= m).astype(f32)]).astype(f32)
    shared["rkd"] = np.stack([127.0 - np.arange(128), np.arange(128)], 1).astype(f32)

    tok = np.arange(T)
    r = (tok // 64).astype(np.float64)
    col = (tok % 64).astype(np.float64)
    inv = 10000.0 ** (-np.arange(16, dtype=np.float64) / 16)
    ang = np.concatenate([r[:, None] * inv, col[:, None] * inv], -1)
    cosS = np.cos(ang).astype(f32).reshape(8, 128, 32).transpose(1, 0, 2)
    sinS = np.sin(ang).astype(f32).reshape(8, 128, 32).transpose(1, 0, 2)

    in_maps = []
    for r_ in range(8):
        d = dict(shared)
        if r_ < 4:
            d["x"] = np.ascontiguousarray(g["x_prompt"][4 * r_:4 * r_ + 4].reshape(T, D), f32)
            d["cvec"] = _col(g["c_ctx"], 8)
            d["cachek"] = np.zeros((2, 256, 1024), f32)
            d["cachev"] = np.zeros((2, 256, 1024), f32)
            d["statef"] = np.zeros((4, 256, 512), f32)
            d["stateb"] = np.zeros((4, 256, 512), f32)
            d["ropec"] = np.ones((128, 8, 32), f32)
            d["ropes"] = np.zeros((128, 8, 32), f32)
            seq_q = np.arange(T) // 256
            seq_k = np.concatenate([np.arange(T) // 256, np.full(256, 4)])
            mq = np.zeros((5, T), f32)
            for grp in range(5):
                mq[grp] = np.where(seq_q == grp, 0.0, NEG)
            mk = np.zeros((5, 1280), f32)
            for grp in range(5):
                mk[grp] = (seq_k == grp).astype(f32)
            d["maskq"] = np.ascontiguousarray(np.broadcast_to(mq[:, None, :], (5, 8, T))).astype(ml_dtypes.bfloat16)
            d["maskk"] = np.ascontiguousarray(np.broadcast_to(mk[:, None, :], (5, 8, 1280))).astype(ml_dtypes.bfloat16)
            d["cflag"] = np.ones((128, 1), f32)
            fstart = np.array([c % 2 == 0 for c in range(8)])
            bstart = np.array([c % 2 == 1 for c in range(8)])
        else:
            b = r_ - 4
            d["x"] = np.ascontiguousarray(g["x_sample"][b], f32)
            d["cvec"] = _col(g["c"][b], 8)
            d["cachek"] = np.ascontiguousarray(g["cache_k"][b].reshape(2, 256, 1024), f32)
            d["cachev"] = np.ascontiguousarray(g["cache_v"][b].reshape(2, 256, 1024), f32)
            d["statef"] = np.ascontiguousarray(g["state_fwd"][b, 0], f32)
            d["stateb"] = np.ascontiguousarray(g["state_bwd"][b, 0], f32)
            d["ropec"] = np.ascontiguousarray(cosS)
            d["ropes"] = np.ascontiguousarray(sinS)
            d["maskq"] = np.zeros((5, 8, T), ml_dtypes.bfloat16)
            d["maskk"] = np.zeros((5, 8, 1280), ml_dtypes.bfloat16)
            d["cflag"] = np.zeros((128, 1), f32)
            fstart = np.zeros(8, bool)
            bstart = np.zeros(8, bool)
        i_in = (np.arange(T) % 128).astype(f32)
        cidx = np.arange(T) // 128
        qf = np.where(fstart[cidx], BIG, i_in + 1.0)
        qb = np.where(bstart[cidx], BIG, 128.0 - i_in)
        d["rtab"] = np.stack([qf, qb]).astype(f32)
        d["rchk"] = np.concatenate([np.where(fstart, BIG, 128.0), np.where(bstart, BIG, 128.0)]).astype(f32)
        in_maps.append(d)
    return in_maps


_NC_CACHE = {}


def kernel(**inputs):
    if "full" not in _NC_CACHE:
        _NC_CACHE["full"] = build()
    nc = _NC_CACHE["full"]
    in_maps = make_in_maps(inputs)
    res = run_bass_kernel_spmd(nc, in_maps, core_ids=list(range(8)))
    R = res.results
    f32 = np.float32
    y_prompt = np.stack([R[r]["y"].reshape(4, 256, D) for r in range(4)]).reshape(16, 256, D).astype(f32)
    y_sample = np.stack([R[4 + b]["y"] for b in range(4)]).astype(f32)
    nk = np.stack([R[r]["cko"].reshape(2, 4, 256, 8, 128).transpose(1, 0, 2, 3, 4) for r in range(4)]).reshape(16, 2, 256, 8, 128)
    nv = np.stack([R[r]["cvo"].reshape(2, 4, 256, 8, 128).transpose(1, 0, 2, 3, 4) for r in range(4)]).reshape(16, 2, 256, 8, 128)
    nsf = np.stack([R[r]["sfo"] for r in range(4)]).reshape(16, 1, 4, 256, 512)
    nsb = np.stack([R[r]["sbo"] for r in range(4)]).reshape(16, 1, 4, 256, 512)
    return (y_prompt, y_sample, nk.astype(f32), nv.astype(f32), nsf.astype(f32), nsb.astype(f32))
```

```python
import math
import numpy as np
import ml_dtypes
import concourse.bass as bass
import concourse.mybir as mybir
from concourse.bass_utils import run_bass_kernel_spmd

F32 = mybir.dt.float32
BF16 = mybir.dt.bfloat16
AF = mybir.ActivationFunctionType
ALU = mybir.AluOpType
AX = mybir.AxisListType

D = 1024
T = 1024
NCH = 8
DEPTH = 4
ALPHA = (2.0 * DEPTH) ** 0.25
LN_EPS = 1e-5
NEG = -30000.0
BIG = 1.0e6
SAME_ENGINE_SYNC = True
NDMA = 16


class Tile:
    __slots__ = ("ap", "w", "r", "name", "psum")

    def __init__(self, ap, name="", psum=False):
        self.ap = ap
        self.w = None
        self.r = {}
        self.name = name
        self.psum = psum

    def __getitem__(self, k):
        return self.ap[k]


class _Eng:
    def __init__(self, name, sem):
        self.name = name
        self.sem = sem
        self.count = 0
        self.ops = []
        self.seen = {}
        self.dma_sems = []
        self.dma_cnt = []
        self.dma_i = 0


class Sched:
    def __init__(self, nc):
        self.nc = nc
        self.E = {}
        self.semobjs = {}

    def add_engine(self, name, sem, dma_sems=()):
        e = _Eng(name, sem)
        e.dma_sems = list(dma_sems)
        e.dma_cnt = [0] * len(e.dma_sems)
        self.E[name] = e
        self.semobjs[id(sem)] = sem
        for s in dma_sems:
            self.semobjs[id(s)] = s

    def _collect(self, E, reads, writes, waits):
        def need(tok):
            if tok is None:
                return
            sid, val = tok
            if sid == id(E.sem) and (E.name == "pe" or not SAME_ENGINE_SYNC):
                return
            if E.seen.get(sid, 0) >= val:
                return
            if waits.get(sid, 0) < val:
                waits[sid] = val

        for t in reads:
            need(t.w)
            if t.psum:
                for sid, val in t.r.items():
                    if sid != id(E.sem):
                        need((sid, val))
        for t in writes:
            need(t.w)
            for sid, val in t.r.items():
                need((sid, val))

    def op(self, eng, fn, reads=(), writes=()):
        E = self.E[eng]
        waits = {}
        self._collect(E, reads, writes, waits)
        for sid, val in waits.items():
            E.seen[sid] = val
        E.count += 1
        tok = (id(E.sem), E.count)
        E.ops.append((list(waits.items()), fn, E.sem, 1))
        for t in reads:
            if t.r.get(tok[0], 0) < tok[1]:
                t.r[tok[0]] = tok[1]
        for t in writes:
            t.w = tok
            t.r = {}
        return tok

    def dma(self, eng, out, in_, reads=(), writes=()):
        E = self.E[eng]
        slot = E.dma_i % len(E.dma_sems)
        E.dma_i += 1
        sem = E.dma_sems[slot]
        waits = {}
        prev = E.dma_cnt[slot]
        if prev > 0 and E.seen.get(id(sem), 0) < prev:
            waits[id(sem)] = prev
        E.dma_cnt[slot] += 16
        val = E.dma_cnt[slot]
        self._collect(E, reads, writes, waits)
        for sid, v in waits.items():
            E.seen[sid] = v
        tok = (id(sem), val)

        def fn(e, out=out, in_=in_):
            return e.dma_start(out=out, in_=in_)

        E.ops.append((list(waits.items()), fn, sem, 16))
        for t in reads:
            if t.r.get(tok[0], 0) < tok[1]:
                t.r[tok[0]] = tok[1]
        for t in writes:
            t.w = tok
            t.r = {}
        return tok

    def finish(self, eng="sp"):
        E = self.E[eng]
        waits = {}
        for o in self.E.values():
            if o.count > 0 and o is not E:
                waits[id(o.sem)] = o.count
            for s, c in zip(o.dma_sems, o.dma_cnt):
                if c > 0:
                    waits[id(s)] = c
        E.ops.append((list(waits.items()), None, None, 0))

    def replay(self, name, e):
        E = self.E[name]
        for waits, fn, sem, inc in E.ops:
            for sid, val in waits:
                e.wait_ge(self.semobjs[sid], val)
            if fn is None:
                continue
            ins = fn(e)
            ins.then_inc(sem, inc)


LAYERS = [(0, 0), (1, 0), (2, 0), (0, 1)]


def build(layers=LAYERS, debug_xT=False, stop_after=None):
    nc = bass.Bass("TRN2", target_bir_lowering=False)
    dt_in = lambda name, shape, dt=F32: nc.dram_tensor(name, list(shape), dt, kind="ExternalInput").ap()
    dt_out = lambda name, shape, dt=F32: nc.dram_tensor(name, list(shape), dt, kind="ExternalOutput").ap()

    x_d = dt_in("x", [T, D])
    cvec_d = dt_in("cvec", [128, 8])
    bmod_d = dt_in("bmod", [DEPTH, 128, 24])
    lng_d = dt_in("lng", [DEPTH, 128, 8])
    lnb_d = dt_in("lnb", [DEPTH, 128, 8])
    wmod_d = dt_in("wmod", [max(1, len(layers)), D, 3 * D])
    kinds = set(k for k, _ in layers)
    if 0 in kinds:
        wina_d = dt_in("wina", [2, D, 4096])
        wouta_d = dt_in("wouta", [2, D, D])
    if 1 in kinds:
        winb_d = dt_in("winb", [D, 6144])
        woutb_d = dt_in("woutb", [2048, D])
    if 2 in kinds:
        winc_d = dt_in("winc", [D, 4096])
        woutc_d = dt_in("woutc", [D, D])
    convw_d = dt_in("convw", [128, 8, 3])
    lam_d = dt_in("lam", [2, 256])
    subln_d = dt_in("subln", [128, 2])
    dec_d = dt_in("dec", [8])
    ck_d = dt_in("cachek", [2, 256, 1024])
    cv_d = dt_in("cachev", [2, 256, 1024])
    sf_d = dt_in("statef", [4, 256, 512])
    sb_d = dt_in("stateb", [4, 256, 512])
    ident_d = dt_in("ident", [128, 128])
    ropec_d = dt_in("ropec", [128, 8, 32])
    ropes_d = dt_in("ropes", [128, 8, 32])
    maskq_d = dt_in("maskq", [5, 8, T], BF16)
    maskk_d = dt_in("maskk", [5, 8, 1280], BF16)
    rtab_d = dt_in("rtab", [2, T])
    rchk_d = dt_in("rchk", [16])
    rdt_d = dt_in("rdt", [4, 128, 128])
    rkd_d = dt_in("rkd", [128, 2])
    cflag_d = dt_in("cflag", [128, 1])

    y_d = dt_out("y", [T, D])
    cko_d = dt_out("cko", [2, T, 1024])
    cvo_d = dt_out("cvo", [2, T, 1024])
    sfo_d = dt_out("sfo", [4, 4, 256, 512])
    sbo_d = dt_out("sbo", [4, 4, 256, 512])
    dbg_d = dt_out("dbg", [128, 8, T]) if debug_xT else None

    S = Sched(nc)
    nsem = 5 + 2 * NDMA
    sems = [nc.alloc_semaphore(name=f"s{i}") for i in range(nsem)]
    S.add_engine("pe", sems[0])
    S.add_engine("act", sems[1])
    S.add_engine("dve", sems[2])
    S.add_engine("pool", sems[3], sems[5:5 + NDMA])
    S.add_engine("sp", sems[4], sems[5 + NDMA:5 + 2 * NDMA])

    def sb(name, shape, dt=F32):
        return nc.alloc_sbuf_tensor("sb_" + name, list(shape), dt)

    def tile(name, shape, dt=F32):
        return Tile(sb(name, shape, dt).ap(), name)

    xT = [Tile(None) for _ in range(NCH)]
    xT_t = sb("xT", [128, NCH, T], F32).ap()
    for c in range(NCH):
        xT[c].ap = xT_t[:, c, :]
    hT_t = sb("hT", [128, NCH, T], BF16).ap()
    hT = [Tile(hT_t[:, c, :], f"hT{c}") for c in range(NCH)]
    NSLAB = 3
    slab_t = sb("slab", [128, NSLAB, 8 * 512], BF16).ap()
    slabs = [Tile(slab_t[:, i, :], f"slab{i}") for i in range(NSLAB)]
    slab_i = [0]

    ident = tile("ident", [128, 128], F32)
    identb = tile("identb", [128, 128], BF16)
    onesM = tile("onesM", [128, 128], BF16)
    cvec = tile("cvec", [128, 8], F32)
    scb = tile("scb", [128, 8], BF16)
    bmod = tile("bmod", [128, DEPTH, 24], F32)
    lng = tile("lng", [128, DEPTH, 8], F32)
    lnb = tile("lnb", [128, DEPTH, 8], F32)
    sc1 = tile("sc1", [128, 8], F32)
    gA = tile("gA", [128, 8], F32)
    epsln = tile("epsln", [128, 1], F32)

    pd_t = [nc.alloc_psum_tensor(f"pd{i}", [128, 1024], F32).ap() for i in range(4)]
    PS = [Tile(pd_t[i // 2][:, (i % 2) * 512:(i % 2 + 1) * 512], f"ps{i}", psum=True) for i in range(8)]
    PD = [Tile(pd_t[i], f"pd{i}", psum=True) for i in range(2)]
    ps_i = [0]
    pd_i = [0]
    ps_mode = {"double": False}

    def _xfer(srcs, dsts):
        acc = {}
        for t in srcs:
            if t.w is not None and acc.get(t.w[0], 0) < t.w[1]:
                acc[t.w[0]] = t.w[1]
            for s_, v_ in t.r.items():
                if acc.get(s_, 0) < v_:
                    acc[s_] = v_
        for t in dsts:
            t.w = None
            t.r = dict(acc)

    def psum_double_mode(on):
        if on:
            _xfer(PS[0:2], [PD[0]])
            _xfer(PS[2:4], [PD[1]])
        else:
            _xfer([PD[0]], PS[0:2])
            _xfer([PD[1]], PS[2:4])
        ps_mode["double"] = on

    def psum():
        if ps_mode["double"]:
            t = PS[4 + ps_i[0] % 4]
        else:
            t = PS[ps_i[0] % 8]
        ps_i[0] += 1
        return t

    def psum2():
        t = PD[pd_i[0] % 2]
        pd_i[0] += 1
        return t

    ARENA_BYTES = 132 * 1024
    arena_t = sb("arena", [128, ARENA_BYTES // 4], F32).ap()

    class Arena:
        def __init__(self):
            self.off = 0
            self.live = []
            self.pending = {}
            self.offs = {}

        def reset(self):
            for t in self.live:
                if t.w is not None:
                    s_, v_ = t.w
                    if self.pending.get(s_, 0) < v_:
                        self.pending[s_] = v_
                for s_, v_ in t.r.items():
                    if self.pending.get(s_, 0) < v_:
                        self.pending[s_] = v_
            self.live = []
            self.off = 0

        def alloc(self, name, shape, dt=F32):
            esz = 4 if dt == F32 else 2
            n = int(np.prod(shape[1:]))
            nbytes = (n * esz + 3) // 4 * 4
            assert self.off + nbytes <= ARENA_BYTES, (name, self.off, nbytes)
            words = nbytes // 4
            base = arena_t[0:shape[0], self.off // 4:self.off // 4 + words]
            if dt != F32:
                base = base.bitcast(dt)
            if len(shape) > 2:
                letters = "abcdefg"[:len(shape) - 1]
                pat = "p (" + " ".join(letters) + ") -> p " + " ".join(letters)
                kw = {l: s for l, s in zip(letters, shape[1:])}
                base = base.rearrange(pat, **kw)
            t = Tile(base, name)
            t.r = dict(self.pending)
            self.live.append(t)
            self.offs[id(t)] = (self.off, nbytes)
            self.off += nbytes
            return t

        def view(self, name, first_tile, shape, dt, dep_tiles):
            off = self.offs[id(first_tile)][0]
            esz = 4 if dt == F32 else 2
            n = int(np.prod(shape[1:]))
            words = (n * esz + 3) // 4
            end = max(self.offs[id(d)][0] + self.offs[id(d)][1] for d in dep_tiles)
            assert off + words * 4 <= end, (name, off, words * 4, end)
            base = arena_t[0:shape[0], off // 4:off // 4 + words]
            if dt != F32:
                base = base.bitcast(dt)
            if len(shape) > 2:
                letters = "abcdefg"[:len(shape) - 1]
                pat = "p (" + " ".join(letters) + ") -> p " + " ".join(letters)
                base = base.rearrange(pat, **{l: s_ for l, s_ in zip(letters, shape[1:])})
            t = Tile(base, name)
            acc = dict(self.pending)
            for d in dep_tiles:
                if d.w is not None and acc.get(d.w[0], 0) < d.w[1]:
                    acc[d.w[0]] = d.w[1]
                for s_, v_ in d.r.items():
                    if acc.get(s_, 0) < v_:
                        acc[s_] = v_
            t.r = acc
            self.live.append(t)
            self.offs[id(t)] = (off, words * 4)
            return t

        def sub(self, parent, ap, name=""):
            t = Tile(ap, name)
            t.r = dict(self.pending)
            self.live.append(t)
            return t

    AR = Arena()

    def act(out, in_, func, reads, writes, bias=None, scale=None, accum_out=None):
        kw = {}
        if bias is not None:
            kw["bias"] = bias
        if scale is not None:
            kw["scale"] = scale
        if accum_out is not None:
            kw["accum_out"] = accum_out
        return S.op("act", lambda e: e.activation(out=out, in_=in_, func=func, **kw), reads, writes)

    def tt(eng, out, in0, in1, op, reads, writes):
        return S.op(eng, lambda e: e.tensor_tensor(out=out, in0=in0, in1=in1, op=op), reads, writes)

    def ts(eng, out, in0, s1, s2, op0, op1, reads, writes):
        if s2 is None:
            return S.op(eng, lambda e: e.tensor_scalar(out=out, in0=in0, scalar1=s1, scalar2=None, op0=op0), reads, writes)
        return S.op(eng, lambda e: e.tensor_scalar(out=out, in0=in0, scalar1=s1, scalar2=s2, op0=op0, op1=op1), reads, writes)

    def stt(out, in0, scalar, in1, op0, op1, reads, writes):
        return S.op("dve", lambda e: e.scalar_tensor_tensor(out=out, in0=in0, scalar=scalar, in1=in1, op0=op0, op1=op1), reads, writes)

    def cp(eng, out, in_, reads, writes):
        if eng == "act":
            return S.op("act", lambda e: e.copy(out=out, in_=in_), reads, writes)
        return S.op(eng, lambda e: e.tensor_copy(out=out, in_=in_), reads, writes)

    def mm_group(out_tile, out_ap, pairs, extra_reads, first=True, last=True):
        n = len(pairs)

        def fn(e):
            ins = None
            for i, (l, r) in enumerate(pairs):
                ins = e.matmul(out_ap, lhsT=l, rhs=r, start=(first and i == 0), stop=(last and i == n - 1))
            return ins

        return S.op("pe", fn, extra_reads, [out_tile])

    def chunk_major_first(tW, groups):
        ps = [psum() for _ in groups]
        for k in range(8):
            for gi, pairs in enumerate(groups):
                l, r = pairs[k]

                def fn(e, l=l, r=r, p=ps[gi], k=k):
                    return e.matmul(p.ap, lhsT=l, rhs=r, start=(k == 0), stop=(k == 7))

                S.op("pe", fn, [tW, hT[k]], [ps[gi]])
        return ps

    def layer_slabs(kind, j):
        L = []
        if kind == 0:
            for g in range(2):
                for sec in range(3):
                    L.append(("wina", j, sec * 1024 + g * 512, 512))
            for g in range(2):
                L.append(("wina", j, 3072 + g * 512, 512))
            for s_ in range(2):
                L.append(("wouta", j, s_ * 512, 512))
        elif kind == 1:
            for h in range(4):
                for q in range(3):
                    L.append(("winb", 0, h * 1536 + q * 512, 512))
            for s_ in range(4):
                L.append(("woutb", 0, s_ * 256, 256))
        else:
            for e_ in range(8):
                L.append(("winc", 0, e_ * 512, 512))
            for s_ in range(2):
                L.append(("woutc", 0, s_ * 512, 512))
        return L

    def slab_plan():
        plan = []
        for li, (kind, j) in enumerate(layers):
            L = layer_slabs(kind, j)
            if li == 0:
                plan += [("wmod", 0, s6 * 512, 512) for s6 in range(4)]
                plan += [L[0], ("wmod", 0, 4 * 512, 512), ("wmod", 0, 5 * 512, 512)] + L[1:]
            else:
                plan += [("wmod", li, s6 * 512, 512) for s6 in range(6)]
                plan += L
        return plan

    PLAN = slab_plan()
    plan_state = {"next": 0, "issued": 0, "views": {}}

    def slab_src(key):
        name, idx, c0, ncol = key
        base = {"wmod": lambda: wmod_d[idx], "wina": lambda: wina_d[idx], "wouta": lambda: wouta_d[idx],
                "winb": lambda: winb_d, "woutb": lambda: woutb_d, "winc": lambda: winc_d, "woutc": lambda: woutc_d}[name]()
        return base[:, c0:c0 + ncol].rearrange("(k p) n -> p k n", p=128)

    def issue_slab(i):
        src_ap = slab_src(PLAN[i])
        t = slabs[i % NSLAB]
        k, n = src_ap.shape[1], src_ap.shape[2]
        dst = t.ap[:, 0:k * n].rearrange("p (k n) -> p k n", k=k)
        S.dma("pool", dst, src_ap, [], [t])
        plan_state["views"][i] = (t, dst)

    def load_slab(key):
        i = plan_state["next"]
        deferred = None
        while key[0] != "wmod" and PLAN[i][0] == "wmod":
            deferred = PLAN[i][1]
            mod_slab(PLAN[i][1], PLAN[i][2] // 512)
            i = plan_state["next"]
        if deferred is not None:
            finish_gate(deferred)
        assert PLAN[i] == key, (i, PLAN[i], key)
        plan_state["next"] += 1
        oldest = plan_state.get("hold", i)
        while plan_state["issued"] < min(len(PLAN), oldest + NSLAB):
            issue_slab(plan_state["issued"])
            plan_state["issued"] += 1
        ret = plan_state["views"].pop(i)
        if False:
            plan_state["hold"] = i
            mod_slab(PLAN[i + 1][1], PLAN[i + 1][2] // 512)
            del plan_state["hold"]
        return ret

    S.dma("sp", ident.ap, ident_d, [], [ident])
    S.dma("sp", cvec.ap, cvec_d, [], [cvec])
    S.dma("sp", bmod.ap, bmod_d.rearrange("l p j -> p l j"), [], [bmod])
    S.dma("sp", lng.ap, lng_d.rearrange("l p j -> p l j"), [], [lng])
    S.dma("sp", lnb.ap, lnb_d.rearrange("l p j -> p l j"), [], [lnb])
    cp("dve", identb.ap, ident.ap, [ident], [identb])
    S.op("dve", lambda e: e.memset(onesM.ap, 1.0 / 1024.0), [], [onesM])
    S.op("dve", lambda e: e.memset(epsln.ap, LN_EPS / (ALPHA * ALPHA)), [], [epsln])
    act(scb.ap, cvec.ap, AF.Silu, [cvec], [scb])

    modv_l = [tile(f"modv{l}", [128, 24], F32) for l in range(len(layers))]

    def mod_slab(li, s6):
        t, v = load_slab(("wmod", li, s6 * 512, 512))
        pm = psum()

        def fn(e, v=v, pm=pm):
            ins = None
            for jj in range(4):
                for k in range(8):
                    ins = e.matmul(pm.ap[:, jj:jj + 1], lhsT=v[:, k, jj * 128:(jj + 1) * 128], rhs=scb.ap[:, k:k + 1],
                                   start=(k == 0), stop=(k == 7))
            return ins

        S.op("pe", fn, [t, scb], [pm])
        tt("dve", modv_l[li].ap[:, 4 * s6:4 * s6 + 4], pm.ap[:, 0:4], bmod.ap[:, li, 4 * s6:4 * s6 + 4], ALU.add,
           [pm, bmod], [modv_l[li]])

    def finish_gate(li):
        modv = modv_l[li]
        ts("dve", gA.ap, modv.ap[:, 16:24], 1.0 / ALPHA, None, ALU.mult, None, [modv], [gA])

    def modulation(li):
        modv = modv_l[li]
        if li > 0:
            for s6 in range(6):
                mod_slab(li, s6)
            finish_gate(li)
        ts("dve", sc1.ap, modv.ap[:, 8:16], 1.0, None, ALU.add, None, [modv], [sc1])
        for c in range(NCH):
            if c % 2 == 0:
                ts("dve", hT[c].ap, xT[c].ap, sc1.ap[:, c:c + 1], modv.ap[:, c:c + 1], ALU.mult, ALU.add,
                   [xT[c], sc1, modv], [hT[c]])
            else:
                act(hT[c].ap, xT[c].ap, AF.Identity, [xT[c], sc1, modv], [hT[c]],
                    bias=modv.ap[:, c:c + 1], scale=sc1.ap[:, c:c + 1])

    def out_proj_residual(wkey, n_e, ogT_tiles, ogT_ap, ws):
        LNWS["ws"] = ws
        ncols = 512 * 8 // n_e
        for s_ in range(D // ncols):
            t, v = load_slab((wkey[0], wkey[1], s_ * ncols, ncols))
            for dcc in range(ncols // 128):
                dc = s_ * (ncols // 128) + dcc
                for th in range(2):
                    p = psum()
                    pairs = [(v[:, e_, dcc * 128:(dcc + 1) * 128], ogT_ap[:, e_, th * 512:(th + 1) * 512]) for e_ in range(n_e)]
                    mm_group(p, p.ap, pairs, [t] + list(ogT_tiles))
                    xs = xT[dc].ap[:, th * 512:(th + 1) * 512]
                    stt(xs, p.ap, gA.ap[:, dc:dc + 1], xs, ALU.mult, ALU.add, [p, gA, xT[dc]], [xT[dc]])
                cp("act", hT[dc].ap, xT[dc].ap, [xT[dc]], [hT[dc]])
                if dc % 2 == 0:
                    act(ws["ysq"].ap[:, dc, :], xT[dc].ap, AF.Square, [xT[dc]], [ws["ysq"]])
                else:
                    tt("dve", ws["ysq"].ap[:, dc, :], xT[dc].ap, xT[dc].ap, ALU.mult, [xT[dc]], [ws["ysq"]])

    LNWS = {}

    def layer_norm(li):
        ws = LNWS["ws"]
        ysq, mean, rstd, tmp = ws["ysq"], ws["mean"], ws["rstd"], ws["tmp"]
        for th in range(2):
            pm = psum()
            mm_group(pm, pm.ap, [(onesM.ap, hT[c].ap[:, th * 512:(th + 1) * 512]) for c in range(NCH)], [onesM] + hT)
            pq = psum()
            mm_group(pq, pq.ap, [(onesM.ap, ysq.ap[:, c, th * 512:(th + 1) * 512]) for c in range(NCH)], [onesM, ysq])
            sl = slice(th * 512, (th + 1) * 512)
            cp("dve", mean.ap[:, sl], pm.ap, [pm], [mean])
            act(tmp[0].ap[:, sl], pm.ap, AF.Square, [pm], [tmp[0]])
            tt("dve", rstd.ap[:, sl], pq.ap, tmp[0].ap[:, sl], ALU.subtract, [pq, tmp[0]], [rstd])
        act(rstd.ap, rstd.ap, AF.Ln, [rstd, epsln], [rstd], bias=epsln.ap[:, 0:1])
        act(rstd.ap, rstd.ap, AF.Exp, [rstd], [rstd], scale=-0.5)
        for c in range(NCH):
            tm = tmp[c % 2]
            tt("dve", tm.ap, xT[c].ap, mean.ap, ALU.subtract, [xT[c], mean], [tm])
            tt("dve", tm.ap, tm.ap, rstd.ap, ALU.mult, [tm, rstd], [tm])
            act(xT[c].ap, tm.ap, AF.Identity, [tm, lng, lnb], [xT[c]],
                bias=lnb.ap[:, li, c:c + 1], scale=lng.ap[:, li, c:c + 1])

    def ln_ws_alloc():
        return {"ysq": AR.alloc("ysq", [128, NCH, T], BF16), "mean": AR.alloc("mean", [128, T], F32),
                "rstd": AR.alloc("rstd", [128, T], F32), "tmp": [AR.alloc(f"lntmp{i}", [128, T], F32) for i in range(2)]}

    def ln_ws_alias(ysq_first, ysq_deps, scr_first, scr_deps):
        ysq = AR.view("ysq", ysq_first, [128, NCH, T], BF16, ysq_deps)
        scr = AR.view("lnscr", scr_first, [128, 4 * T], F32, scr_deps)
        parts = []
        for i in range(4):
            t_ = Tile(scr.ap[:, i * T:(i + 1) * T], f"lnscr{i}")
            t_.r = dict(scr.r)
            AR.live.append(t_)
            parts.append(t_)
        return {"ysq": ysq, "mean": parts[0], "rstd": parts[1], "tmp": [parts[2], parts[3]]}

    def conv_layer(li, j):
        AR.reset()
        ygT = AR.alloc("ygT", [128, 8, T], BF16)
        cu = AR.alloc("cu", [128, T + 2], F32)
        usb = AR.alloc("usb", [128, T], F32)
        szb = AR.alloc("szb", [128, T], F32)
        bgz = AR.alloc("bgz", [128, T], F32)
        yc = AR.alloc("yc", [128, T], F32)
        cw = AR.alloc("cw", [128, 8, 3], F32)
        cwf = AR.alloc("cwf", [128, 8, 3], F32)
        cfl = AR.alloc("cfl", [128, 1], F32)
        ws = ln_ws_alloc()
        S.dma("sp", cw.ap, convw_d, [], [cw])
        S.dma("sp", cfl.ap, cflag_d, [], [cfl])
        ts("dve", cwf.ap, cw.ap, cfl.ap[:, 0:1], -1.0, ALU.mult, ALU.mult, [cw, cfl], [cwf])
        S.op("dve", lambda e: e.memset(cu.ap[:, 0:1], 0.0), [], [cu])
        S.op("dve", lambda e: e.memset(cu.ap[:, T + 1:T + 2], 0.0), [], [cu])
        for ech in range(8):
            t, v = load_slab(("winc", 0, ech * 512, 512))
            preC = None
            if ech == 0:
                preC = chunk_major_first(t, [[(v[:, k, q * 128:(q + 1) * 128], hT[k].ap[:, th * 512:(th + 1) * 512]) for k in range(8)]
                                             for th in range(2) for q in range(4)])
            for th in range(2):
                sl = slice(th * 512, (th + 1) * 512)
                pp = []
                for q in range(4):
                    if preC:
                        p = preC[th * 4 + q]
                    else:
                        p = psum()
                        mm_group(p, p.ap, [(v[:, k, q * 128:(q + 1) * 128], hT[k].ap[:, sl]) for k in range(8)], [t] + hT)
                    pp.append(p)
                cp("act", usb.ap[:, sl], pp[2].ap, [pp[2]], [usb])
                act(szb.ap[:, sl], pp[3].ap, AF.Silu, [pp[3]], [szb])
                tt("dve", cu.ap[:, 1 + th * 512:1 + (th + 1) * 512], pp[1].ap, usb.ap[:, sl], ALU.mult, [pp[1], usb], [cu])
                tt("dve", bgz.ap[:, sl], pp[0].ap, szb.ap[:, sl], ALU.mult, [pp[0], szb], [bgz])
            ts("dve", yc.ap, cu.ap[:, 1:T + 1], cw.ap[:, ech, 1:2], None, ALU.mult, None, [cu, cw], [yc])
            stt(yc.ap, cu.ap[:, 0:T], cw.ap[:, ech, 0:1], yc.ap, ALU.mult, ALU.add, [cu, cw, yc], [yc])
            stt(yc.ap, cu.ap[:, 2:T + 2], cw.ap[:, ech, 2:3], yc.ap, ALU.mult, ALU.add, [cu, cw, yc], [yc])
            ycb = yc.ap.rearrange("p (s t) -> p s t", t=256)
            cub = cu.ap[:, 1:T + 1].rearrange("p (s t) -> p s t", t=256)
            stt(ycb[:, 1:4, 0:1], cub[:, 0:3, 255:256], cwf.ap[:, ech, 0:1], ycb[:, 1:4, 0:1], ALU.mult, ALU.add,
                [cu, cwf, yc], [yc])
            stt(ycb[:, 0:3, 255:256], cub[:, 1:4, 0:1], cwf.ap[:, ech, 2:3], ycb[:, 0:3, 255:256], ALU.mult, ALU.add,
                [cu, cwf, yc], [yc])
            tt("dve", ygT.ap[:, ech, :], yc.ap, bgz.ap, ALU.mult, [yc, bgz], [ygT])
        out_proj_residual(("woutc", 0), 8, [ygT], ygT.ap, ws)


    def ret_layer(li, j):
        AR.reset()
        ogT = AR.alloc("ogT", [128, 16, T], BF16)
        qT = AR.alloc("qT", [128, 2, T], BF16)
        qTf = AR.alloc("qTf", [128, 2, T], BF16)
        qTb = AR.alloc("qTb", [128, 2, T], BF16)
        kT = AR.alloc("kT", [128, 2, T], BF16)
        ktf = AR.alloc("ktf", [128, 8, 256], BF16)
        ktb = AR.alloc("ktb", [128, 8, 256], BF16)
        vt = AR.alloc("vt", [128, 8, 512], BF16)
        sgT = AR.alloc("sgT", [128, 4, T], BF16)
        SbE = [AR.alloc(f"SbE{c}", [128, 2, 512], BF16) for c in range(8)]
        Sf = [AR.alloc(f"Sf{d}", [128, 512], F32) for d in range(2)]
        Sb = [AR.alloc(f"Sb{d}", [128, 512], F32) for d in range(2)]
        Sfb = [AR.alloc(f"Sfb{i}", [128, 2, 512], BF16) for i in range(2)]
        QDF = AR.alloc("QDF", [128, T], F32)
        QDB = AR.alloc("QDB", [128, T], F32)
        itab = AR.alloc("itab", [128, 2, T], F32)
        DT = AR.alloc("DT", [128, 128], F32)
        rdt = AR.alloc("rdt", [128, 4, 128], F32)
        dtmp = [AR.alloc(f"dtmp{i}", [128, 128], F32) for i in range(2)]
        lg = AR.alloc("lg", [128, 8], F32)
        rchk = AR.alloc("rchk", [128, 16], F32)
        rkd = AR.alloc("rkd", [128, 2], F32)
        CDf = AR.alloc("CDf", [128, 8], F32)
        CDb = AR.alloc("CDb", [128, 8], F32)
        KD = AR.alloc("KD", [128, 2], F32)
        ATb = [AR.alloc(f"ATb{c}", [128, 128], BF16) for c in range(8)]
        onb = [AR.alloc(f"onb{i}", [128, 512], BF16) for i in range(3)]
        junk = AR.alloc("junk", [128, 512], BF16)
        ss = [AR.alloc(f"ss{i}", [128, 1], F32) for i in range(3)]
        rs = [AR.alloc(f"rs{i}", [128, 1], F32) for i in range(3)]

        S.dma("sp", lg.ap, dec_d.partition_broadcast(128), [], [lg])
        S.dma("sp", itab.ap, rtab_d.partition_broadcast(128), [], [itab])
        S.dma("sp", rchk.ap, rchk_d.partition_broadcast(128), [], [rchk])
        S.dma("sp", rkd.ap, rkd_d, [], [rkd])
        S.dma("sp", rdt.ap, rdt_d.rearrange("a m n -> m a n"), [], [rdt])
        act(lg.ap, lg.ap, AF.Exp, [lg], [lg])
        ts("dve", lg.ap, lg.ap, -1.0, 1.0, ALU.mult, ALU.add, [lg], [lg])
        act(lg.ap, lg.ap, AF.Ln, [lg], [lg])

        for h in range(4):
            lgf = lg.ap[:, h:h + 1]
            lgb = lg.ap[:, 4 + h:5 + h]
            act(QDF.ap, itab.ap[:, 0, :], AF.Exp, [itab, lg], [QDF], scale=lgf)
            act(QDB.ap, itab.ap[:, 1, :], AF.Exp, [itab, lg], [QDB], scale=lgb)
            act(CDf.ap, rchk.ap[:, 0:8], AF.Exp, [rchk, lg], [CDf], scale=lgf)
            act(CDb.ap, rchk.ap[:, 8:16], AF.Exp, [rchk, lg], [CDb], scale=lgb)
            act(KD.ap[:, 0:1], rkd.ap[:, 0:1], AF.Exp, [rkd, lg], [KD], scale=lgf)
            act(KD.ap[:, 1:2], rkd.ap[:, 1:2], AF.Exp, [rkd, lg], [KD], scale=lgb)
            ts("dve", KD.ap, KD.ap, 1.0 / 16.0, None, ALU.mult, None, [KD], [KD])
            act(dtmp[0].ap, rdt.ap[:, 0, :], AF.Exp, [rdt, lg], [dtmp[0]], scale=lgf)
            tt("dve", dtmp[0].ap, dtmp[0].ap, rdt.ap[:, 1, :], ALU.mult, [dtmp[0], rdt], [dtmp[0]])
            act(dtmp[1].ap, rdt.ap[:, 2, :], AF.Exp, [rdt, lg], [dtmp[1]], scale=lgb)
            tt("dve", dtmp[1].ap, dtmp[1].ap, rdt.ap[:, 3, :], ALU.mult, [dtmp[1], rdt], [dtmp[1]])
            tt("dve", DT.ap, dtmp[0].ap, dtmp[1].ap, ALU.add, [dtmp[0], dtmp[1]], [DT])
            ts("dve", DT.ap, DT.ap, 1.0 / 16.0, None, ALU.mult, None, [DT], [DT])

            base = h * 1536
            tA, vA = load_slab(("winb", 0, base, 512))
            preA = None
            if h == 0:
                preA = chunk_major_first(tA, [[(vA[:, k, f * 128:(f + 1) * 128], hT[k].ap[:, th * 512:(th + 1) * 512]) for k in range(8)]
                                              for f in range(4) for th in range(2)])
            for f in range(4):
                for th in range(2):
                    sl = slice(th * 512, (th + 1) * 512)
                    if preA:
                        p = preA[f * 2 + th]
                    else:
                        p = psum()
                        mm_group(p, p.ap, [(vA[:, k, f * 128:(f + 1) * 128], hT[k].ap[:, sl]) for k in range(8)], [tA] + hT)
                    if f < 2:
                        cp("act", qT.ap[:, f, sl], p.ap, [p], [qT])
                        tt("dve", qTf.ap[:, f, sl], p.ap, QDF.ap[:, sl], ALU.mult, [p, QDF], [qTf])
                        tt("dve", qTb.ap[:, f, sl], p.ap, QDB.ap[:, sl], ALU.mult, [p, QDB], [qTb])
                    else:
                        cp("act" if th else "dve", kT.ap[:, f - 2, sl], p.ap, [p], [kT])
            for half in range(2):
                pT = psum()
                pTb = pT.ap.bitcast(BF16)

                def fnT(e, pTb=pTb, half=half):
                    ins = None
                    for tbl in range(4):
                        tb = half * 4 + tbl
                        for dh in range(2):
                            ins = e.transpose(pTb[:, (tbl * 2 + dh) * 128:(tbl * 2 + dh + 1) * 128],
                                              kT.ap[:, dh, tb * 128:(tb + 1) * 128], identb.ap)
                    return ins

                S.op("pe", fnT, [kT, identb], [pT])
                src = pTb.rearrange("p (a n) -> p a n", a=4)
                ts("dve", ktf.ap[:, half * 4:(half + 1) * 4, :], src, KD.ap[:, 0:1], None, ALU.mult, None, [pT, KD], [ktf])
                act(ktb.ap[:, half * 4:(half + 1) * 4, :], src, AF.Identity, [pT, KD], [ktb], scale=KD.ap[:, 1:2])
            tB, vB = load_slab(("winb", 0, base + 512, 512))
            for tb in range(8):
                p = psum()
                mm_group(p, p.ap, [(hT[k].ap[:, tb * 128:(tb + 1) * 128], vB[:, k, :]) for k in range(8)], [tB] + hT)
                cp("act" if tb % 2 else "dve", vt.ap[:, tb, :], p.ap, [p], [vt])
            for dh in range(2):
                S.dma("sp", Sf[dh].ap, sf_d[h, dh * 128:(dh + 1) * 128, :], [], [Sf[dh]])
                S.dma("sp", Sb[dh].ap, sb_d[h, dh * 128:(dh + 1) * 128, :], [], [Sb[dh]])
            for c in range(8):
                csl = slice(c * 128, (c + 1) * 128)
                pA = psum()
                mm_group(pA, pA.ap[:, 0:128], [(kT.ap[:, dh, csl], qT.ap[:, dh, csl]) for dh in range(2)], [kT, qT])
                tt("dve", ATb[c].ap, pA.ap[:, 0:128], DT.ap, ALU.mult, [pA, DT], [ATb[c]])
            tC, vC = load_slab(("winb", 0, base + 1024, 512))
            idx = 0
            for f in range(4):
                for th in range(2):
                    sl = slice(th * 512, (th + 1) * 512)
                    p = psum()
                    mm_group(p, p.ap, [(vC[:, k, f * 128:(f + 1) * 128], hT[k].ap[:, sl]) for k in range(8)], [tC] + hT)
                    act(sgT.ap[:, f, sl], p.ap, AF.Silu, [p], [sgT])
                    c = 7 - idx
                    idx += 1
                    for dh in range(2):
                        cp("act", SbE[c].ap[:, dh, :], Sb[dh].ap, [Sb[dh]], [SbE[c]])
                        p = psum()
                        mm_group(p, p.ap, [(ktb.ap[:, c, dh * 128:(dh + 1) * 128], vt.ap[:, c, :])], [ktb, vt])
                        stt(Sb[dh].ap, Sb[dh].ap, CDb.ap[:, c:c + 1], p.ap, ALU.mult, ALU.add, [Sb[dh], CDb, p], [Sb[dh]])
                        if c % 2 == 0:
                            S.dma("sp", sbo_d[c // 2, h, dh * 128:(dh + 1) * 128, :], Sb[dh].ap, [Sb[dh]], [])
            for dh in range(2):
                cp("act", Sfb[0].ap[:, dh, :], Sf[dh].ap, [Sf[dh]], [Sfb[0]])

            def finish_chunk(c):
                csl = slice(c * 128, (c + 1) * 128)
                ob = onb[c % 3]
                pT = psum()
                pTb = pT.ap.bitcast(BF16)

                def fnT(e, pTb=pTb, ob=ob):
                    ins = None
                    for jv in range(4):
                        ins = e.transpose(pTb[:, jv * 128:(jv + 1) * 128], ob.ap[:, jv * 128:(jv + 1) * 128], identb.ap)
                    return ins

                S.op("pe", fnT, [ob, identb], [pT])
                tt("dve", ogT.ap[:, h * 4:(h + 1) * 4, csl], pTb[:, 0:512].rearrange("p (a n) -> p a n", a=4),
                   sgT.ap[:, :, csl], ALU.mult, [pT, sgT], [ogT])

            pO_l = {}

            def rms_tail(c):
                pO = pO_l.pop(c)
                ss_, rs_, ob = ss[c % 3], rs[c % 3], onb[c % 3]
                ts("dve", rs_.ap, ss_.ap, 1.0 / 512.0, 1e-6, ALU.mult, ALU.add, [ss_], [rs_])
                act(rs_.ap, rs_.ap, AF.Ln, [rs_], [rs_])
                act(rs_.ap, rs_.ap, AF.Exp, [rs_], [rs_], scale=-0.5)
                ts("dve", ob.ap, pO.ap, rs_.ap[:, 0:1], None, ALU.mult, None, [pO, rs_], [ob])

            for c in range(8):
                csl = slice(c * 128, (c + 1) * 128)
                cur = Sfb[c % 2]
                nxt = Sfb[(c + 1) % 2]
                pU = []
                for dh in range(2):
                    p = psum()
                    mm_group(p, p.ap, [(ktf.ap[:, c, dh * 128:(dh + 1) * 128], vt.ap[:, c, :])], [ktf, vt])
                    pU.append(p)
                pO = psum()
                pairs = [(ATb[c].ap, vt.ap[:, c, :])]
                pairs += [(qTf.ap[:, dh, csl], cur.ap[:, dh, :]) for dh in range(2)]
                pairs += [(qTb.ap[:, dh, csl], SbE[c].ap[:, dh, :]) for dh in range(2)]
                mm_group(pO, pO.ap, pairs, [ATb[c], vt, qTf, cur, qTb, SbE[c]])
                pO_l[c] = pO
                for dh in range(2):
                    stt(Sf[dh].ap, Sf[dh].ap, CDf.ap[:, c:c + 1], pU[dh].ap, ALU.mult, ALU.add, [Sf[dh], CDf, pU[dh]], [Sf[dh]])
                    cp("act", nxt.ap[:, dh, :], Sf[dh].ap, [Sf[dh]], [nxt])
                    if c % 2 == 1:
                        S.dma("sp", sfo_d[c // 2, h, dh * 128:(dh + 1) * 128, :], Sf[dh].ap, [Sf[dh]], [])
                ss_ = ss[c % 3]
                act(junk.ap, pO.ap, AF.Square, [pO], [junk, ss_], accum_out=ss_.ap[:, 0:1])
                if c >= 1:
                    rms_tail(c - 1)
                if c >= 2:
                    finish_chunk(c - 2)
            rms_tail(7)
            finish_chunk(6)
            finish_chunk(7)
        ws = ln_ws_alias(SbE[0], SbE, qT, [qT, qTf, qTb, kT])
        out_proj_residual(("woutb", 0), 16, [ogT], ogT.ap, ws)

    def attn_layer(li, j):
        lam_init = 0.8 - 0.6 * math.exp(-0.3 * li)
        AR.reset()
        on_all = AR.alloc("on_all", [128, 8, 8, 128], BF16)
        QT = AR.alloc("QT", [128, 8, T], BF16)
        ogT = QT
        KT = AR.alloc("KT", [128, 8, 1280], BF16)
        Va = AR.alloc("Va", [128, 10, 4, 129], BF16)
        ET = [AR.alloc(f"ET{i}", [128, 10, 1024], BF16) for i in range(2)]
        kst = [AR.alloc(f"kst{i}", [128, 512], F32) for i in range(2)]
        vst = kst
        qr = [AR.alloc(f"qr{i}", [128, 512], BF16) for i in range(3)]
        rt = [AR.alloc(f"rt{i}", [128, 8, 32], F32) for i in range(8)]
        ropeC = AR.alloc("ropeC", [128, 8, 32], F32)
        ropeS = AR.alloc("ropeS", [128, 8, 32], F32)
        Osave = AR.alloc("Osave", [128, 8, 129], F32)
        otmp = AR.alloc("otmp", [128, 128], F32)
        ofp = AR.alloc("ofp", [128, 8, 128], F32)
        junk = AR.alloc("junk", [128, 128], F32)
        ss = AR.alloc("ss", [128, 8], F32)
        rstd = AR.alloc("rstd", [128, 8], F32)
        r0 = AR.alloc("r0", [128, 1], F32)
        r1 = AR.alloc("r1", [128, 1], F32)
        lamt = AR.alloc("lamt", [128, 256], F32)
        lprod = AR.alloc("lprod", [128, 2, 64], F32)
        lsum = AR.alloc("lsum", [128, 2], F32)
        nlam = AR.alloc("nlam", [128, 1], F32)
        subS = AR.alloc("subS", [128, 2], F32)
        szb = [qr[0], qr[1]]

        S.dma("sp", ropeC.ap, ropec_d, [], [ropeC])
        S.dma("sp", ropeS.ap, ropes_d, [], [ropeS])
        S.dma("sp", lamt.ap, lam_d[j].partition_broadcast(128), [], [lamt])
        S.dma("sp", subS.ap, subln_d, [], [subS])
        S.dma("sp", QT.ap[64:69, :, :], maskq_d, [], [QT])
        S.dma("sp", KT.ap[64:69, :, :], maskk_d, [], [KT])
        lt = lamt.ap.rearrange("p (a b) -> p a b", a=4)
        tt("dve", lprod.ap[:, 0, :], lt[:, 0, :], lt[:, 1, :], ALU.mult, [lamt], [lprod])
        tt("dve", lprod.ap[:, 1, :], lt[:, 2, :], lt[:, 3, :], ALU.mult, [lamt], [lprod])
        S.op("dve", lambda e: e.tensor_reduce(out=lsum.ap, in_=lprod.ap, axis=AX.X, op=ALU.add), [lprod], [lsum])
        act(lsum.ap, lsum.ap, AF.Exp, [lsum], [lsum])
        tt("dve", nlam.ap, lsum.ap[:, 1:2], lsum.ap[:, 0:1], ALU.subtract, [lsum], [nlam])
        ts("dve", nlam.ap, nlam.ap, -lam_init, None, ALU.add, None, [nlam], [nlam])
        ts("dve", subS.ap, subS.ap, 1.0 - lam_init, None, ALU.mult, None, [subS], [subS])
        S.op("dve", lambda e: e.memset(Va.ap[:, :, :, 128:129], 1.0), [], [Va])

        w_in = wina_d[j]
        cnt = [0]

        rope_i = [0]

        def rope(p, tb, dst4):
            rt_ = rt[4 * (rope_i[0] % 2):4 * (rope_i[0] % 2) + 4]
            rope_i[0] += 1
            p4 = p.ap.rearrange("p (a b c) -> p a b c", a=8, b=2, c=32)
            x1 = p4[:, :, 0, :]
            x2 = p4[:, :, 1, :]
            Cb = ropeC.ap[:, tb:tb + 1, :].to_broadcast([128, 8, 32])
            Sb_ = ropeS.ap[:, tb:tb + 1, :].to_broadcast([128, 8, 32])
            tt("dve", rt_[0].ap, x1, Cb, ALU.mult, [p, ropeC], [rt_[0]])
            tt("dve", rt_[1].ap, x2, Sb_, ALU.mult, [p, ropeS], [rt_[1]])
            tt("dve", rt_[2].ap, x1, Sb_, ALU.mult, [p, ropeS], [rt_[2]])
            tt("dve", rt_[3].ap, x2, Cb, ALU.mult, [p, ropeC], [rt_[3]])
            return [(dst4[:, :, 0, :], rt_[0], rt_[1], ALU.subtract), (dst4[:, :, 1, :], rt_[2], rt_[3], ALU.add)]

        for g in range(2):
            tQ, vQ = load_slab(("wina", j, g * 512, 512))
            QTv = QT.ap.rearrange("p (h m) t -> p h m t", m=2)
            KTv = KT.ap.rearrange("p (h m) t -> p h m t", m=2)

            def transpose_evac(src_tile, src_ap, dstv, dst_tile, col0, idn, bf):
                pT = psum()
                pTv = pT.ap.bitcast(BF16) if bf else pT.ap

                def fnT(e, pTv=pTv, src_ap=src_ap):
                    ins = None
                    for hl in range(4):
                        ins = e.transpose(pTv[:, hl * 128:(hl + 1) * 128], src_ap[:, hl * 128:(hl + 1) * 128], idn.ap)
                    return ins

                S.op("pe", fnT, [src_tile, idn], [pT])
                cs = slice(col0, col0 + 128)
                cp("act", dstv[0:64, :, 0, cs], pTv[0:64, 0:512].rearrange("p (a n) -> p a n", a=4), [pT], [dst_tile])
                cp("act", dstv[0:64, :, 1, cs], pTv[64:128, 0:512].rearrange("p (a n) -> p a n", a=4), [pT], [dst_tile])

            def proj_rope(tW, vW, tb, dst_tile, p_pre=None):
                tsl = slice(tb * 128, (tb + 1) * 128)
                if p_pre is not None:
                    p = p_pre
                else:
                    p = psum()
                    mm_group(p, p.ap, [(hT[k].ap[:, tsl], vW[:, k, :]) for k in range(8)], [tW] + hT)
                for dst, a_, b_, op in rope(p, tb, dst_tile.ap.rearrange("p (a b c) -> p a b c", a=8, b=2, c=32)):
                    tt("pool", dst, a_.ap, b_.ap, op, [a_, b_], [dst_tile])

            preQ = None
            if g == 0:
                preQ = chunk_major_first(tQ, [[(hT[k].ap[:, tb * 128:(tb + 1) * 128], vQ[:, k, :]) for k in range(8)] for tb in range(8)])
            for tb in range(10):
                if tb < 8:
                    proj_rope(tQ, vQ, tb, qr[tb % 3], p_pre=(preQ[tb] if preQ else None))
                if tb >= 2:
                    transpose_evac(qr[(tb - 2) % 3], qr[(tb - 2) % 3].ap, QTv, QT, (tb - 2) * 128, identb, True)
            tK, vK = load_slab(("wina", j, 1024 + g * 512, 512))
            for tb in range(10):
                if tb < 8:
                    proj_rope(tK, vK, tb, kst[tb % 2])
                if 1 <= tb <= 8:
                    k_ = kst[(tb - 1) % 2]
                    kb_ = qr[(tb - 1) % 3]
                    tsl = slice((tb - 1) * 128, tb * 128)
                    S.dma("sp", cko_d[j, tsl, g * 512:(g + 1) * 512], k_.ap, [k_], [])
                    cp("act", kb_.ap, k_.ap, [k_], [kb_])
                if tb >= 2:
                    kb2 = qr[(tb - 2) % 3]
                    transpose_evac(kb2, kb2.ap, KTv, KT, (tb - 2) * 128, identb, True)
            for sb_ in range(2):
                S.dma("pool", qr[sb_].ap, ck_d[j, sb_ * 128:(sb_ + 1) * 128, g * 512:(g + 1) * 512], [], [qr[sb_]])
            for sb_ in range(2):
                transpose_evac(qr[sb_], qr[sb_].ap, KTv, KT, 1024 + sb_ * 128, identb, True)
            tV, vV = load_slab(("wina", j, 2048 + g * 512, 512))
            for tb in range(8):
                tsl = slice(tb * 128, (tb + 1) * 128)
                p = psum()
                mm_group(p, p.ap, [(hT[k].ap[:, tsl], vV[:, k, :]) for k in range(8)], [tV] + hT)
                cp("act", Va.ap[:, tb, :, 0:128], p.ap.rearrange("p (a n) -> p a n", a=4), [p], [Va])
                v_ = vst[tb % 2]
                cp("dve", v_.ap, p.ap, [p], [v_])
                S.dma("sp", cvo_d[j, tsl, g * 512:(g + 1) * 512], v_.ap, [v_], [])
            for sb_ in range(2):
                S.dma("pool", Va.ap[:, 8 + sb_, :, 0:128],
                      cv_d[j, sb_ * 128:(sb_ + 1) * 128, g * 512:(g + 1) * 512].rearrange("p (a n) -> p a n", a=4), [], [Va])
            units = [(hl, m) for hl in range(4) for m in range(2)]
            psum_double_mode(True)

            def stageA(u, E):
                hl, m = u
                hm = hl * 2 + m
                items = []
                for kb in range(10):
                    def it(kb=kb):
                        pS = psum2()

                        def fn(e, pS=pS, kb=kb):
                            ins = None
                            for qh in range(2):
                                ins = e.matmul(pS.ap[:, qh * 512:(qh + 1) * 512], lhsT=KT.ap[0:69, hm, kb * 128:(kb + 1) * 128],
                                               rhs=QT.ap[0:69, hm, qh * 512:(qh + 1) * 512], start=True, stop=True)
                            return ins

                        S.op("pe", fn, [KT, QT], [pS])
                        act(E.ap[:, kb, :], pS.ap, AF.Exp, [pS], [E], scale=0.125)
                    items.append(it)
                return items

            def stageB(u, E):
                hl, m = u
                h = g * 4 + hl
                items = []
                pOs = {}
                for qb in range(8):
                    def it0(qb=qb):
                        pO = psum()
                        pOs[qb] = pO
                        mm_group(pO, pO.ap[:, 0:129],
                                 [(E.ap[:, kb, qb * 128:(qb + 1) * 128], Va.ap[:, kb, hl, :]) for kb in range(5)], [E, Va], last=False)
                    items.append(it0)

                    def it(qb=qb):
                        pO = pOs.pop(qb)
                        mm_group(pO, pO.ap[:, 0:129],
                                 [(E.ap[:, kb, qb * 128:(qb + 1) * 128], Va.ap[:, kb, hl, :]) for kb in range(5, 10)], [E, Va], first=False)
                        if m == 0:
                            cp("dve", Osave.ap[:, qb, :], pO.ap[:, 0:129], [pO], [Osave])
                        else:
                            S.op("dve", lambda e, pO=pO: e.reciprocal(out=r1.ap, in_=pO.ap[:, 128:129]), [pO], [r1])
                            S.op("dve", lambda e, qb=qb: e.reciprocal(out=r0.ap, in_=Osave.ap[:, qb, 128:129]), [Osave], [r0])
                            ts("dve", r1.ap, r1.ap, nlam.ap[:, 0:1], None, ALU.mult, None, [r1, nlam], [r1])
                            ts("dve", otmp.ap, Osave.ap[:, qb, 0:128], r0.ap[:, 0:1], None, ALU.mult, None, [Osave, r0], [otmp])
                            stt(ofp.ap[:, qb, :], pO.ap[:, 0:128], r1.ap[:, 0:1], otmp.ap, ALU.mult, ALU.add, [pO, r1, otmp], [ofp])
                            S.op("dve", lambda e, qb=qb: e.scalar_tensor_tensor(out=junk.ap, in0=ofp.ap[:, qb, :], scalar=1.0, in1=ofp.ap[:, qb, :],
                                                                             op0=ALU.mult, op1=ALU.mult, accum_out=ss.ap[:, qb:qb + 1]),
                                 [ofp], [junk, ss])
                    items.append(it)
                if m == 1:
                    def fin():
                        ts("dve", rstd.ap, ss.ap, 1.0 / 128.0, 1e-6, ALU.mult, ALU.add, [ss], [rstd])
                        act(rstd.ap, rstd.ap, AF.Ln, [rstd], [rstd])
                        act(rstd.ap, rstd.ap, AF.Exp, [rstd], [rstd], scale=-0.5)
                        tt("dve", on_all.ap[:, :, h, :], ofp.ap, rstd.ap.unsqueeze(2).to_broadcast([128, 8, 128]), ALU.mult,
                           [ofp, rstd], [on_all])
                    items.append(fin)
                return items

            Ebuf = {}
            for i in range(len(units) + 1):
                A = []
                B = []
                if i < len(units):
                    Ebuf[i] = ET[cnt[0] % 2]
                    cnt[0] += 1
                    A = stageA(units[i], Ebuf[i])
                if i >= 1:
                    B = stageB(units[i - 1], Ebuf[i - 1])
                pattern = [1] * 10
                bsteps = [2, 1, 2, 1, 2, 2, 1, 2, 1, 2]
                ai = 0
                bi = 0
                for si, npat in enumerate(pattern):
                    for _ in range(npat):
                        if ai < len(A):
                            A[ai]()
                            ai += 1
                    for _ in range(bsteps[si]):
                        if bi < len(B):
                            B[bi]()
                            bi += 1
                while ai < len(A):
                    A[ai]()
                    ai += 1
                while bi < len(B):
                    B[bi]()
                    bi += 1
            psum_double_mode(False)
        for g in range(2):
            tZ, vZ = load_slab(("wina", j, 3072 + g * 512, 512))
            for tb in range(8):
                tsl = slice(tb * 128, (tb + 1) * 128)
                p = psum()
                mm_group(p, p.ap, [(hT[k].ap[:, tsl], vZ[:, k, :]) for k in range(8)], [tZ] + hT)
                z_ = szb[tb % 2]
                act(z_.ap, p.ap, AF.Silu, [p], [z_])
                dst = on_all.ap[:, tb, g * 4:(g + 1) * 4, :]
                tt("dve", dst, dst, z_.ap.rearrange("p (a n) -> p a n", a=4), ALU.mult, [on_all, z_], [on_all])
        for tb in range(8):
            tsl = slice(tb * 128, (tb + 1) * 128)
            pT = psum()
            pTb = pT.ap.bitcast(BF16)

            def fnT(e, pTb=pTb, tb=tb):
                ins = None
                for h in range(8):
                    ins = e.transpose(pTb[:, h * 128:(h + 1) * 128], on_all.ap[:, tb, h, :], identb.ap)
                return ins

            S.op("pe", fnT, [on_all, identb], [pT])
            ts("dve" if tb % 2 else "dve", ogT.ap[:, :, tsl], pTb.rearrange("p (a n) -> p a n", a=8), subS.ap[:, j:j + 1], None,
               ALU.mult, None, [pT, subS], [ogT])
        ws = ln_ws_alias(ET[0], [ET[0]], ET[1], [ET[1]])
        out_proj_residual(("wouta", j), 8, [ogT], ogT.ap, ws)

    AR.reset()
    xin = [AR.alloc(f"xin{i}", [128, D], F32) for i in range(4)]

    def x_block(tb):
        xi = xin[tb % 4]
        S.dma("sp", xi.ap, x_d[tb * 128:(tb + 1) * 128, :], [], [xi])
        for half in range(2):
            p = psum()

            def fn(e, p=p, xi=xi, half=half):
                ins = None
                for q in range(4):
                    c = half * 4 + q
                    ins = e.transpose(p.ap[:, q * 128:(q + 1) * 128], xi.ap[:, c * 128:(c + 1) * 128], ident.ap)
                return ins

            S.op("pe", fn, [xi, ident], [p])
            dst = xT_t[:, half * 4:half * 4 + 4, tb * 128:(tb + 1) * 128]
            src = p.ap.rearrange("p (q n) -> p q n", q=4)
            cp("dve" if half == 0 else "act", dst, src, [p], xT[half * 4:half * 4 + 4])

    x_block(0)
    x_block(1)
    for s6 in range(4):
        mod_slab(0, s6)
        if s6 < 3:
            x_block(2 + 2 * s6)
            x_block(3 + 2 * s6)

    for li, (kind, j) in enumerate(layers):
        modulation(li)
        if stop_after == "mod":
            break
        if kind == 2:
            conv_layer(li, j)
        elif kind == 1:
            ret_layer(li, j)
        else:
            attn_layer(li, j)
        if stop_after == "mix":
            break
        layer_norm(li)

    AR.reset()
    yout = [AR.alloc(f"yout{i}", [128, D], F32) for i in range(2)]
    if debug_xT:
        for c in range(NCH):
            S.dma("sp", dbg_d[:, c, :], xT[c].ap, [xT[c]], [])
    for tb in range(8):
        yo = yout[tb % 2]
        for half in range(2):
            p = psum()

            def fn(e, p=p, tb=tb, half=half):
                ins = None
                for q in range(4):
                    c = half * 4 + q
                    ins = e.transpose(p.ap[:, q * 128:(q + 1) * 128], xT_t[:, c, tb * 128:(tb + 1) * 128], ident.ap)
                return ins

            S.op("pe", fn, xT[half * 4:half * 4 + 4] + [ident], [p])
            cp("dve" if half == 0 else "act", yo.ap[:, half * 512:(half + 1) * 512], p.ap, [p], [yo])
        S.dma("sp", y_d[tb * 128:(tb + 1) * 128, :], yo.ap, [yo], [])
    S.finish("sp")

    with nc.Block() as block:
        @block.tensor
        def _(e):
            S.replay("pe", e)

        @block.scalar
        def _(e):
            S.replay("act", e)

        @block.vector
        def _(e):
            S.replay("dve", e)

        @block.gpsimd
        def _(e):
            S.replay("pool", e)

        @block.sync
        def _(e):
            S.replay("sp", e)

    return nc


def _col(v, n):
    return np.ascontiguousarray(np.asarray(v, np.float32).reshape(n, 128).T)


def make_in_maps(inp, layers=LAYERS):
    f32 = np.float32
    g = {k: np.asarray(v) for k, v in inp.items()}
    shared = {}
    shared["bmod"] = np.stack([_col(g["b_mod"][i], 24) for i in range(DEPTH)]).astype(f32)
    shared["lng"] = np.stack([_col(g["ln_g"][i], 8) for i in range(DEPTH)]).astype(f32)
    shared["lnb"] = np.stack([_col(g["ln_b"][i], 8) for i in range(DEPTH)]).astype(f32)
    kinds = set(k for k, _ in layers)
    shared["wmod"] = np.ascontiguousarray(g["w_mod"][:max(1, len(layers))], f32)
    if 0 in kinds:
        shared["wina"] = np.ascontiguousarray(g["w_in_a"], f32)
        shared["wouta"] = np.ascontiguousarray(g["w_out_a"], f32)
    wb = g["w_in_b"][0]
    cols = []
    for h in range(4):
        cols.append(wb[:, h * 256:(h + 1) * 256])
        cols.append(wb[:, 1024 + h * 256:1024 + (h + 1) * 256])
        cols.append(wb[:, 2048 + h * 512:2048 + (h + 1) * 512])
        cols.append(wb[:, 4096 + h * 512:4096 + (h + 1) * 512])
    if 1 in kinds:
        shared["winb"] = np.ascontiguousarray(np.concatenate(cols, 1), f32)
        shared["woutb"] = np.ascontiguousarray(g["w_out_b"][0], f32)
    wc = g["w_in_c"][0]
    cols = []
    for e in range(8):
        for q in range(4):
            cols.append(wc[:, q * 1024 + e * 128:q * 1024 + (e + 1) * 128])
    if 2 in kinds:
        shared["winc"] = np.ascontiguousarray(np.concatenate(cols, 1), f32)
        shared["woutc"] = np.ascontiguousarray(g["w_out_c"][0], f32)
    cw = g["conv_c"][0]
    shared["convw"] = np.ascontiguousarray(cw.reshape(3, 8, 128).transpose(2, 1, 0), f32)
    shared["lam"] = np.ascontiguousarray(g["lam_a"].reshape(2, 256), f32)
    shared["subln"] = np.ascontiguousarray(g["subln_a"].T, f32)
    shared["dec"] = np.concatenate([g["decay_fwd"][0], g["decay_bwd"][0]]).astype(f32)
    shared["ident"] = np.eye(128, dtype=f32)
    m = np.arange(128, dtype=f32)[:, None]
    n = np.arange(128, dtype=f32)[None, :]
    shared["rdt"] = np.stack([np.maximum(n - m, 0), (n >= m).astype(f32), np.maximum(m - n, 0), (n <= m).astype(f32)]).astype(f32)
    shared["rkd"] = np.stack([127.0 - np.arange(128), np.arange(128)], 1).astype(f32)

    tok = np.arange(T)
    r = (tok // 64).astype(np.float64)
    col = (tok % 64).astype(np.float64)
    inv = 10000.0 ** (-np.arange(16, dtype=np.float64) / 16)
    ang = np.concatenate([r[:, None] * inv, col[:, None] * inv], -1)
    cosS = np.cos(ang).astype(f32).reshape(8, 128, 32).transpose(1, 0, 2)
    sinS = np.sin(ang).astype(f32).reshape(8, 128, 32).transpose(1, 0, 2)

    in_maps = []
    for r_ in range(8):
        d = dict(shared)
        if r_ < 4:
            d["x"] = np.ascontiguousarray(g["x_prompt"][4 * r_:4 * r_ + 4].reshape(T, D), f32)
            d["cvec"] = _col(g["c_ctx"], 8)
            d["cachek"] = np.zeros((2, 256, 1024), f32)
            d["cachev"] = np.zeros((2, 256, 1024), f32)
            d["statef"] = np.zeros((4, 256, 512), f32)
            d["stateb"] = np.zeros((4, 256, 512), f32)
            d["ropec"] = np.ones((128, 8, 32), f32)
            d["ropes"] = np.zeros((128, 8, 32), f32)
            seq_q = np.arange(T) // 256
            seq_k = np.concatenate([np.arange(T) // 256, np.full(256, 4)])
            mq = np.zeros((5, T), f32)
            for grp in range(5):
                mq[grp] = np.where(seq_q == grp, 0.0, NEG)
            mk = np.zeros((5, 1280), f32)
            for grp in range(5):
                mk[grp] = (seq_k == grp).astype(f32)
            d["maskq"] = np.ascontiguousarray(np.broadcast_to(mq[:, None, :], (5, 8, T))).astype(ml_dtypes.bfloat16)
            d["maskk"] = np.ascontiguousarray(np.broadcast_to(mk[:, None, :], (5, 8, 1280))).astype(ml_dtypes.bfloat16)
            d["cflag"] = np.ones((128, 1), f32)
            fstart = np.array([c % 2 == 0 for c in range(8)])
            bstart = np.array([c % 2 == 1 for c in range(8)])
        else:
            b = r_ - 4
            d["x"] = np.ascontiguousarray(g["x_sample"][b], f32)
            d["cvec"] = _col(g["c"][b], 8)
            d["cachek"] = np.ascontiguousarray(g["cache_k"][b].reshape(2, 256, 1024), f32)
            d["cachev"] = np.ascontiguousarray(g["cache_v"][b].reshape(2, 256, 1024), f32)
            d["statef"] = np.ascontiguousarray(g["state_fwd"][b, 0], f32)
            d["stateb"] = np.ascontiguousarray(g["state_bwd"][b, 0], f32)
            d["ropec"] = np.ascontiguousarray(cosS)
            d["ropes"] = np.ascontiguousarray(sinS)
            d["maskq"] = np.zeros((5, 8, T), ml_dtypes.bfloat16)
            d["maskk"] = np.zeros((5, 8, 1280), ml_dtypes.bfloat16)
            d["cflag"] = np.zeros((128, 1), f32)
            fstart = np.zeros(8, bool)
            bstart = np.zeros(8, bool)
        i_in = (np.arange(T) % 128).astype(f32)
        cidx = np.arange(T) // 128
        qf = np.where(fstart[cidx], BIG, i_in + 1.0)
        qb = np.where(bstart[cidx], BIG, 128.0 - i_in)
        d["rtab"] = np.stack([qf, qb]).astype(f32)
        d["rchk"] = np.concatenate([np.where(fstart, BIG, 128.0), np.where(bstart, BIG, 128.0)]).astype(f32)
        in_maps.append(d)
    return in_maps


_NC_CACHE = {}


def kernel(**inputs):
    if "full" not in _NC_CACHE:
        _NC_CACHE["full"] = build()
    nc = _NC_CACHE["full"]
    in_maps = make_in_maps(inputs)
    res = run_bass_kernel_spmd(nc, in_maps, core_ids=list(range(8)))
    R = res.results
    f32 = np.float32
    y_prompt = np.stack([R[r]["y"].reshape(4, 256, D) for r in range(4)]).reshape(16, 256, D).astype(f32)
    y_sample = np.stack([R[4 + b]["y"] for b in range(4)]).astype(f32)
    nk = np.stack([R[r]["cko"].reshape(2, 4, 256, 8, 128).transpose(1, 0, 2, 3, 4) for r in range(4)]).reshape(16, 2, 256, 8, 128)
    nv = np.stack([R[r]["cvo"].reshape(2, 4, 256, 8, 128).transpose(1, 0, 2, 3, 4) for r in range(4)]).reshape(16, 2, 256, 8, 128)
    nsf = np.stack([R[r]["sfo"] for r in range(4)]).reshape(16, 1, 4, 256, 512)
    nsb = np.stack([R[r]["sbo"] for r in range(4)]).reshape(16, 1, 4, 256, 512)
    return (y_prompt, y_sample, nk.astype(f32), nv.astype(f32), nsf.astype(f32), nsb.astype(f32))
```

```python
import math
import numpy as np
import ml_dtypes
import concourse.bass as bass
import concourse.mybir as mybir
from concourse.bass_utils import run_bass_kernel_spmd

F32 = mybir.dt.float32
BF16 = mybir.dt.bfloat16
AF = mybir.ActivationFunctionType
ALU = mybir.AluOpType
AX = mybir.AxisListType

D = 1024
T = 1024
NCH = 8
DEPTH = 4
ALPHA = (2.0 * DEPTH) ** 0.25
LN_EPS = 1e-5
NEG = -30000.0
BIG = 1.0e6
SAME_ENGINE_SYNC = True
NDMA = 16


class Tile:
    __slots__ = ("ap", "w", "r", "name", "psum")

    def __init__(self, ap, name="", psum=False):
        self.ap = ap
        self.w = None
        self.r = {}
        self.name = name
        self.psum = psum

    def __getitem__(self, k):
        return self.ap[k]


class _Eng:
    def __init__(self, name, sem):
        self.name = name
        self.sem = sem
        self.count = 0
        self.ops = []
        self.seen = {}
        self.dma_sems = []
        self.dma_cnt = []
        self.dma_i = 0


class Sched:
    def __init__(self, nc):
        self.nc = nc
        self.E = {}
        self.semobjs = {}

    def add_engine(self, name, sem, dma_sems=()):
        e = _Eng(name, sem)
        e.dma_sems = list(dma_sems)
        e.dma_cnt = [0] * len(e.dma_sems)
        self.E[name] = e
        self.semobjs[id(sem)] = sem
        for s in dma_sems:
            self.semobjs[id(s)] = s

    def _collect(self, E, reads, writes, waits):
        def need(tok):
            if tok is None:
                return
            sid, val = tok
            if sid == id(E.sem) and (E.name == "pe" or not SAME_ENGINE_SYNC):
                return
            if E.seen.get(sid, 0) >= val:
                return
            if waits.get(sid, 0) < val:
                waits[sid] = val

        for t in reads:
            need(t.w)
            if t.psum:
                for sid, val in t.r.items():
                    if sid != id(E.sem):
                        need((sid, val))
        for t in writes:
            need(t.w)
            for sid, val in t.r.items():
                need((sid, val))

    def op(self, eng, fn, reads=(), writes=()):
        E = self.E[eng]
        waits = {}
        self._collect(E, reads, writes, waits)
        for sid, val in waits.items():
            E.seen[sid] = val
        E.count += 1
        tok = (id(E.sem), E.count)
        E.ops.append((list(waits.items()), fn, E.sem, 1))
        for t in reads:
            if t.r.get(tok[0], 0) < tok[1]:
                t.r[tok[0]] = tok[1]
        for t in writes:
            t.w = tok
            t.r = {}
        return tok

    def dma(self, eng, out, in_, reads=(), writes=()):
        E = self.E[eng]
        slot = E.dma_i % len(E.dma_sems)
        E.dma_i += 1
        sem = E.dma_sems[slot]
        waits = {}
        prev = E.dma_cnt[slot]
        if prev > 0 and E.seen.get(id(sem), 0) < prev:
            waits[id(sem)] = prev
        E.dma_cnt[slot] += 16
        val = E.dma_cnt[slot]
        self._collect(E, reads, writes, waits)
        for sid, v in waits.items():
            E.seen[sid] = v
        tok = (id(sem), val)

        def fn(e, out=out, in_=in_):
            return e.dma_start(out=out, in_=in_)

        E.ops.append((list(waits.items()), fn, sem, 16))
        for t in reads:
            if t.r.get(tok[0], 0) < tok[1]:
                t.r[tok[0]] = tok[1]
        for t in writes:
            t.w = tok
            t.r = {}
        return tok

    def finish(self, eng="sp"):
        E = self.E[eng]
        waits = {}
        for o in self.E.values():
            if o.count > 0 and o is not E:
                waits[id(o.sem)] = o.count
            for s, c in zip(o.dma_sems, o.dma_cnt):
                if c > 0:
                    waits[id(s)] = c
        E.ops.append((list(waits.items()), None, None, 0))

    def replay(self, name, e):
        E = self.E[name]
        for waits, fn, sem, inc in E.ops:
            for sid, val in waits:
                e.wait_ge(self.semobjs[sid], val)
            if fn is None:
                continue
            ins = fn(e)
            ins.then_inc(sem, inc)


LAYERS = [(0, 0), (1, 0), (2, 0), (0, 1)]


def build(layers=LAYERS, debug_xT=False, stop_after=None):
    nc = bass.Bass("TRN2", target_bir_lowering=False)
    dt_in = lambda name, shape, dt=F32: nc.dram_tensor(name, list(shape), dt, kind="ExternalInput").ap()
    dt_out = lambda name, shape, dt=F32: nc.dram_tensor(name, list(shape), dt, kind="ExternalOutput").ap()

    x_d = dt_in("x", [T, D])
    cvec_d = dt_in("cvec", [128, 8])
    bmod_d = dt_in("bmod", [DEPTH, 128, 24])
    lng_d = dt_in("lng", [DEPTH, 128, 8])
    lnb_d = dt_in("lnb", [DEPTH, 128, 8])
    wmod_d = dt_in("wmod", [max(1, len(layers)), D, 3 * D])
    kinds = set(k for k, _ in layers)
    if 0 in kinds:
        wina_d = dt_in("wina", [2, D, 4096])
        wouta_d = dt_in("wouta", [2, D, D])
    if 1 in kinds:
        winb_d = dt_in("winb", [D, 6144])
        woutb_d = dt_in("woutb", [2048, D])
    if 2 in kinds:
        winc_d = dt_in("winc", [D, 4096])
        woutc_d = dt_in("woutc", [D, D])
    convw_d = dt_in("convw", [128, 8, 3])
    lam_d = dt_in("lam", [2, 256])
    subln_d = dt_in("subln", [128, 2])
    dec_d = dt_in("dec", [8])
    ck_d = dt_in("cachek", [2, 256, 1024])
    cv_d = dt_in("cachev", [2, 256, 1024])
    sf_d = dt_in("statef", [4, 256, 512])
    sb_d = dt_in("stateb", [4, 256, 512])
    ident_d = dt_in("ident", [128, 128])
    ropec_d = dt_in("ropec", [128, 8, 32])
    ropes_d = dt_in("ropes", [128, 8, 32])
    maskq_d = dt_in("maskq", [5, 8, T], BF16)
    maskk_d = dt_in("maskk", [5, 8, 1280], BF16)
    rtab_d = dt_in("rtab", [2, T])
    rchk_d = dt_in("rchk", [16])
    rdt_d = dt_in("rdt", [4, 128, 128])
    rkd_d = dt_in("rkd", [128, 2])
    cflag_d = dt_in("cflag", [128, 1])

    y_d = dt_out("y", [T, D])
    cko_d = dt_out("cko", [2, T, 1024])
    cvo_d = dt_out("cvo", [2, T, 1024])
    sfo_d = dt_out("sfo", [4, 4, 256, 512])
    sbo_d = dt_out("sbo", [4, 4, 256, 512])
    dbg_d = dt_out("dbg", [128, 8, T]) if debug_xT else None

    S = Sched(nc)
    nsem = 5 + 2 * NDMA
    sems = [nc.alloc_semaphore(name=f"s{i}") for i in range(nsem)]
    S.add_engine("pe", sems[0])
    S.add_engine("act", sems[1])
    S.add_engine("dve", sems[2])
    S.add_engine("pool", sems[3], sems[5:5 + NDMA])
    S.add_engine("sp", sems[4], sems[5 + NDMA:5 + 2 * NDMA])

    def sb(name, shape, dt=F32):
        return nc.alloc_sbuf_tensor("sb_" + name, list(shape), dt)

    def tile(name, shape, dt=F32):
        return Tile(sb(name, shape, dt).ap(), name)

    xT = [Tile(None) for _ in range(NCH)]
    xT_t = sb("xT", [128, NCH, T], F32).ap()
    for c in range(NCH):
        xT[c].ap = xT_t[:, c, :]
    hT_t = sb("hT", [128, NCH, T], BF16).ap()
    hT = [Tile(hT_t[:, c, :], f"hT{c}") for c in range(NCH)]
    NSLAB = 3
    slab_t = sb("slab", [128, NSLAB, 8 * 512], BF16).ap()
    slabs = [Tile(slab_t[:, i, :], f"slab{i}") for i in range(NSLAB)]
    slab_i = [0]

    ident = tile("ident", [128, 128], F32)
    identb = tile("identb", [128, 128], BF16)
    onesM = tile("onesM", [128, 128], BF16)
    cvec = tile("cvec", [128, 8], F32)
    scb = tile("scb", [128, 8], BF16)
    bmod = tile("bmod", [128, DEPTH, 24], F32)
    lng = tile("lng", [128, DEPTH, 8], F32)
    lnb = tile("lnb", [128, DEPTH, 8], F32)
    sc1 = tile("sc1", [128, 8], F32)
    gA = tile("gA", [128, 8], F32)
    epsln = tile("epsln", [128, 1], F32)

    pd_t = [nc.alloc_psum_tensor(f"pd{i}", [128, 1024], F32).ap() for i in range(4)]
    PS = [Tile(pd_t[i // 2][:, (i % 2) * 512:(i % 2 + 1) * 512], f"ps{i}", psum=True) for i in range(8)]
    PD = [Tile(pd_t[i], f"pd{i}", psum=True) for i in range(2)]
    ps_i = [0]
    pd_i = [0]
    ps_mode = {"double": False}

    def _xfer(srcs, dsts):
        acc = {}
        for t in srcs:
            if t.w is not None and acc.get(t.w[0], 0) < t.w[1]:
                acc[t.w[0]] = t.w[1]
            for s_, v_ in t.r.items():
                if acc.get(s_, 0) < v_:
                    acc[s_] = v_
        for t in dsts:
            t.w = None
            t.r = dict(acc)

    def psum_double_mode(on):
        if on:
            _xfer(PS[0:2], [PD[0]])
            _xfer(PS[2:4], [PD[1]])
        else:
            _xfer([PD[0]], PS[0:2])
            _xfer([PD[1]], PS[2:4])
        ps_mode["double"] = on

    def psum():
        if ps_mode["double"]:
            t = PS[4 + ps_i[0] % 4]
        else:
            t = PS[ps_i[0] % 8]
        ps_i[0] += 1
        return t

    def psum2():
        t = PD[pd_i[0] % 2]
        pd_i[0] += 1
        return t

    ARENA_BYTES = 132 * 1024
    arena_t = sb("arena", [128, ARENA_BYTES // 4], F32).ap()

    class Arena:
        def __init__(self):
            self.off = 0
            self.live = []
            self.pending = {}
            self.offs = {}

        def reset(self):
            for t in self.live:
                if t.w is not None:
                    s_, v_ = t.w
                    if self.pending.get(s_, 0) < v_:
                        self.pending[s_] = v_
                for s_, v_ in t.r.items():
                    if self.pending.get(s_, 0) < v_:
                        self.pending[s_] = v_
            self.live = []
            self.off = 0

        def alloc(self, name, shape, dt=F32):
            esz = 4 if dt == F32 else 2
            n = int(np.prod(shape[1:]))
            nbytes = (n * esz + 3) // 4 * 4
            assert self.off + nbytes <= ARENA_BYTES, (name, self.off, nbytes)
            words = nbytes // 4
            base = arena_t[0:shape[0], self.off // 4:self.off // 4 + words]
            if dt != F32:
                base = base.bitcast(dt)
            if len(shape) > 2:
                letters = "abcdefg"[:len(shape) - 1]
                pat = "p (" + " ".join(letters) + ") -> p " + " ".join(letters)
                kw = {l: s for l, s in zip(letters, shape[1:])}
                base = base.rearrange(pat, **kw)
            t = Tile(base, name)
            t.r = dict(self.pending)
            self.live.append(t)
            self.offs[id(t)] = (self.off, nbytes)
            self.off += nbytes
            return t

        def view(self, name, first_tile, shape, dt, dep_tiles):
            off = self.offs[id(first_tile)][0]
            esz = 4 if dt == F32 else 2
            n = int(np.prod(shape[1:]))
            words = (n * esz + 3) // 4
            end = max(self.offs[id(d)][0] + self.offs[id(d)][1] for d in dep_tiles)
            assert off + words * 4 <= end, (name, off, words * 4, end)
            base = arena_t[0:shape[0], off // 4:off // 4 + words]
            if dt != F32:
                base = base.bitcast(dt)
            if len(shape) > 2:
                letters = "abcdefg"[:len(shape) - 1]
                pat = "p (" + " ".join(letters) + ") -> p " + " ".join(letters)
                base = base.rearrange(pat, **{l: s_ for l, s_ in zip(letters, shape[1:])})
            t = Tile(base, name)
            acc = dict(self.pending)
            for d in dep_tiles:
                if d.w is not None and acc.get(d.w[0], 0) < d.w[1]:
                    acc[d.w[0]] = d.w[1]
                for s_, v_ in d.r.items():
                    if acc.get(s_, 0) < v_:
                        acc[s_] = v_
            t.r = acc
            self.live.append(t)
            self.offs[id(t)] = (off, words * 4)
            return t

        def sub(self, parent, ap, name=""):
            t = Tile(ap, name)
            t.r = dict(self.pending)
            self.live.append(t)
            return t

    AR = Arena()

    def act(out, in_, func, reads, writes, bias=None, scale=None, accum_out=None):
        kw = {}
        if bias is not None:
            kw["bias"] = bias
        if scale is not None:
            kw["scale"] = scale
        if accum_out is not None:
            kw["accum_out"] = accum_out
        return S.op("act", lambda e: e.activation(out=out, in_=in_, func=func, **kw), reads, writes)

    def tt(eng, out, in0, in1, op, reads, writes):
        return S.op(eng, lambda e: e.tensor_tensor(out=out, in0=in0, in1=in1, op=op), reads, writes)

    def ts(eng, out, in0, s1, s2, op0, op1, reads, writes):
        if s2 is None:
            return S.op(eng, lambda e: e.tensor_scalar(out=out, in0=in0, scalar1=s1, scalar2=None, op0=op0), reads, writes)
        return S.op(eng, lambda e: e.tensor_scalar(out=out, in0=in0, scalar1=s1, scalar2=s2, op0=op0, op1=op1), reads, writes)

    def stt(out, in0, scalar, in1, op0, op1, reads, writes):
        return S.op("dve", lambda e: e.scalar_tensor_tensor(out=out, in0=in0, scalar=scalar, in1=in1, op0=op0, op1=op1), reads, writes)

    def cp(eng, out, in_, reads, writes):
        if eng == "act":
            return S.op("act", lambda e: e.copy(out=out, in_=in_), reads, writes)
        return S.op(eng, lambda e: e.tensor_copy(out=out, in_=in_), reads, writes)

    def mm_group(out_tile, out_ap, pairs, extra_reads, first=True, last=True):
        n = len(pairs)

        def fn(e):
            ins = None
            for i, (l, r) in enumerate(pairs):
                ins = e.matmul(out_ap, lhsT=l, rhs=r, start=(first and i == 0), stop=(last and i == n - 1))
            return ins

        return S.op("pe", fn, extra_reads, [out_tile])

    def layer_slabs(kind, j):
        L = []
        if kind == 0:
            for g in range(2):
                for sec in range(3):
                    L.append(("wina", j, sec * 1024 + g * 512, 512))
            for g in range(2):
                L.append(("wina", j, 3072 + g * 512, 512))
            for s_ in range(2):
                L.append(("wouta", j, s_ * 512, 512))
        elif kind == 1:
            for h in range(4):
                for q in range(3):
                    L.append(("winb", 0, h * 1536 + q * 512, 512))
            for s_ in range(4):
                L.append(("woutb", 0, s_ * 256, 256))
        else:
            for e_ in range(8):
                L.append(("winc", 0, e_ * 512, 512))
            for s_ in range(2):
                L.append(("woutc", 0, s_ * 512, 512))
        return L

    def slab_plan():
        plan = []
        for li, (kind, j) in enumerate(layers):
            L = layer_slabs(kind, j)
            if li == 0:
                plan += [("wmod", 0, s6 * 512, 512) for s6 in range(4)]
                plan += [L[0], ("wmod", 0, 4 * 512, 512), ("wmod", 0, 5 * 512, 512)] + L[1:]
            else:
                plan += [("wmod", li, s6 * 512, 512) for s6 in range(6)]
                plan += L
        return plan

    PLAN = slab_plan()
    plan_state = {"next": 0, "issued": 0, "views": {}}

    def slab_src(key):
        name, idx, c0, ncol = key
        base = {"wmod": lambda: wmod_d[idx], "wina": lambda: wina_d[idx], "wouta": lambda: wouta_d[idx],
                "winb": lambda: winb_d, "woutb": lambda: woutb_d, "winc": lambda: winc_d, "woutc": lambda: woutc_d}[name]()
        return base[:, c0:c0 + ncol].rearrange("(k p) n -> p k n", p=128)

    def issue_slab(i):
        src_ap = slab_src(PLAN[i])
        t = slabs[i % NSLAB]
        k, n = src_ap.shape[1], src_ap.shape[2]
        dst = t.ap[:, 0:k * n].rearrange("p (k n) -> p k n", k=k)
        S.dma("pool", dst, src_ap, [], [t])
        plan_state["views"][i] = (t, dst)

    def load_slab(key):
        i = plan_state["next"]
        deferred = None
        while key[0] != "wmod" and PLAN[i][0] == "wmod":
            deferred = PLAN[i][1]
            mod_slab(PLAN[i][1], PLAN[i][2] // 512)
            i = plan_state["next"]
        if deferred is not None:
            finish_gate(deferred)
        assert PLAN[i] == key, (i, PLAN[i], key)
        plan_state["next"] += 1
        oldest = plan_state.get("hold", i)
        while plan_state["issued"] < min(len(PLAN), oldest + NSLAB):
            issue_slab(plan_state["issued"])
            plan_state["issued"] += 1
        ret = plan_state["views"].pop(i)
        if False:
            plan_state["hold"] = i
            mod_slab(PLAN[i + 1][1], PLAN[i + 1][2] // 512)
            del plan_state["hold"]
        return ret

    S.dma("sp", ident.ap, ident_d, [], [ident])
    S.dma("sp", cvec.ap, cvec_d, [], [cvec])
    S.dma("sp", bmod.ap, bmod_d.rearrange("l p j -> p l j"), [], [bmod])
    S.dma("sp", lng.ap, lng_d.rearrange("l p j -> p l j"), [], [lng])
    S.dma("sp", lnb.ap, lnb_d.rearrange("l p j -> p l j"), [], [lnb])
    cp("dve", identb.ap, ident.ap, [ident], [identb])
    S.op("dve", lambda e: e.memset(onesM.ap, 1.0 / 1024.0), [], [onesM])
    S.op("dve", lambda e: e.memset(epsln.ap, LN_EPS / (ALPHA * ALPHA)), [], [epsln])
    act(scb.ap, cvec.ap, AF.Silu, [cvec], [scb])

    modv_l = [tile(f"modv{l}", [128, 24], F32) for l in range(len(layers))]

    def mod_slab(li, s6):
        t, v = load_slab(("wmod", li, s6 * 512, 512))
        pm = psum()

        def fn(e, v=v, pm=pm):
            ins = None
            for jj in range(4):
                for k in range(8):
                    ins = e.matmul(pm.ap[:, jj:jj + 1], lhsT=v[:, k, jj * 128:(jj + 1) * 128], rhs=scb.ap[:, k:k + 1],
                                   start=(k == 0), stop=(k == 7))
            return ins

        S.op("pe", fn, [t, scb], [pm])
        tt("dve", modv_l[li].ap[:, 4 * s6:4 * s6 + 4], pm.ap[:, 0:4], bmod.ap[:, li, 4 * s6:4 * s6 + 4], ALU.add,
           [pm, bmod], [modv_l[li]])

    def finish_gate(li):
        modv = modv_l[li]
        ts("dve", gA.ap, modv.ap[:, 16:24], 1.0 / ALPHA, None, ALU.mult, None, [modv], [gA])

    def modulation(li):
        modv = modv_l[li]
        if li > 0:
            for s6 in range(6):
                mod_slab(li, s6)
            finish_gate(li)
        ts("dve", sc1.ap, modv.ap[:, 8:16], 1.0, None, ALU.add, None, [modv], [sc1])
        for c in range(NCH):
            if c in (0, 4):
                ts("dve", hT[c].ap, xT[c].ap, sc1.ap[:, c:c + 1], modv.ap[:, c:c + 1], ALU.mult, ALU.add,
                   [xT[c], sc1, modv], [hT[c]])
            else:
                act(hT[c].ap, xT[c].ap, AF.Identity, [xT[c], sc1, modv], [hT[c]],
                    bias=modv.ap[:, c:c + 1], scale=sc1.ap[:, c:c + 1])

    def out_proj_residual(wkey, n_e, ogT_tiles, ogT_ap, ws):
        LNWS["ws"] = ws
        ncols = 512 * 8 // n_e
        for s_ in range(D // ncols):
            t, v = load_slab((wkey[0], wkey[1], s_ * ncols, ncols))
            for dcc in range(ncols // 128):
                dc = s_ * (ncols // 128) + dcc
                for th in range(2):
                    p = psum()
                    pairs = [(v[:, e_, dcc * 128:(dcc + 1) * 128], ogT_ap[:, e_, th * 512:(th + 1) * 512]) for e_ in range(n_e)]
                    mm_group(p, p.ap, pairs, [t] + list(ogT_tiles))
                    xs = xT[dc].ap[:, th * 512:(th + 1) * 512]
                    stt(xs, p.ap, gA.ap[:, dc:dc + 1], xs, ALU.mult, ALU.add, [p, gA, xT[dc]], [xT[dc]])
                cp("act", hT[dc].ap, xT[dc].ap, [xT[dc]], [hT[dc]])
                if dc % 2 == 0:
                    act(ws["ysq"].ap[:, dc, :], xT[dc].ap, AF.Square, [xT[dc]], [ws["ysq"]])
                else:
                    tt("dve", ws["ysq"].ap[:, dc, :], xT[dc].ap, xT[dc].ap, ALU.mult, [xT[dc]], [ws["ysq"]])

    LNWS = {}

    def layer_norm(li):
        ws = LNWS["ws"]
        ysq, mean, rstd, tmp = ws["ysq"], ws["mean"], ws["rstd"], ws["tmp"]
        for th in range(2):
            pm = psum()
            mm_group(pm, pm.ap, [(onesM.ap, hT[c].ap[:, th * 512:(th + 1) * 512]) for c in range(NCH)], [onesM] + hT)
            pq = psum()
            mm_group(pq, pq.ap, [(onesM.ap, ysq.ap[:, c, th * 512:(th + 1) * 512]) for c in range(NCH)], [onesM, ysq])
            sl = slice(th * 512, (th + 1) * 512)
            cp("dve", mean.ap[:, sl], pm.ap, [pm], [mean])
            act(tmp[0].ap[:, sl], pm.ap, AF.Square, [pm], [tmp[0]])
            tt("dve", rstd.ap[:, sl], pq.ap, tmp[0].ap[:, sl], ALU.subtract, [pq, tmp[0]], [rstd])
        act(rstd.ap, rstd.ap, AF.Ln, [rstd, epsln], [rstd], bias=epsln.ap[:, 0:1])
        act(rstd.ap, rstd.ap, AF.Exp, [rstd], [rstd], scale=-0.5)
        for c in range(NCH):
            tm = tmp[c % 2]
            tt("dve", tm.ap, xT[c].ap, mean.ap, ALU.subtract, [xT[c], mean], [tm])
            tt("dve", tm.ap, tm.ap, rstd.ap, ALU.mult, [tm, rstd], [tm])
            act(xT[c].ap, tm.ap, AF.Identity, [tm, lng, lnb], [xT[c]],
                bias=lnb.ap[:, li, c:c + 1], scale=lng.ap[:, li, c:c + 1])

    def ln_ws_alloc():
        return {"ysq": AR.alloc("ysq", [128, NCH, T], BF16), "mean": AR.alloc("mean", [128, T], F32),
                "rstd": AR.alloc("rstd", [128, T], F32), "tmp": [AR.alloc(f"lntmp{i}", [128, T], F32) for i in range(2)]}

    def ln_ws_alias(ysq_first, ysq_deps, scr_first, scr_deps):
        ysq = AR.view("ysq", ysq_first, [128, NCH, T], BF16, ysq_deps)
        scr = AR.view("lnscr", scr_first, [128, 4 * T], F32, scr_deps)
        parts = []
        for i in range(4):
            t_ = Tile(scr.ap[:, i * T:(i + 1) * T], f"lnscr{i}")
            t_.r = dict(scr.r)
            AR.live.append(t_)
            parts.append(t_)
        return {"ysq": ysq, "mean": parts[0], "rstd": parts[1], "tmp": [parts[2], parts[3]]}

    def conv_layer(li, j):
        AR.reset()
        ygT = AR.alloc("ygT", [128, 8, T], BF16)
        cu = AR.alloc("cu", [128, T + 2], F32)
        usb = AR.alloc("usb", [128, T], F32)
        szb = AR.alloc("szb", [128, T], F32)
        bgz = AR.alloc("bgz", [128, T], F32)
        yc = AR.alloc("yc", [128, T], F32)
        cw = AR.alloc("cw", [128, 8, 3], F32)
        cwf = AR.alloc("cwf", [128, 8, 3], F32)
        cfl = AR.alloc("cfl", [128, 1], F32)
        ws = ln_ws_alloc()
        S.dma("sp", cw.ap, convw_d, [], [cw])
        S.dma("sp", cfl.ap, cflag_d, [], [cfl])
        ts("dve", cwf.ap, cw.ap, cfl.ap[:, 0:1], -1.0, ALU.mult, ALU.mult, [cw, cfl], [cwf])
        S.op("dve", lambda e: e.memset(cu.ap[:, 0:1], 0.0), [], [cu])
        S.op("dve", lambda e: e.memset(cu.ap[:, T + 1:T + 2], 0.0), [], [cu])
        for ech in range(8):
            t, v = load_slab(("winc", 0, ech * 512, 512))
            for th in range(2):
                sl = slice(th * 512, (th + 1) * 512)
                pp = []
                for q in range(4):
                    p = psum()
                    mm_group(p, p.ap, [(v[:, k, q * 128:(q + 1) * 128], hT[k].ap[:, sl]) for k in range(8)], [t] + hT)
                    pp.append(p)
                cp("act", usb.ap[:, sl], pp[2].ap, [pp[2]], [usb])
                act(szb.ap[:, sl], pp[3].ap, AF.Silu, [pp[3]], [szb])
                tt("dve", cu.ap[:, 1 + th * 512:1 + (th + 1) * 512], pp[1].ap, usb.ap[:, sl], ALU.mult, [pp[1], usb], [cu])
                tt("dve", bgz.ap[:, sl], pp[0].ap, szb.ap[:, sl], ALU.mult, [pp[0], szb], [bgz])
            ts("dve", yc.ap, cu.ap[:, 1:T + 1], cw.ap[:, ech, 1:2], None, ALU.mult, None, [cu, cw], [yc])
            stt(yc.ap, cu.ap[:, 0:T], cw.ap[:, ech, 0:1], yc.ap, ALU.mult, ALU.add, [cu, cw, yc], [yc])
            stt(yc.ap, cu.ap[:, 2:T + 2], cw.ap[:, ech, 2:3], yc.ap, ALU.mult, ALU.add, [cu, cw, yc], [yc])
            ycb = yc.ap.rearrange("p (s t) -> p s t", t=256)
            cub = cu.ap[:, 1:T + 1].rearrange("p (s t) -> p s t", t=256)
            stt(ycb[:, 1:4, 0:1], cub[:, 0:3, 255:256], cwf.ap[:, ech, 0:1], ycb[:, 1:4, 0:1], ALU.mult, ALU.add,
                [cu, cwf, yc], [yc])
            stt(ycb[:, 0:3, 255:256], cub[:, 1:4, 0:1], cwf.ap[:, ech, 2:3], ycb[:, 0:3, 255:256], ALU.mult, ALU.add,
                [cu, cwf, yc], [yc])
            tt("dve", ygT.ap[:, ech, :], yc.ap, bgz.ap, ALU.mult, [yc, bgz], [ygT])
        out_proj_residual(("woutc", 0), 8, [ygT], ygT.ap, ws)


    def ret_layer(li, j):
        AR.reset()
        ogT = AR.alloc("ogT", [128, 16, T], BF16)
        qT = AR.alloc("qT", [128, 2, T], BF16)
        qTf = AR.alloc("qTf", [128, 2, T], BF16)
        qTb = AR.alloc("qTb", [128, 2, T], BF16)
        kT = AR.alloc("kT", [128, 2, T], BF16)
        ktf = AR.alloc("ktf", [128, 8, 256], BF16)
        ktb = AR.alloc("ktb", [128, 8, 256], BF16)
        vt = AR.alloc("vt", [128, 8, 512], BF16)
        sgT = AR.alloc("sgT", [128, 4, T], BF16)
        SbE = [AR.alloc(f"SbE{c}", [128, 2, 512], BF16) for c in range(8)]
        Sf = [AR.alloc(f"Sf{d}", [128, 512], F32) for d in range(2)]
        Sb = [AR.alloc(f"Sb{d}", [128, 512], F32) for d in range(2)]
        Sfb = [AR.alloc(f"Sfb{i}", [128, 2, 512], BF16) for i in range(2)]
        QDF = AR.alloc("QDF", [128, T], F32)
        QDB = AR.alloc("QDB", [128, T], F32)
        itab = AR.alloc("itab", [128, 2, T], F32)
        DT = AR.alloc("DT", [128, 128], F32)
        rdt = AR.alloc("rdt", [128, 4, 128], F32)
        dtmp = [AR.alloc(f"dtmp{i}", [128, 128], F32) for i in range(2)]
        lg = AR.alloc("lg", [128, 8], F32)
        rchk = AR.alloc("rchk", [128, 16], F32)
        rkd = AR.alloc("rkd", [128, 2], F32)
        CDf = AR.alloc("CDf", [128, 8], F32)
        CDb = AR.alloc("CDb", [128, 8], F32)
        KD = AR.alloc("KD", [128, 2], F32)
        ATb = [AR.alloc(f"ATb{c}", [128, 128], BF16) for c in range(8)]
        onb = [AR.alloc(f"onb{i}", [128, 512], BF16) for i in range(3)]
        junk = AR.alloc("junk", [128, 512], BF16)
        ss = [AR.alloc(f"ss{i}", [128, 1], F32) for i in range(3)]
        rs = [AR.alloc(f"rs{i}", [128, 1], F32) for i in range(3)]

        S.dma("sp", lg.ap, dec_d.partition_broadcast(128), [], [lg])
        S.dma("sp", itab.ap, rtab_d.partition_broadcast(128), [], [itab])
        S.dma("sp", rchk.ap, rchk_d.partition_broadcast(128), [], [rchk])
        S.dma("sp", rkd.ap, rkd_d, [], [rkd])
        S.dma("sp", rdt.ap, rdt_d.rearrange("a m n -> m a n"), [], [rdt])
        act(lg.ap, lg.ap, AF.Exp, [lg], [lg])
        ts("dve", lg.ap, lg.ap, -1.0, 1.0, ALU.mult, ALU.add, [lg], [lg])
        act(lg.ap, lg.ap, AF.Ln, [lg], [lg])

        for h in range(4):
            lgf = lg.ap[:, h:h + 1]
            lgb = lg.ap[:, 4 + h:5 + h]
            act(QDF.ap, itab.ap[:, 0, :], AF.Exp, [itab, lg], [QDF], scale=lgf)
            act(QDB.ap, itab.ap[:, 1, :], AF.Exp, [itab, lg], [QDB], scale=lgb)
            act(CDf.ap, rchk.ap[:, 0:8], AF.Exp, [rchk, lg], [CDf], scale=lgf)
            act(CDb.ap, rchk.ap[:, 8:16], AF.Exp, [rchk, lg], [CDb], scale=lgb)
            act(KD.ap[:, 0:1], rkd.ap[:, 0:1], AF.Exp, [rkd, lg], [KD], scale=lgf)
            act(KD.ap[:, 1:2], rkd.ap[:, 1:2], AF.Exp, [rkd, lg], [KD], scale=lgb)
            ts("dve", KD.ap, KD.ap, 1.0 / 16.0, None, ALU.mult, None, [KD], [KD])
            act(dtmp[0].ap, rdt.ap[:, 0, :], AF.Exp, [rdt, lg], [dtmp[0]], scale=lgf)
            tt("dve", dtmp[0].ap, dtmp[0].ap, rdt.ap[:, 1, :], ALU.mult, [dtmp[0], rdt], [dtmp[0]])
            act(dtmp[1].ap, rdt.ap[:, 2, :], AF.Exp, [rdt, lg], [dtmp[1]], scale=lgb)
            tt("dve", dtmp[1].ap, dtmp[1].ap, rdt.ap[:, 3, :], ALU.mult, [dtmp[1], rdt], [dtmp[1]])
            tt("dve", DT.ap, dtmp[0].ap, dtmp[1].ap, ALU.add, [dtmp[0], dtmp[1]], [DT])
            ts("dve", DT.ap, DT.ap, 1.0 / 16.0, None, ALU.mult, None, [DT], [DT])

            base = h * 1536
            tA, vA = load_slab(("winb", 0, base, 512))
            for f in range(4):
                for th in range(2):
                    sl = slice(th * 512, (th + 1) * 512)
                    p = psum()
                    mm_group(p, p.ap, [(vA[:, k, f * 128:(f + 1) * 128], hT[k].ap[:, sl]) for k in range(8)], [tA] + hT)
                    if f < 2:
                        cp("act", qT.ap[:, f, sl], p.ap, [p], [qT])
                        tt("dve", qTf.ap[:, f, sl], p.ap, QDF.ap[:, sl], ALU.mult, [p, QDF], [qTf])
                        tt("dve", qTb.ap[:, f, sl], p.ap, QDB.ap[:, sl], ALU.mult, [p, QDB], [qTb])
                    else:
                        cp("act" if th else "dve", kT.ap[:, f - 2, sl], p.ap, [p], [kT])
            for half in range(2):
                pT = psum()
                pTb = pT.ap.bitcast(BF16)

                def fnT(e, pTb=pTb, half=half):
                    ins = None
                    for tbl in range(4):
                        tb = half * 4 + tbl
                        for dh in range(2):
                            ins = e.transpose(pTb[:, (tbl * 2 + dh) * 128:(tbl * 2 + dh + 1) * 128],
                                              kT.ap[:, dh, tb * 128:(tb + 1) * 128], identb.ap)
                    return ins

                S.op("pe", fnT, [kT, identb], [pT])
                src = pTb.rearrange("p (a n) -> p a n", a=4)
                ts("dve", ktf.ap[:, half * 4:(half + 1) * 4, :], src, KD.ap[:, 0:1], None, ALU.mult, None, [pT, KD], [ktf])
                act(ktb.ap[:, half * 4:(half + 1) * 4, :], src, AF.Identity, [pT, KD], [ktb], scale=KD.ap[:, 1:2])
            tB, vB = load_slab(("winb", 0, base + 512, 512))
            for tb in range(8):
                p = psum()
                mm_group(p, p.ap, [(hT[k].ap[:, tb * 128:(tb + 1) * 128], vB[:, k, :]) for k in range(8)], [tB] + hT)
                cp("act" if tb % 2 else "dve", vt.ap[:, tb, :], p.ap, [p], [vt])
            for dh in range(2):
                S.dma("sp", Sf[dh].ap, sf_d[h, dh * 128:(dh + 1) * 128, :], [], [Sf[dh]])
                S.dma("sp", Sb[dh].ap, sb_d[h, dh * 128:(dh + 1) * 128, :], [], [Sb[dh]])
            for c in range(8):
                csl = slice(c * 128, (c + 1) * 128)
                pA = psum()
                mm_group(pA, pA.ap[:, 0:128], [(kT.ap[:, dh, csl], qT.ap[:, dh, csl]) for dh in range(2)], [kT, qT])
                tt("dve", ATb[c].ap, pA.ap[:, 0:128], DT.ap, ALU.mult, [pA, DT], [ATb[c]])
            tC, vC = load_slab(("winb", 0, base + 1024, 512))
            idx = 0
            for f in range(4):
                for th in range(2):
                    sl = slice(th * 512, (th + 1) * 512)
                    p = psum()
                    mm_group(p, p.ap, [(vC[:, k, f * 128:(f + 1) * 128], hT[k].ap[:, sl]) for k in range(8)], [tC] + hT)
                    act(sgT.ap[:, f, sl], p.ap, AF.Silu, [p], [sgT])
                    c = 7 - idx
                    idx += 1
                    for dh in range(2):
                        cp("act", SbE[c].ap[:, dh, :], Sb[dh].ap, [Sb[dh]], [SbE[c]])
                        p = psum()
                        mm_group(p, p.ap, [(ktb.ap[:, c, dh * 128:(dh + 1) * 128], vt.ap[:, c, :])], [ktb, vt])
                        stt(Sb[dh].ap, Sb[dh].ap, CDb.ap[:, c:c + 1], p.ap, ALU.mult, ALU.add, [Sb[dh], CDb, p], [Sb[dh]])
                        if c % 2 == 0:
                            S.dma("sp", sbo_d[c // 2, h, dh * 128:(dh + 1) * 128, :], Sb[dh].ap, [Sb[dh]], [])
            for dh in range(2):
                cp("act", Sfb[0].ap[:, dh, :], Sf[dh].ap, [Sf[dh]], [Sfb[0]])

            def finish_chunk(c):
                csl = slice(c * 128, (c + 1) * 128)
                ob = onb[c % 3]
                pT = psum()
                pTb = pT.ap.bitcast(BF16)

                def fnT(e, pTb=pTb, ob=ob):
                    ins = None
                    for jv in range(4):
                        ins = e.transpose(pTb[:, jv * 128:(jv + 1) * 128], ob.ap[:, jv * 128:(jv + 1) * 128], identb.ap)
                    return ins

                S.op("pe", fnT, [ob, identb], [pT])
                tt("dve", ogT.ap[:, h * 4:(h + 1) * 4, csl], pTb[:, 0:512].rearrange("p (a n) -> p a n", a=4),
                   sgT.ap[:, :, csl], ALU.mult, [pT, sgT], [ogT])

            pO_l = {}

            def rms_tail(c):
                pO = pO_l.pop(c)
                ss_, rs_, ob = ss[c % 3], rs[c % 3], onb[c % 3]
                ts("dve", rs_.ap, ss_.ap, 1.0 / 512.0, 1e-6, ALU.mult, ALU.add, [ss_], [rs_])
                act(rs_.ap, rs_.ap, AF.Ln, [rs_], [rs_])
                act(rs_.ap, rs_.ap, AF.Exp, [rs_], [rs_], scale=-0.5)
                ts("dve", ob.ap, pO.ap, rs_.ap[:, 0:1], None, ALU.mult, None, [pO, rs_], [ob])

            for c in range(8):
                csl = slice(c * 128, (c + 1) * 128)
                cur = Sfb[c % 2]
                nxt = Sfb[(c + 1) % 2]
                pU = []
                for dh in range(2):
                    p = psum()
                    mm_group(p, p.ap, [(ktf.ap[:, c, dh * 128:(dh + 1) * 128], vt.ap[:, c, :])], [ktf, vt])
                    pU.append(p)
                pO = psum()
                pairs = [(ATb[c].ap, vt.ap[:, c, :])]
                pairs += [(qTf.ap[:, dh, csl], cur.ap[:, dh, :]) for dh in range(2)]
                pairs += [(qTb.ap[:, dh, csl], SbE[c].ap[:, dh, :]) for dh in range(2)]
                mm_group(pO, pO.ap, pairs, [ATb[c], vt, qTf, cur, qTb, SbE[c]])
                pO_l[c] = pO
                for dh in range(2):
                    stt(Sf[dh].ap, Sf[dh].ap, CDf.ap[:, c:c + 1], pU[dh].ap, ALU.mult, ALU.add, [Sf[dh], CDf, pU[dh]], [Sf[dh]])
                    cp("act", nxt.ap[:, dh, :], Sf[dh].ap, [Sf[dh]], [nxt])
                    if c % 2 == 1:
                        S.dma("sp", sfo_d[c // 2, h, dh * 128:(dh + 1) * 128, :], Sf[dh].ap, [Sf[dh]], [])
                ss_ = ss[c % 3]
                act(junk.ap, pO.ap, AF.Square, [pO], [junk, ss_], accum_out=ss_.ap[:, 0:1])
                if c >= 1:
                    rms_tail(c - 1)
                if c >= 2:
                    finish_chunk(c - 2)
            rms_tail(7)
            finish_chunk(6)
            finish_chunk(7)
        ws = ln_ws_alias(SbE[0], SbE, qT, [qT, qTf, qTb, kT])
        out_proj_residual(("woutb", 0), 16, [ogT], ogT.ap, ws)

    def attn_layer(li, j):
        lam_init = 0.8 - 0.6 * math.exp(-0.3 * li)
        AR.reset()
        on_all = AR.alloc("on_all", [128, 8, 8, 128], BF16)
        on_tb = []
        for tb_ in range(8):
            t_ = Tile(on_all.ap[:, tb_], f"on_tb{tb_}")
            t_.r = dict(on_all.r)
            AR.live.append(t_)
            on_tb.append(t_)
        QT = AR.alloc("QT", [128, 8, T], BF16)
        ogT = QT
        KT = AR.alloc("KT", [128, 8, 1280], BF16)
        Va = AR.alloc("Va", [128, 10, 4, 129], BF16)
        ET = [AR.alloc(f"ET{i}", [128, 10, 1024], BF16) for i in range(2)]
        kst = [AR.alloc(f"kst{i}", [128, 512], F32) for i in range(2)]
        vst = kst
        qr = [AR.alloc(f"qr{i}", [128, 512], BF16) for i in range(3)]
        rt = [AR.alloc(f"rt{i}", [128, 8, 32], F32) for i in range(8)]
        ropeC = AR.alloc("ropeC", [128, 8, 32], F32)
        ropeS = AR.alloc("ropeS", [128, 8, 32], F32)
        Osave = AR.alloc("Osave", [128, 8, 129], F32)
        otmp = AR.alloc("otmp", [128, 128], F32)
        ofp = AR.alloc("ofp", [128, 8, 128], F32)
        junk = AR.alloc("junk", [128, 128], F32)
        ss = AR.alloc("ss", [128, 8], F32)
        rstd = AR.alloc("rstd", [128, 8], F32)
        r0 = AR.alloc("r0", [128, 1], F32)
        r1 = AR.alloc("r1", [128, 1], F32)
        lamt = AR.alloc("lamt", [128, 256], F32)
        lprod = AR.alloc("lprod", [128, 2, 64], F32)
        lsum = AR.alloc("lsum", [128, 2], F32)
        nlam = AR.alloc("nlam", [128, 1], F32)
        subS = AR.alloc("subS", [128, 2], F32)
        szb = [qr[0], qr[1]]

        S.dma("sp", ropeC.ap, ropec_d, [], [ropeC])
        S.dma("sp", ropeS.ap, ropes_d, [], [ropeS])
        S.dma("sp", lamt.ap, lam_d[j].partition_broadcast(128), [], [lamt])
        S.dma("sp", subS.ap, subln_d, [], [subS])
        S.dma("sp", QT.ap[64:69, :, :], maskq_d, [], [QT])
        S.dma("sp", KT.ap[64:69, :, :], maskk_d, [], [KT])
        lt = lamt.ap.rearrange("p (a b) -> p a b", a=4)
        tt("dve", lprod.ap[:, 0, :], lt[:, 0, :], lt[:, 1, :], ALU.mult, [lamt], [lprod])
        tt("dve", lprod.ap[:, 1, :], lt[:, 2, :], lt[:, 3, :], ALU.mult, [lamt], [lprod])
        S.op("dve", lambda e: e.tensor_reduce(out=lsum.ap, in_=lprod.ap, axis=AX.X, op=ALU.add), [lprod], [lsum])
        act(lsum.ap, lsum.ap, AF.Exp, [lsum], [lsum])
        tt("dve", nlam.ap, lsum.ap[:, 1:2], lsum.ap[:, 0:1], ALU.subtract, [lsum], [nlam])
        ts("dve", nlam.ap, nlam.ap, -lam_init, None, ALU.add, None, [nlam], [nlam])
        ts("dve", subS.ap, subS.ap, 1.0 - lam_init, None, ALU.mult, None, [subS], [subS])
        S.op("dve", lambda e: e.memset(Va.ap[:, :, :, 128:129], 1.0), [], [Va])

        w_in = wina_d[j]
        cnt = [0]

        rope_i = [0]

        def rope(p, tb, dst4):
            rt_ = rt[4 * (rope_i[0] % 2):4 * (rope_i[0] % 2) + 4]
            rope_i[0] += 1
            p4 = p.ap.rearrange("p (a b c) -> p a b c", a=8, b=2, c=32)
            x1 = p4[:, :, 0, :]
            x2 = p4[:, :, 1, :]
            Cb = ropeC.ap[:, tb:tb + 1, :].to_broadcast([128, 8, 32])
            Sb_ = ropeS.ap[:, tb:tb + 1, :].to_broadcast([128, 8, 32])
            tt("dve", rt_[0].ap, x1, Cb, ALU.mult, [p, ropeC], [rt_[0]])
            tt("dve", rt_[1].ap, x2, Sb_, ALU.mult, [p, ropeS], [rt_[1]])
            tt("dve", rt_[2].ap, x1, Sb_, ALU.mult, [p, ropeS], [rt_[2]])
            tt("dve", rt_[3].ap, x2, Cb, ALU.mult, [p, ropeC], [rt_[3]])
            return [(dst4[:, :, 0, :], rt_[0], rt_[1], ALU.subtract), (dst4[:, :, 1, :], rt_[2], rt_[3], ALU.add)]

        for g in range(2):
            tQ, vQ = load_slab(("wina", j, g * 512, 512))
            QTv = QT.ap.rearrange("p (h m) t -> p h m t", m=2)
            KTv = KT.ap.rearrange("p (h m) t -> p h m t", m=2)

            def transpose_evac(src_tile, src_ap, dstv, dst_tile, col0, idn, bf):
                pT = psum()
                pTv = pT.ap.bitcast(BF16) if bf else pT.ap

                def fnT(e, pTv=pTv, src_ap=src_ap):
                    ins = None
                    for hl in range(4):
                        ins = e.transpose(pTv[:, hl * 128:(hl + 1) * 128], src_ap[:, hl * 128:(hl + 1) * 128], idn.ap)
                    return ins

                S.op("pe", fnT, [src_tile, idn], [pT])
                cs = slice(col0, col0 + 128)
                cp("act", dstv[0:64, :, 0, cs], pTv[0:64, 0:512].rearrange("p (a n) -> p a n", a=4), [pT], [dst_tile])
                cp("act", dstv[0:64, :, 1, cs], pTv[64:128, 0:512].rearrange("p (a n) -> p a n", a=4), [pT], [dst_tile])

            def proj_rope(tW, vW, tb, dst_tile):
                tsl = slice(tb * 128, (tb + 1) * 128)
                p = psum()
                mm_group(p, p.ap, [(hT[k].ap[:, tsl], vW[:, k, :]) for k in range(8)], [tW] + hT)
                for dst, a_, b_, op in rope(p, tb, dst_tile.ap.rearrange("p (a b c) -> p a b c", a=8, b=2, c=32)):
                    tt("pool", dst, a_.ap, b_.ap, op, [a_, b_], [dst_tile])

            for tb in range(10):
                if tb < 8:
                    proj_rope(tQ, vQ, tb, qr[tb % 3])
                if tb >= 2:
                    transpose_evac(qr[(tb - 2) % 3], qr[(tb - 2) % 3].ap, QTv, QT, (tb - 2) * 128, identb, True)
            tK, vK = load_slab(("wina", j, 1024 + g * 512, 512))
            for tb in range(10):
                if tb < 8:
                    proj_rope(tK, vK, tb, kst[tb % 2])
                if 1 <= tb <= 8:
                    k_ = kst[(tb - 1) % 2]
                    kb_ = qr[(tb - 1) % 3]
                    tsl = slice((tb - 1) * 128, tb * 128)
                    S.dma("sp", cko_d[j, tsl, g * 512:(g + 1) * 512], k_.ap, [k_], [])
                    cp("act", kb_.ap, k_.ap, [k_], [kb_])
                if tb >= 2:
                    kb2 = qr[(tb - 2) % 3]
                    transpose_evac(kb2, kb2.ap, KTv, KT, (tb - 2) * 128, identb, True)
            for sb_ in range(2):
                S.dma("pool", qr[sb_].ap, ck_d[j, sb_ * 128:(sb_ + 1) * 128, g * 512:(g + 1) * 512], [], [qr[sb_]])
            for sb_ in range(2):
                transpose_evac(qr[sb_], qr[sb_].ap, KTv, KT, 1024 + sb_ * 128, identb, True)
            tV, vV = load_slab(("wina", j, 2048 + g * 512, 512))
            for tb in range(8):
                tsl = slice(tb * 128, (tb + 1) * 128)
                p = psum()
                mm_group(p, p.ap, [(hT[k].ap[:, tsl], vV[:, k, :]) for k in range(8)], [tV] + hT)
                cp("act", Va.ap[:, tb, :, 0:128], p.ap.rearrange("p (a n) -> p a n", a=4), [p], [Va])
                v_ = vst[tb % 2]
                cp("dve", v_.ap, p.ap, [p], [v_])
                S.dma("sp", cvo_d[j, tsl, g * 512:(g + 1) * 512], v_.ap, [v_], [])
            for sb_ in range(2):
                S.dma("pool", Va.ap[:, 8 + sb_, :, 0:128],
                      cv_d[j, sb_ * 128:(sb_ + 1) * 128, g * 512:(g + 1) * 512].rearrange("p (a n) -> p a n", a=4), [], [Va])
            units = [(hl, m) for hl in range(4) for m in range(2)]
            psum_double_mode(True)

            def stageA(u, E):
                hl, m = u
                hm = hl * 2 + m
                items = []
                for kb in range(10):
                    def it(kb=kb):
                        pS = psum2()

                        def fn(e, pS=pS, kb=kb):
                            ins = None
                            for qh in range(2):
                                ins = e.matmul(pS.ap[:, qh * 512:(qh + 1) * 512], lhsT=KT.ap[0:69, hm, kb * 128:(kb + 1) * 128],
                                               rhs=QT.ap[0:69, hm, qh * 512:(qh + 1) * 512], start=True, stop=True)
                            return ins

                        S.op("pe", fn, [KT, QT], [pS])
                        act(E.ap[:, kb, :], pS.ap, AF.Exp, [pS], [E], scale=0.125)
                    items.append(it)
                return items

            def stageB(u, E):
                hl, m = u
                h = g * 4 + hl
                items = []
                pOs = {}
                for qb in range(8):
                    def it0(qb=qb):
                        pO = psum()
                        pOs[qb] = pO
                        mm_group(pO, pO.ap[:, 0:129],
                                 [(E.ap[:, kb, qb * 128:(qb + 1) * 128], Va.ap[:, kb, hl, :]) for kb in range(5)], [E, Va], last=False)
                    items.append(it0)

                    def it(qb=qb):
                        pO = pOs.pop(qb)
                        mm_group(pO, pO.ap[:, 0:129],
                                 [(E.ap[:, kb, qb * 128:(qb + 1) * 128], Va.ap[:, kb, hl, :]) for kb in range(5, 10)], [E, Va], first=False)
                        if m == 0:
                            cp("dve", Osave.ap[:, qb, :], pO.ap[:, 0:129], [pO], [Osave])
                        else:
                            S.op("dve", lambda e, pO=pO: e.reciprocal(out=r1.ap, in_=pO.ap[:, 128:129]), [pO], [r1])
                            S.op("dve", lambda e, qb=qb: e.reciprocal(out=r0.ap, in_=Osave.ap[:, qb, 128:129]), [Osave], [r0])
                            ts("dve", r1.ap, r1.ap, nlam.ap[:, 0:1], None, ALU.mult, None, [r1, nlam], [r1])
                            ts("dve", otmp.ap, Osave.ap[:, qb, 0:128], r0.ap[:, 0:1], None, ALU.mult, None, [Osave, r0], [otmp])
                            stt(ofp.ap[:, qb, :], pO.ap[:, 0:128], r1.ap[:, 0:1], otmp.ap, ALU.mult, ALU.add, [pO, r1, otmp], [ofp])
                            S.op("dve", lambda e, qb=qb: e.scalar_tensor_tensor(out=junk.ap, in0=ofp.ap[:, qb, :], scalar=1.0, in1=ofp.ap[:, qb, :],
                                                                             op0=ALU.mult, op1=ALU.mult, accum_out=ss.ap[:, qb:qb + 1]),
                                 [ofp], [junk, ss])
                    items.append(it)
                if m == 1:
                    def fin():
                        ts("dve", rstd.ap, ss.ap, 1.0 / 128.0, 1e-6, ALU.mult, ALU.add, [ss], [rstd])
                        act(rstd.ap, rstd.ap, AF.Ln, [rstd], [rstd])
                        act(rstd.ap, rstd.ap, AF.Exp, [rstd], [rstd], scale=-0.5)
                        tt("dve", on_all.ap[:, :, h, :], ofp.ap, rstd.ap.unsqueeze(2).to_broadcast([128, 8, 128]), ALU.mult,
                           [ofp, rstd], on_tb)
                    items.append(fin)
                return items

            Ebuf = {}
            for i in range(len(units) + 1):
                A = []
                B = []
                if i < len(units):
                    Ebuf[i] = ET[cnt[0] % 2]
                    cnt[0] += 1
                    A = stageA(units[i], Ebuf[i])
                if i >= 1:
                    B = stageB(units[i - 1], Ebuf[i - 1])
                pattern = [1] * 10
                bsteps = [2, 1, 2, 1, 2, 2, 1, 2, 1, 2]
                ai = 0
                bi = 0
                for si, npat in enumerate(pattern):
                    for _ in range(npat):
                        if ai < len(A):
                            A[ai]()
                            ai += 1
                    for _ in range(bsteps[si]):
                        if bi < len(B):
                            B[bi]()
                            bi += 1
                while ai < len(A):
                    A[ai]()
                    ai += 1
                while bi < len(B):
                    B[bi]()
                    bi += 1
            psum_double_mode(False)
        def og_transpose(tb):
            tsl = slice(tb * 128, (tb + 1) * 128)
            pT = psum()
            pTb = pT.ap.bitcast(BF16)

            def fnT(e, pTb=pTb, tb=tb):
                ins = None
                for h in range(8):
                    ins = e.transpose(pTb[:, h * 128:(h + 1) * 128], on_all.ap[:, tb, h, :], identb.ap)
                return ins

            S.op("pe", fnT, [on_tb[tb], identb], [pT])
            ts("dve", ogT.ap[:, :, tsl], pTb.rearrange("p (a n) -> p a n", a=8), subS.ap[:, j:j + 1], None,
               ALU.mult, None, [pT, subS], [ogT])

        for g in range(2):
            tZ, vZ = load_slab(("wina", j, 3072 + g * 512, 512))
            for tb in range(8):
                tsl = slice(tb * 128, (tb + 1) * 128)
                p = psum()
                mm_group(p, p.ap, [(hT[k].ap[:, tsl], vZ[:, k, :]) for k in range(8)], [tZ] + hT)
                z_ = szb[tb % 2]
                act(z_.ap, p.ap, AF.Silu, [p], [z_])
                dst = on_all.ap[:, tb, g * 4:(g + 1) * 4, :]
                tt("dve", dst, dst, z_.ap.rearrange("p (a n) -> p a n", a=4), ALU.mult, [on_tb[tb], z_], [on_tb[tb]])
                if g == 1 and tb >= 2:
                    og_transpose(tb - 2)
        og_transpose(6)
        og_transpose(7)
        ws = ln_ws_alias(ET[0], [ET[0]], ET[1], [ET[1]])
        out_proj_residual(("wouta", j), 8, [ogT], ogT.ap, ws)

    AR.reset()
    xin = [AR.alloc(f"xin{i}", [128, D], F32) for i in range(4)]

    def x_block(tb):
        xi = xin[tb % 4]
        S.dma("sp", xi.ap, x_d[tb * 128:(tb + 1) * 128, :], [], [xi])
        for half in range(2):
            p = psum()

            def fn(e, p=p, xi=xi, half=half):
                ins = None
                for q in range(4):
                    c = half * 4 + q
                    ins = e.transpose(p.ap[:, q * 128:(q + 1) * 128], xi.ap[:, c * 128:(c + 1) * 128], ident.ap)
                return ins

            S.op("pe", fn, [xi, ident], [p])
            dst = xT_t[:, half * 4:half * 4 + 4, tb * 128:(tb + 1) * 128]
            src = p.ap.rearrange("p (q n) -> p q n", q=4)
            cp("dve" if half == 0 else "act", dst, src, [p], xT[half * 4:half * 4 + 4])

    x_block(0)
    x_block(1)
    for s6 in range(4):
        mod_slab(0, s6)
        if s6 < 3:
            x_block(2 + 2 * s6)
            x_block(3 + 2 * s6)

    for li, (kind, j) in enumerate(layers):
        modulation(li)
        if stop_after == "mod":
            break
        if kind == 2:
            conv_layer(li, j)
        elif kind == 1:
            ret_layer(li, j)
        else:
            attn_layer(li, j)
        if stop_after == "mix":
            break
        layer_norm(li)

    AR.reset()
    yout = [AR.alloc(f"yout{i}", [128, D], F32) for i in range(2)]
    if debug_xT:
        for c in range(NCH):
            S.dma("sp", dbg_d[:, c, :], xT[c].ap, [xT[c]], [])
    for tb in range(8):
        yo = yout[tb % 2]
        for half in range(2):
            p = psum()

            def fn(e, p=p, tb=tb, half=half):
                ins = None
                for q in range(4):
                    c = half * 4 + q
                    ins = e.transpose(p.ap[:, q * 128:(q + 1) * 128], xT_t[:, c, tb * 128:(tb + 1) * 128], ident.ap)
                return ins

            S.op("pe", fn, xT[half * 4:half * 4 + 4] + [ident], [p])
            cp("dve" if half == 0 else "act", yo.ap[:, half * 512:(half + 1) * 512], p.ap, [p], [yo])
        S.dma("sp", y_d[tb * 128:(tb + 1) * 128, :], yo.ap, [yo], [])
    S.finish("sp")

    with nc.Block() as block:
        @block.tensor
        def _(e):
            S.replay("pe", e)

        @block.scalar
        def _(e):
            S.replay("act", e)

        @block.vector
        def _(e):
            S.replay("dve", e)

        @block.gpsimd
        def _(e):
            S.replay("pool", e)

        @block.sync
        def _(e):
            S.replay("sp", e)

    return nc


def _col(v, n):
    return np.ascontiguousarray(np.asarray(v, np.float32).reshape(n, 128).T)


def make_in_maps(inp, layers=LAYERS):
    f32 = np.float32
    g = {k: np.asarray(v) for k, v in inp.items()}
    shared = {}
    shared["bmod"] = np.stack([_col(g["b_mod"][i], 24) for i in range(DEPTH)]).astype(f32)
    shared["lng"] = np.stack([_col(g["ln_g"][i], 8) for i in range(DEPTH)]).astype(f32)
    shared["lnb"] = np.stack([_col(g["ln_b"][i], 8) for i in range(DEPTH)]).astype(f32)
    kinds = set(k for k, _ in layers)
    shared["wmod"] = np.ascontiguousarray(g["w_mod"][:max(1, len(layers))], f32)
    if 0 in kinds:
        shared["wina"] = np.ascontiguousarray(g["w_in_a"], f32)
        shared["wouta"] = np.ascontiguousarray(g["w_out_a"], f32)
    wb = g["w_in_b"][0]
    cols = []
    for h in range(4):
        cols.append(wb[:, h * 256:(h + 1) * 256])
        cols.append(wb[:, 1024 + h * 256:1024 + (h + 1) * 256])
        cols.append(wb[:, 2048 + h * 512:2048 + (h + 1) * 512])
        cols.append(wb[:, 4096 + h * 512:4096 + (h + 1) * 512])
    if 1 in kinds:
        shared["winb"] = np.ascontiguousarray(np.concatenate(cols, 1), f32)
        shared["woutb"] = np.ascontiguousarray(g["w_out_b"][0], f32)
    wc = g["w_in_c"][0]
    cols = []
    for e in range(8):
        for q in range(4):
            cols.append(wc[:, q * 1024 + e * 128:q * 1024 + (e + 1) * 128])
    if 2 in kinds:
        shared["winc"] = np.ascontiguousarray(np.concatenate(cols, 1), f32)
        shared["woutc"] = np.ascontiguousarray(g["w_out_c"][0], f32)
    cw = g["conv_c"][0]
    shared["convw"] = np.ascontiguousarray(cw.reshape(3, 8, 128).transpose(2, 1, 0), f32)
    shared["lam"] = np.ascontiguousarray(g["lam_a"].reshape(2, 256), f32)
    shared["subln"] = np.ascontiguousarray(g["subln_a"].T, f32)
    shared["dec"] = np.concatenate([g["decay_fwd"][0], g["decay_bwd"][0]]).astype(f32)
    shared["ident"] = np.eye(128, dtype=f32)
    m = np.arange(128, dtype=f32)[:, None]
    n = np.arange(128, dtype=f32)[None, :]
    shared["rdt"] = np.stack([np.maximum(n - m, 0), (n >= m).astype(f32), np.maximum(m - n, 0), (n <= m).astype(f32)]).astype(f32)
    shared["rkd"] = np.stack([127.0 - np.arange(128), np.arange(128)], 1).astype(f32)

    tok = np.arange(T)
    r = (tok // 64).astype(np.float64)
    col = (tok % 64).astype(np.float64)
    inv = 10000.0 ** (-np.arange(16, dtype=np.float64) / 16)
    ang = np.concatenate([r[:, None] * inv, col[:, None] * inv], -1)
    cosS = np.cos(ang).astype(f32).reshape(8, 128, 32).transpose(1, 0, 2)
    sinS = np.sin(ang).astype(f32).reshape(8, 128, 32).transpose(1, 0, 2)

    in_maps = []
    for r_ in range(8):
        d = dict(shared)
        if r_ < 4:
            d["x"] = np.ascontiguousarray(g["x_prompt"][4 * r_:4 * r_ + 4].reshape(T, D), f32)
            d["cvec"] = _col(g["c_ctx"], 8)
            d["cachek"] = np.zeros((2, 256, 1024), f32)
            d["cachev"] = np.zeros((2, 256, 1024), f32)
            d["statef"] = np.zeros((4, 256, 512), f32)
            d["stateb"] = np.zeros((4, 256, 512), f32)
            d["ropec"] = np.ones((128, 8, 32), f32)
            d["ropes"] = np.zeros((128, 8, 32), f32)
            seq_q = np.arange(T) // 256
            seq_k = np.concatenate([np.arange(T) // 256, np.full(256, 4)])
            mq = np.zeros((5, T), f32)
            for grp in range(5):
                mq[grp] = np.where(seq_q == grp, 0.0, NEG)
            mk = np.zeros((5, 1280), f32)
            for grp in range(5):
                mk[grp] = (seq_k == grp).astype(f32)
            d["maskq"] = np.ascontiguousarray(np.broadcast_to(mq[:, None, :], (5, 8, T))).astype(ml_dtypes.bfloat16)
            d["maskk"] = np.ascontiguousarray(np.broadcast_to(mk[:, None, :], (5, 8, 1280))).astype(ml_dtypes.bfloat16)
            d["cflag"] = np.ones((128, 1), f32)
            fstart = np.array([c % 2 == 0 for c in range(8)])
            bstart = np.array([c % 2 == 1 for c in range(8)])
        else:
            b = r_ - 4
            d["x"] = np.ascontiguousarray(g["x_sample"][b], f32)
            d["cvec"] = _col(g["c"][b], 8)
            d["cachek"] = np.ascontiguousarray(g["cache_k"][b].reshape(2, 256, 1024), f32)
            d["cachev"] = np.ascontiguousarray(g["cache_v"][b].reshape(2, 256, 1024), f32)
            d["statef"] = np.ascontiguousarray(g["state_fwd"][b, 0], f32)
            d["stateb"] = np.ascontiguousarray(g["state_bwd"][b, 0], f32)
            d["ropec"] = np.ascontiguousarray(cosS)
            d["ropes"] = np.ascontiguousarray(sinS)
            d["maskq"] = np.zeros((5, 8, T), ml_dtypes.bfloat16)
            d["maskk"] = np.zeros((5, 8, 1280), ml_dtypes.bfloat16)
            d["cflag"] = np.zeros((128, 1), f32)
            fstart = np.zeros(8, bool)
            bstart = np.zeros(8, bool)
        i_in = (np.arange(T) % 128).astype(f32)
        cidx = np.arange(T) // 128
        qf = np.where(fstart[cidx], BIG, i_in + 1.0)
        qb = np.where(bstart[cidx], BIG, 128.0 - i_in)
        d["rtab"] = np.stack([qf, qb]).astype(f32)
        d["rchk"] = np.concatenate([np.where(fstart, BIG, 128.0), np.where(bstart, BIG, 128.0)]).astype(f32)
        in_maps.append(d)
    return in_maps


_NC_CACHE = {}


def kernel(**inputs):
    if "full" not in _NC_CACHE:
        _NC_CACHE["full"] = build()
    nc = _NC_CACHE["full"]
    in_maps = make_in_maps(inputs)
    res = run_bass_kernel_spmd(nc, in_maps, core_ids=list(range(8)))
    R = res.results
    f32 = np.float32
    y_prompt = np.stack([R[r]["y"].reshape(4, 256, D) for r in range(4)]).reshape(16, 256, D).astype(f32)
    y_sample = np.stack([R[4 + b]["y"] for b in range(4)]).astype(f32)
    nk = np.stack([R[r]["cko"].reshape(2, 4, 256, 8, 128).transpose(1, 0, 2, 3, 4) for r in range(4)]).reshape(16, 2, 256, 8, 128)
    nv = np.stack([R[r]["cvo"].reshape(2, 4, 256, 8, 128).transpose(1, 0, 2, 3, 4) for r in range(4)]).reshape(16, 2, 256, 8, 128)
    nsf = np.stack([R[r]["sfo"] for r in range(4)]).reshape(16, 1, 4, 256, 512)
    nsb = np.stack([R[r]["sbo"] for r in range(4)]).reshape(16, 1, 4, 256, 512)
    return (y_prompt, y_sample, nk.astype(f32), nv.astype(f32), nsf.astype(f32), nsb.astype(f32))
```
